# Optimizing a Trainium2 kernel written in Bass

```python
import math
import jax, jax.numpy as jnp
from jax import lax
import numpy as np

D_MODEL = 1024
BATCH = 8
SEQ = 2048
DEPTH = 4

D_MIX = D_MODEL
N_DIRS = 2
DN_HEADS = 4
DN_DK = 128
DN_DV = 128
DN_KEY = DN_HEADS * DN_DK
DN_VAL = DN_HEADS * DN_DV
DN_CONV = 5
DN_CHUNK = 64
HY_WIDTH = D_MIX - DN_VAL
HY_ORDER = 2
HY_SHORT = 3
HY_EMB = 33
HY_FILTER_HIDDEN = 64
HY_DIRS = 2
HY_FAST_DECAY_PCT = 0.3
HY_SLOW_DECAY_PCT = 1.5
HY_DECAY_TARGET = 1e-2
D_FF = 2816
RMS_EPS = 1e-6

_O_Q = DN_KEY
_O_K = 2 * DN_KEY
_O_V = _O_K + DN_VAL
_O_Z = _O_V + DN_VAL
_O_B = _O_Z + N_DIRS * DN_HEADS
_O_A = _O_B + N_DIRS * DN_HEADS
D_IN = _O_A + 3 * HY_WIDTH
IN_SPLITS = (_O_Q, _O_K, _O_V, _O_Z, _O_B, _O_A)

kernel_name = "hybrid_deltanet_hyena_macaron_encoder"


def _rmsnorm(x, w):
    xf = x.astype(jnp.float32)
    y = xf * lax.rsqrt(jnp.mean(xf * xf, axis=-1, keepdims=True) + RMS_EPS)
    return (y * w.astype(jnp.float32)).astype(x.dtype)


def _l2norm(t):
    return t * lax.rsqrt(jnp.sum(t * t, axis=-1, keepdims=True) + 1e-6)


def _swiglu(h, w_gate, w_up, w_down):
    return (jax.nn.silu(h @ w_gate) * (h @ w_up)) @ w_down


def _dwconv(x, w):
    k = w.shape[0]
    return lax.conv_general_dilated(
        x, w[:, None, :].astype(x.dtype), window_strides=(1,),
        padding=[(k // 2, k // 2)], dimension_numbers=("NWC", "WIO", "NWC"),
        feature_group_count=x.shape[-1])


def _gated_delta_rule(q, k, v, beta, g):
    b, h, l, dk = q.shape
    dv = v.shape[-1]
    c = DN_CHUNK
    n = l // c
    q = q.reshape(b, h, n, c, dk)
    k = k.reshape(b, h, n, c, dk)
    v = v.reshape(b, h, n, c, dv)
    beta = beta.reshape(b, h, n, c)
    g = jnp.cumsum(g.reshape(b, h, n, c), axis=-1)
    incl = jnp.tril(jnp.ones((c, c), bool))
    strict = jnp.tril(jnp.ones((c, c), bool), -1)
    diff = g[..., :, None] - g[..., None, :]
    decay = jnp.where(incl, jnp.exp(jnp.where(incl, diff, 0.0)), 0.0)
    kb = k * beta[..., None]
    a_mat = jnp.where(strict, jnp.einsum("bhnid,bhnjd->bhnij", kb, k) * decay, 0.0)
    a_mat = a_mat + jnp.eye(c, dtype=q.dtype)
    rhs = jnp.concatenate([v * beta[..., None], kb * jnp.exp(g)[..., None]], axis=-1)
    sol = lax.linalg.triangular_solve(a_mat, rhs, left_side=True, lower=True,
                                      unit_diagonal=True)
    u, w = sol[..., :dv], sol[..., dv:]
    attn = jnp.einsum("bhnid,bhnjd->bhnij", q, k) * decay
    g_last = g[..., -1]
    q_dec = q * jnp.exp(g)[..., None]
    k_dec = k * jnp.exp(g_last[..., None] - g)[..., None]

    def step(state, inp):
        q_i, k_i, u_i, w_i, attn_i, gl_i = inp
        v_new = u_i - jnp.einsum("bhcd,bhde->bhce", w_i, state)
        o_i = (jnp.einsum("bhcd,bhde->bhce", q_i, state)
               + jnp.einsum("bhcs,bhse->bhce", attn_i, v_new))
        state = (state * jnp.exp(gl_i)[..., None, None]
                 + jnp.einsum("bhcd,bhce->bhde", k_i, v_new))
        return state, o_i

    xs = tuple(jnp.moveaxis(t, 2, 0) for t in (q_dec, k_dec, u, w, attn, g_last))
    state0 = jnp.zeros((b, h, dk, dv), q.dtype)
    _, o = lax.scan(step, state0, xs)
    return jnp.moveaxis(o, 0, 2).reshape(b, h, l, dv)


def _deltanet_group(q, k, v, z, beta_logit, alpha, a_log, dt_bias, conv_w, norm_w):
    f32 = jnp.float32
    b, l, _ = q.shape
    qkv = jax.nn.silu(_dwconv(jnp.concatenate([q, k, v], axis=-1), conv_w))
    q, k, v = jnp.split(qkv, [DN_KEY, 2 * DN_KEY], axis=-1)

    def to_heads(t, d):
        return t.reshape(b, l, DN_HEADS, d).transpose(0, 2, 1, 3).astype(f32)

    q = _l2norm(to_heads(q, DN_DK)) * (DN_DK ** -0.5)
    k = _l2norm(to_heads(k, DN_DK))
    v = to_heads(v, DN_DV)
    beta = jax.nn.sigmoid(beta_logit.astype(f32)).reshape(b, l, N_DIRS, DN_HEADS).transpose(2, 0, 3, 1)
    a_in = alpha.astype(f32).reshape(b, l, N_DIRS, DN_HEADS).transpose(2, 0, 3, 1)
    g = -jnp.exp(a_log.astype(f32))[:, None, :, None] * jax.nn.softplus(
        a_in + dt_bias.astype(f32)[:, None, :, None])
    o_fwd = _gated_delta_rule(q, k, v, beta[0], g[0])
    flip = lambda t: jnp.flip(t, axis=2)
    o_bwd = flip(_gated_delta_rule(flip(q), flip(k), flip(v), flip(beta[1]), flip(g[1])))
    o = (o_fwd + o_bwd).transpose(0, 2, 1, 3)
    o = _rmsnorm(o, norm_w) * jax.nn.silu(z.reshape(b, l, DN_HEADS, DN_DV).astype(f32))
    return o.reshape(b, l, DN_VAL).astype(z.dtype)


def _hyena_filter_spectrum(l, w1, b1, w2, b2, w3, b3, freq, wout):
    f32 = jnp.float32
    t = jnp.linspace(0.0, 1.0, l, dtype=f32)[:, None]
    bands = (HY_EMB - 1) // 2
    ang = ((2.0 * math.pi / l) * jnp.arange(l, dtype=f32)[:, None]
           * jnp.linspace(1e-4, bands - 1, bands, dtype=f32)[None, :])
    feats = jnp.concatenate([t, jnp.cos(ang), -jnp.sin(ang)], axis=-1)
    fr = freq.astype(f32)
    hdn = jnp.sin(fr * (feats @ w1.astype(f32) + b1.astype(f32)))
    hdn = jnp.sin(fr * (hdn @ w2.astype(f32) + b2.astype(f32)))
    hdn = jnp.sin(fr * (hdn @ w3.astype(f32) + b3.astype(f32)))
    filt = (hdn @ wout.astype(f32)).reshape(l, HY_DIRS, HY_ORDER, HY_WIDTH)
    max_decay = math.log(HY_DECAY_TARGET) / HY_FAST_DECAY_PCT
    min_decay = math.log(HY_DECAY_TARGET) / HY_SLOW_DECAY_PCT
    deltas = jnp.abs(jnp.linspace(min_decay, max_decay, HY_WIDTH, dtype=f32))
    filt = filt * jnp.exp(-t * deltas)[:, None, None, :]
    fwd, bwd = filt[:, 0], filt[:, 1]
    two_sided = jnp.concatenate(
        [fwd, jnp.zeros((1, HY_ORDER, HY_WIDTH), f32), jnp.flip(bwd[: l - 1], axis=0)], axis=0)
    return jnp.fft.rfft(two_sided, axis=0)


def _fft_conv(z, kspec):
    l = z.shape[1]
    zs = jnp.fft.rfft(z, n=2 * l, axis=1)
    return jnp.fft.irfft(zs * kspec[None], n=2 * l, axis=1)[:, :l]


def _hyena_group(u, conv_w, conv_b, w1, b1, w2, b2, w3, b3, freq, wout, skip, norm_w):
    f32 = jnp.float32
    l = u.shape[1]
    u = _dwconv(u, conv_w) + conv_b
    v, x1, x2 = [t.astype(f32) for t in jnp.split(u, 3, axis=-1)]
    kspec = _hyena_filter_spectrum(l, w1, b1, w2, b2, w3, b3, freq, wout)
    skip = skip.astype(f32)
    zz = v
    for o, gate in enumerate((x1, x2)):
        zz = gate * (_fft_conv(zz, kspec[:, o]) + skip[o] * zz)
    return _rmsnorm(zz, norm_w).astype(u.dtype)


def setup_inputs(seed: int = 0) -> dict:
    key = jax.random.key(seed)
    ks = iter(jax.random.split(key, 40))
    f32 = jnp.float32

    def nrm(shape, scale):
        return scale * jax.random.normal(next(ks), shape, f32)

    def gain(shape):
        return 1.0 + 0.02 * jax.random.normal(next(ks), shape, f32)

    x = jax.random.normal(next(ks), (BATCH, SEQ, D_MODEL), f32)
    a_log = jnp.log(jax.random.uniform(next(ks), (DEPTH, N_DIRS, DN_HEADS), f32, 1.0, 16.0))
    dt = jnp.exp(jax.random.uniform(next(ks), (DEPTH, N_DIRS, DN_HEADS), f32,
                                    math.log(1e-3), math.log(1e-1)))
    dt_bias = dt + jnp.log(-jnp.expm1(-dt))
    hf = HY_FILTER_HIDDEN
    return {
        "x": x,
        "ffn1_norm": gain((DEPTH, D_MODEL)),
        "ffn1_w_gate": nrm((DEPTH, D_MODEL, D_FF), D_MODEL ** -0.5),
        "ffn1_w_up": nrm((DEPTH, D_MODEL, D_FF), D_MODEL ** -0.5),
        "ffn1_w_down": nrm((DEPTH, D_FF, D_MODEL), D_FF ** -0.5),
        "mix_norm": gain((DEPTH, D_MODEL)),
        "w_in": nrm((DEPTH, D_MODEL, D_IN), D_MODEL ** -0.5),
        "dn_conv": nrm((DEPTH, DN_CONV, 2 * DN_KEY + DN_VAL), DN_CONV ** -0.5),
        "dn_a_log": a_log,
        "dn_dt_bias": dt_bias,
        "dn_norm": gain((DEPTH, DN_DV)),
        "hy_conv": nrm((DEPTH, HY_SHORT, 3 * HY_WIDTH), HY_SHORT ** -0.5),
        "hy_conv_bias": nrm((DEPTH, 3 * HY_WIDTH), 0.02),
        "hy_f_w1": nrm((DEPTH, HY_EMB, hf), HY_EMB ** -0.5),
        "hy_f_b1": nrm((DEPTH, hf), 0.1),
        "hy_f_w2": nrm((DEPTH, hf, hf), hf ** -0.5),
        "hy_f_b2": nrm((DEPTH, hf), 0.1),
        "hy_f_w3": nrm((DEPTH, hf, hf), hf ** -0.5),
        "hy_f_b3": nrm((DEPTH, hf), 0.1),
        "hy_f_freq": gain((DEPTH, hf)),
        "hy_f_wout": nrm((DEPTH, hf, HY_DIRS * HY_ORDER * HY_WIDTH), 0.05 * hf ** -0.5),
        "hy_skip": nrm((DEPTH, HY_ORDER, HY_WIDTH), 1.0),
        "hy_norm": gain((DEPTH, HY_WIDTH)),
        "w_out": nrm((DEPTH, D_MIX, D_MODEL), D_MIX ** -0.5),
        "ffn2_norm": gain((DEPTH, D_MODEL)),
        "ffn2_w_gate": nrm((DEPTH, D_MODEL, D_FF), D_MODEL ** -0.5),
        "ffn2_w_up": nrm((DEPTH, D_MODEL, D_FF), D_MODEL ** -0.5),
        "ffn2_w_down": nrm((DEPTH, D_FF, D_MODEL), D_FF ** -0.5),
        "final_norm": gain((D_MODEL,)),
    }


def reference(x, ffn1_norm, ffn1_w_gate, ffn1_w_up, ffn1_w_down, mix_norm, w_in,
              dn_conv, dn_a_log, dn_dt_bias, dn_norm, hy_conv, hy_conv_bias,
              hy_f_w1, hy_f_b1, hy_f_w2, hy_f_b2, hy_f_w3, hy_f_b3, hy_f_freq,
              hy_f_wout, hy_skip, hy_norm, w_out, ffn2_norm, ffn2_w_gate,
              ffn2_w_up, ffn2_w_down, final_norm):
    for i in range(DEPTH):
        x = x + 0.5 * _swiglu(_rmsnorm(x, ffn1_norm[i]), ffn1_w_gate[i], ffn1_w_up[i], ffn1_w_down[i])
        proj = _rmsnorm(x, mix_norm[i]) @ w_in[i]
        q, k, v, z, beta_logit, alpha, hy_in = jnp.split(proj, IN_SPLITS, axis=-1)
        o_dn = _deltanet_group(q, k, v, z, beta_logit, alpha, dn_a_log[i], dn_dt_bias[i],
                               dn_conv[i], dn_norm[i])
        o_hy = _hyena_group(hy_in, hy_conv[i], hy_conv_bias[i], hy_f_w1[i], hy_f_b1[i],
                            hy_f_w2[i], hy_f_b2[i], hy_f_w3[i], hy_f_b3[i], hy_f_freq[i],
                            hy_f_wout[i], hy_skip[i], hy_norm[i])
        x = x + jnp.concatenate([o_dn, o_hy], axis=-1) @ w_out[i]
        x = x + 0.5 * _swiglu(_rmsnorm(x, ffn2_norm[i]), ffn2_w_gate[i], ffn2_w_up[i], ffn2_w_down[i])
    return _rmsnorm(x, final_norm)
```

```python
import math
import numpy as np
import ml_dtypes
from contextlib import ExitStack
import concourse.bass as bass
import concourse.mybir as mybir
from concourse.bass_utils import run_bass_kernel_spmd

F32 = mybir.dt.float32
BF16 = mybir.dt.bfloat16
ALU = mybir.AluOpType
AF = mybir.ActivationFunctionType

D_MODEL = 1024
SEQ = 2048
DEPTH = 4
D_FF = 2816
D_IN = 3600
HYW = 512
NT = 4
KC = 8
RMS_EPS = 1e-6

ENGS = ("pe", "act", "dve", "pool", "sp")
ENGATTR = {"pe": "tensor", "act": "scalar", "dve": "vector", "pool": "gpsimd", "sp": "sync"}
I32 = mybir.dt.int32
DTSIZE = {F32: 4, BF16: 2, I32: 4}


class Res:
    __slots__ = ("last_w", "readers", "psum")

    def __init__(self, psum=False):
        self.last_w = None
        self.readers = []
        self.psum = psum


class Batch:
    __slots__ = ("sem", "final")

    def __init__(self, sem):
        self.sem = sem
        self.final = 0


class DSem:
    def __init__(self, prog, name):
        self.prog = prog
        self.name = name
        self.gen = 0
        self.sem = prog.sem(name)
        self.count = 0
        self.batch = None
        self.prev = None

    def new_batch(self):
        self.prev = self.batch
        if self.count > 3000:
            self.gen += 1
            self.sem = self.prog.sem("%s_g%d" % (self.name, self.gen))
            self.count = 0
        self.batch = Batch(self.sem)
        self.batch.final = self.count
        return self.batch


class Instr:
    __slots__ = ("eng", "fn", "deps", "idx", "marked", "batch")

    def __init__(self, eng, fn):
        self.eng = eng
        self.fn = fn
        self.deps = []
        self.marked = False
        self.batch = None


class Mem:
    def __init__(self, prog, name, nbytes, page, space="sb"):
        self.name = name
        self.nbytes = nbytes
        self.page = page
        self.pages = [Res(psum=(space != "sb")) for _ in range((nbytes + page - 1) // page)]
        if space == "sb":
            self.t = prog.es.enter_context(prog.nc.sbuf_tensor(name, [128, nbytes // 4], F32))
        else:
            self.t = prog.es.enter_context(prog.nc.psum_tensor(name, [128, nbytes // 4], F32))
        self.bump = 0

    def reset(self, to=0):
        self.bump = to

    def mark(self):
        return self.bump

    def alloc(self, shape, dt, align=None):
        n = int(np.prod(shape)) * DTSIZE[dt]
        al = align or min(self.page, 512)
        base = (self.bump + al - 1) // al * al
        assert base + n <= self.nbytes, (self.name, base, n, self.nbytes)
        self.bump = base + n
        return Tile(self, base, shape, dt)


class View:
    __slots__ = ("ap", "pages")

    def __init__(self, ap, pages):
        self.ap = ap
        self.pages = pages


class Tile:
    def __init__(self, mem, base, shape, dt):
        self.mem = mem
        self.base = base
        self.shape = tuple(shape)
        self.dt = dt
        esz = DTSIZE[dt]
        n = int(np.prod(shape))
        assert base % 4 == 0 and (n * esz) % 4 == 0
        ap = mem.t[:, base // 4:(base + n * esz) // 4]
        if dt != F32:
            ap = ap.bitcast(dt)
        if len(shape) == 2:
            ap = ap.rearrange("p (a b) -> p a b", b=shape[1])
        elif len(shape) == 3:
            ap = ap.rearrange("p (a b c) -> p a b c", b=shape[1], c=shape[2])
        self.ap = ap
        self.strides = [int(np.prod(shape[i + 1:])) for i in range(len(shape))]

    def __getitem__(self, idx):
        if not isinstance(idx, tuple):
            idx = (idx,)
        ap = self.ap[idx]
        lo = 0
        hi = 0
        fidx = idx[1:]
        for d in range(len(self.shape)):
            if d < len(fidx):
                ix = fidx[d]
                if isinstance(ix, slice):
                    a = 0 if ix.start is None else ix.start
                    b = self.shape[d] if ix.stop is None else ix.stop
                else:
                    a, b = ix, ix + 1
            else:
                a, b = 0, self.shape[d]
            lo += a * self.strides[d]
            hi += (b - 1) * self.strides[d]
        esz = DTSIZE[self.dt]
        b0 = self.base + lo * esz
        b1 = self.base + (hi + 1) * esz
        pg = self.mem.page
        pages = self.mem.pages[b0 // pg:(b1 + pg - 1) // pg]
        return View(ap, pages)

    def all(self):
        return self[:]


class Ring:
    def __init__(self, items):
        self.items = items
        self.i = 0

    def next(self):
        it = self.items[self.i % len(self.items)]
        self.i += 1
        return it


class Prog:
    def __init__(self, nc, same_eng_sync=True):
        self.nc = nc
        self.es = ExitStack()
        self.streams = {e: [] for e in ENGS}
        self.same_eng_sync = same_eng_sync
        self.final_batches = []

    def sem(self, name):
        return self.es.enter_context(self.nc.semaphore(name))

    def op(self, eng, fn, reads=(), writes=(), batch=None, dsem=None):
        ins = Instr(eng, fn)
        st = self.streams[eng]
        ins.idx = len(st)
        st.append(ins)
        deps = ins.deps
        rp = []
        for r in reads:
            rp.extend(r.pages)
        wp = []
        for w in writes:
            wp.extend(w.pages)
        for r in rp:
            if r.last_w is not None:
                deps.append(r.last_w)
            if r.psum:
                for t_ in r.readers:
                    if t_[0] == "eng" and t_[1] != eng:
                        deps.append(t_)
        for w in wp:
            if w.last_w is not None:
                deps.append(w.last_w)
            deps.extend(w.readers)
        if batch is not None:
            ins.deps = deps = [d for d in deps if not (d[0] == "dma" and d[1] is batch)]
            ins.batch = batch
            dsem.count += 16
            batch.final = dsem.count
            tok = ("dma", batch)
        else:
            tok = ("eng", eng, ins.idx)
        for r in rp:
            r.readers.append(tok)
        for w in wp:
            w.last_w = tok
            w.readers = []
        return ins

    def pe(self, fn, reads=(), writes=()):
        return self.op("pe", fn, reads, writes)

    def act(self, fn, reads=(), writes=()):
        return self.op("act", fn, reads, writes)

    def dve(self, fn, reads=(), writes=()):
        return self.op("dve", fn, reads, writes)

    def pool(self, fn, reads=(), writes=()):
        return self.op("pool", fn, reads, writes)

    def dma(self, eng, dsem, fn, reads=(), writes=(), new_batch=True):
        if new_batch or dsem.batch is None:
            dsem.new_batch()
        ins = self.op(eng, fn, reads, writes, batch=dsem.batch, dsem=dsem)
        pv = getattr(dsem, "prev", None)
        if pv is not None:
            ins.deps.append(("dma", pv))
        return ins

    def emit(self):
        nc = self.nc
        sync_same = self.same_eng_sync
        for e in ENGS:
            for ins in self.streams[e]:
                for d in ins.deps:
                    if d[0] == "eng":
                        if d[1] == ins.eng and (ins.eng == "pe" or (not sync_same and ins.eng != "pool")):
                            continue
                        self.streams[d[1]][d[2]].marked = True
        cum = {}
        for e in ENGS:
            c = 0
            arr = []
            for ins in self.streams[e]:
                if ins.marked and ins.batch is None:
                    c += 1
                arr.append(c)
            cum[e] = arr
        SEG = 4000
        esem = {e: [self.sem("s_%s%d" % (e, k)) for k in range(max(1, (cum[e][-1] + SEG - 1) // SEG if cum[e] else 1))]
                for e in ENGS}
        stats = {"waits": 0, "incs": 0, "n": {e: len(self.streams[e]) for e in ENGS}}
        final_batches = self.final_batches
        with nc.Block() as block:
            for e in ENGS:
                stream = self.streams[e]

                def body(engine, e=e, stream=stream):
                    waited = {}
                    for ins in stream:
                        need = {}
                        for d in ins.deps:
                            if d[0] == "eng":
                                if d[1] == e and (e == "pe" or (not sync_same and e != "pool")):
                                    continue
                                c_ = cum[d[1]][d[2]]
                                k_ = (c_ - 1) // SEG
                                key = (d[1], k_)
                                sem = esem[d[1]][k_]
                                val = (c_ - 1) % SEG + 1
                            else:
                                key = id(d[1].sem)
                                sem = d[1].sem
                                val = d[1].final
                            if waited.get(key, 0) >= val:
                                continue
                            if key not in need or need[key][1] < val:
                                need[key] = (sem, val)
                        for key, (sem, val) in need.items():
                            engine.wait_ge(sem, val)
                            waited[key] = val
                            stats["waits"] += 1
                        bi = ins.fn(engine)
                        if ins.batch is not None:
                            bi.then_inc(ins.batch.sem, 16)
                        elif ins.marked:
                            bi.then_inc(esem[e][(cum[e][ins.idx] - 1) // SEG], 1)
                            stats["incs"] += 1
                    if e == "sp":
                        for b in final_batches:
                            engine.wait_ge(b.sem, b.final)

                getattr(block, ENGATTR[e])(body)
        return stats


class DramRes:
    def __init__(self):
        self.pages = [Res()]


PI = math.pi
CLAY = {}
NCST = 0


def _mk_layout():
    global NCST
    off = 0
    for name, n in (("ffn1_norm", DEPTH * 8), ("mix_norm", DEPTH * 8), ("ffn2_norm", DEPTH * 8), ("final_norm", 8),
                    ("eps", 1), ("eps_l2", 1), ("one", 1), ("negpi", 1),
                    ("dn_conv", DEPTH * 60), ("dn_norm", DEPTH), ("hy_conv", DEPTH * 36), ("hy_bias", DEPTH * 12),
                    ("hy_skip", DEPTH * 8), ("hy_norm", DEPTH * 4), ("dtb", DEPTH * 8), ("alog", DEPTH * 8),
                    ("f_b1", DEPTH), ("f_b2", DEPTH), ("f_b3", DEPTH), ("f_freq", DEPTH), ("negt", 16),
                    ("f_w1", DEPTH * 64), ("f_w2", DEPTH * 64), ("f_w3", DEPTH * 64)):
        CLAY[name] = (off, n)
        off += n
    NCST = off


_mk_layout()
MATS = ("ident", "ones", "neg_ls", "neg_us", "neg_li", "neg_ui", "cum_f", "cum_b", "inda", "indb", "blk")
NEGV = -30000.0


def _chunked(v):
    return np.ascontiguousarray(np.asarray(v, np.float32).reshape(-1, 128).T)


def build_cst(inp):
    cst = np.zeros((128, NCST), np.float32)

    def put(name, arr, rows=128):
        o, n = CLAY[name]
        arr = np.asarray(arr, np.float32)
        assert arr.shape == (rows, n), (name, arr.shape, n)
        cst[:rows, o:o + n] = arr

    for nm in ("ffn1_norm", "mix_norm", "ffn2_norm"):
        put(nm, np.concatenate([_chunked(inp[nm][l]) for l in range(DEPTH)], axis=1))
    put("final_norm", _chunked(inp["final_norm"]))
    cst[:, CLAY["eps"][0]] = RMS_EPS
    cst[:, CLAY["eps_l2"][0]] = 1e-6
    cst[:, CLAY["one"][0]] = 1.0
    cst[:, CLAY["negpi"][0]] = -math.pi
    a = np.asarray(inp["dn_conv"], np.float32).reshape(DEPTH, 5, 12, 128).transpose(3, 0, 2, 1).reshape(128, -1)
    put("dn_conv", a)
    put("dn_norm", np.asarray(inp["dn_norm"], np.float32).T)
    a = np.asarray(inp["hy_conv"], np.float32).reshape(DEPTH, 3, 12, 128).transpose(3, 0, 2, 1).reshape(128, -1)
    put("hy_conv", a)
    a = np.asarray(inp["hy_conv_bias"], np.float32).reshape(DEPTH, 12, 128).transpose(2, 0, 1).reshape(128, -1)
    put("hy_bias", a)
    a = np.asarray(inp["hy_skip"], np.float32).reshape(DEPTH, 2, 4, 128).transpose(3, 0, 1, 2).reshape(128, -1)
    put("hy_skip", a)
    a = np.asarray(inp["hy_norm"], np.float32).reshape(DEPTH, 4, 128).transpose(2, 0, 1).reshape(128, -1)
    put("hy_norm", a)
    put("dtb", np.broadcast_to(np.asarray(inp["dn_dt_bias"], np.float32).reshape(1, -1), (128, DEPTH * 8)))
    put("alog", np.broadcast_to(np.asarray(inp["dn_a_log"], np.float32).reshape(1, -1), (128, DEPTH * 8)))
    for nm, key in (("f_b1", "hy_f_b1"), ("f_b2", "hy_f_b2"), ("f_b3", "hy_f_b3"), ("f_freq", "hy_f_freq")):
        put(nm, np.asarray(inp[key], np.float32).T, rows=64)
    t = np.linspace(0.0, 1.0, SEQ, dtype=np.float32)
    put("negt", -t.reshape(16, 128).T)
    put("f_w1", np.asarray(inp["hy_f_w1"], np.float32).transpose(1, 0, 2).reshape(33, -1), rows=33)
    put("f_w2", np.asarray(inp["hy_f_w2"], np.float32).transpose(1, 0, 2).reshape(64, -1), rows=64)
    put("f_w3", np.asarray(inp["hy_f_w3"], np.float32).transpose(1, 0, 2).reshape(64, -1), rows=64)
    return cst


_STATIC = {}


def static_consts():
    if _STATIC:
        return _STATIC
    i = np.arange(128)
    same = (i[:, None] // 64) == (i[None, :] // 64)
    row, col = i[:, None], i[None, :]
    m = {}
    m["ident"] = np.eye(128)
    m["ones"] = np.ones((128, 128))
    m["neg_ls"] = np.where(same & (row > col), 0.0, NEGV)
    m["neg_us"] = np.where(same & (row < col), 0.0, NEGV)
    m["neg_li"] = np.where(same & (row >= col), 0.0, NEGV)
    m["neg_ui"] = np.where(same & (row <= col), 0.0, NEGV)
    m["cum_f"] = (same & (row <= col)).astype(np.float64)
    m["cum_b"] = (same & (row >= col)).astype(np.float64)
    m["inda"] = np.broadcast_to((i < 64)[:, None], (128, 128)).astype(np.float64)
    m["indb"] = np.broadcast_to((i >= 64)[:, None], (128, 128)).astype(np.float64)
    m["blk"] = same.astype(np.float64)
    _STATIC["mats"] = np.ascontiguousarray(np.concatenate([m[k] for k in MATS], axis=1), dtype=np.float32)
    l = SEQ
    t = np.linspace(0.0, 1.0, l, dtype=np.float32)[:, None]
    bands = 16
    ang = ((np.float32(2.0 * math.pi / l)) * np.arange(l, dtype=np.float32)[:, None]
           * np.linspace(1e-4, bands - 1, bands, dtype=np.float32)[None, :]).astype(np.float32)
    feats = np.concatenate([t, np.cos(ang), -np.sin(ang)], axis=-1).astype(np.float32)
    _STATIC["featsT"] = np.ascontiguousarray(feats.T)
    max_decay = math.log(1e-2) / 0.3
    min_decay = math.log(1e-2) / 1.5
    deltas = np.abs(np.linspace(min_decay, max_decay, HYW, dtype=np.float32))
    _STATIC["deltas"] = np.ascontiguousarray(np.broadcast_to(deltas[None, :], (128, HYW)), dtype=np.float32)
    n = np.arange(SEQ, dtype=np.float64)
    ang = 2.0 * np.pi * np.outer(n, n) / (2 * SEQ)
    cf = np.cos(ang)
    sf = -np.sin(ang)
    sf[:, 0] = (-1.0) ** n
    ci = (2.0 / (2 * SEQ)) * np.cos(ang)
    ci[0, :] = 1.0 / (2 * SEQ)
    si = -(2.0 / (2 * SEQ)) * np.sin(ang)
    si[0, :] = ((-1.0) ** n) / (2 * SEQ)
    bf = ml_dtypes.bfloat16

    def fwd_layout(a):
        return np.ascontiguousarray(a.reshape(16, 128, 16, 128).transpose(2, 1, 0, 3).reshape(16, 128, 2048)).astype(bf)

    _STATIC["dft_cf"] = fwd_layout(cf)
    _STATIC["dft_sf"] = fwd_layout(sf)
    _STATIC["dft_ci"] = np.ascontiguousarray(ci).astype(bf)
    _STATIC["dft_si"] = np.ascontiguousarray(si).astype(bf)
    return _STATIC


def build_program(cfg):
    n_layers = cfg.get("n_layers", DEPTH)
    layers = tuple(cfg.get("layers", tuple(range(n_layers))))
    LW = {l_: i_ for i_, l_ in enumerate(layers)}
    LD = len(layers) if "layers" in cfg else DEPTH
    if "layers" not in cfg:
        LW = {l_: l_ for l_ in range(DEPTH)}
    stages = cfg.get("stages", ("ffn1", "dn", "hy", "ffn2"))
    nc = bass.Bass("TRN2", target_bir_lowering=False)
    P = Prog(nc, same_eng_sync=cfg.get("same_eng_sync", True))

    def din(name, shape, dt=F32):
        return nc.dram_tensor(name, list(shape), dt, kind="ExternalInput").ap()

    xT_d = din("xT", [D_MODEL, SEQ])
    cst_d = din("cst", [128, NCST])
    mats_d = din("mats", [128, len(MATS) * 128])
    featsT_d = din("featsT", [33, SEQ])
    deltas_d = din("deltas", [128, HYW])
    cf_d = din("dft_cf", [16, 128, 2048], BF16)
    sf_d = din("dft_sf", [16, 128, 2048], BF16)
    ci_d = din("dft_ci", [2048, 2048], BF16)
    si_d = din("dft_si", [2048, 2048], BF16)
    Wd_ = {}
    for nm, shp in (("ffn1_w_gate", [LD, D_MODEL, D_FF]), ("ffn1_w_up", [LD, D_MODEL, D_FF]),
                    ("ffn1_w_down", [LD, D_FF, D_MODEL]), ("ffn2_w_gate", [LD, D_MODEL, D_FF]),
                    ("ffn2_w_up", [LD, D_MODEL, D_FF]), ("ffn2_w_down", [LD, D_FF, D_MODEL]),
                    ("w_in", [LD, D_MODEL, D_IN]), ("w_out", [LD, D_MODEL, D_MODEL]),
                    ("hy_f_wout", [LD, 64, 2048])):
        if nm.startswith("ffn") and nm[:4] not in stages:
            continue
        Wd_[nm] = din(nm, shp)
    outT_d = nc.dram_tensor("outT", [D_MODEL, SEQ], F32, kind="ExternalOutput").ap()

    CM = Mem(P, "cm", 15 * 1024, 256)
    AR = Mem(P, "arena", cfg.get("arena_kib", 128) * 1024, 512)
    XM = Mem(P, "xT_sb", 8 * SEQ * 4, 2048)
    xT = XM.alloc([8, SEQ], F32)
    hnb = [None]
    PS = [Mem(P, "ps%d" % i, 2048, 2048, space="ps") for i in range(8)]
    psb = [m.alloc([512], F32) for m in PS]
    psbf = {id(t): Tile(t.mem, 0, [256], BF16) for t in psb}
    psring = Ring(psb)

    cst = CM.alloc([NCST], F32)
    mats = CM.alloc([len(MATS), 128], F32)
    ones_bf = CM.alloc([128], BF16)
    ident_bf = CM.alloc([128], BF16)
    negA = CM.alloc([DEPTH * 8], F32)
    deltas = CM.alloc([HYW], F32)

    def mat(name):
        return mats[:, MATS.index(name), :]

    d_const = DSem(P, "d_const")
    d_x = DSem(P, "d_x")
    d_out = DSem(P, "d_out")
    d_outs = [DSem(P, "d_out%d" % i) for i in range(3)]

    def ccol(name, i=0, n=1, rows=None):
        o, _ = CLAY[name]
        if rows is None:
            return cst[:, o + i:o + i + n]
        return cst[0:rows, o + i:o + i + n]

    def MM(out, lhsT, rhs, start=True, stop=True):
        P.pe(lambda e: e.matmul(out.ap, lhsT=lhsT.ap, rhs=rhs.ap, start=start, stop=stop),
             reads=[lhsT, rhs], writes=[out])

    def ACTF(out, in_, func, bias=None, scale=1.0):
        rd = [in_]
        kw = {}
        if bias is not None:
            rd.append(bias)
            kw["bias"] = bias.ap
        if isinstance(scale, View):
            rd.append(scale)
            kw["scale"] = scale.ap
        else:
            kw["scale"] = scale
        P.act(lambda e: e.activation(out=out.ap, in_=in_.ap, func=func, **kw), reads=rd, writes=[out])

    def TT(eng, out, in0, in1, op):
        P.op(eng, lambda e: e.tensor_tensor(out=out.ap, in0=in0.ap, in1=in1.ap, op=op), reads=[in0, in1], writes=[out])

    def TS(eng, out, in0, s1, s2, op0, op1=None):
        rd = [in0]
        a1 = s1
        a2 = s2
        if isinstance(s1, View):
            rd.append(s1)
            a1 = s1.ap
        if isinstance(s2, View):
            rd.append(s2)
            a2 = s2.ap
        if op1 is None:
            P.op(eng, lambda e: e.tensor_scalar(out=out.ap, in0=in0.ap, scalar1=a1, scalar2=None, op0=op0),
                 reads=rd, writes=[out])
        else:
            P.op(eng, lambda e: e.tensor_scalar(out=out.ap, in0=in0.ap, scalar1=a1, scalar2=a2, op0=op0, op1=op1),
                 reads=rd, writes=[out])

    def STT(eng, out, in0, sc, in1, op0, op1):
        rd = [in0, in1]
        a = sc
        if isinstance(sc, View):
            rd.append(sc)
            a = sc.ap
        P.op(eng, lambda e: e.scalar_tensor_tensor(out=out.ap, in0=in0.ap, scalar=a, in1=in1.ap, op0=op0, op1=op1),
             reads=rd, writes=[out])

    def CP(eng, out, in_):
        if eng == "act":
            P.act(lambda e: e.copy(out=out.ap, in_=in_.ap), reads=[in_], writes=[out])
        else:
            P.op(eng, lambda e: e.tensor_copy(out=out.ap, in_=in_.ap), reads=[in_], writes=[out])

    def MEMSET(eng, out, val):
        P.op(eng, lambda e: e.memset(out.ap, val), writes=[out])

    def DMA(eng, dsem, out, src_ap, new_batch=True):
        P.dma(eng, dsem, lambda e: e.dma_start(out=out.ap, in_=src_ap), writes=[out], new_batch=new_batch)

    def bc(view, shape):
        return View(view.ap.to_broadcast(list(shape)), view.pages)

    DMA("sp", d_const, cst[:], cst_d)
    DMA("sp", d_const, mats[:], mats_d.rearrange("p (m c) -> p m c", c=128), new_batch=False)
    DMA("sp", d_const, deltas[:], deltas_d, new_batch=False)
    MEMSET("pool", ones_bf[:], 1.0)
    CP("dve", ident_bf[:], mat("ident"))
    ACTF(negA[:], ccol("alog", 0, DEPTH * 8), AF.Exp)
    TS("dve", negA[:], negA[:], -1.0, None, ALU.mult)
    d_x.new_batch()
    for k in range(8):
        DMA("sp", d_x, xT[:, k, :], xT_d[k * 128:(k + 1) * 128, :], new_batch=False)

    def rstd_from(chunks, ps, r, sqring, mean_div, eps_view):
        n = len(chunks)
        for k, cv in enumerate(chunks):
            s = sqring.next()
            sv = s[:, 0:cv.ap.shape[-1]]
            ACTF(sv, cv, AF.Square)
            MM(ps, ones_bf[:], sv, start=(k == 0), stop=(k == n - 1))
        ACTF(r, ps, AF.Sqrt, bias=eps_view, scale=1.0 / mean_div)
        P.dve(lambda e: e.reciprocal(out=r.ap, in_=r.ap), reads=[r], writes=[r])

    def rmsnorm(wname, wcol0, final=False):
        hn = hnb[0]
        m0_ = AR.mark()
        sq = Ring([AR.alloc([512], BF16) for _ in range(3)])
        rs = Ring([AR.alloc([512], F32) for _ in range(2)])
        ost = Ring([AR.alloc([512], F32) for _ in range(3)]) if final else None
        oi = 0
        for tt in range(NT):
            ts = slice(tt * 512, (tt + 1) * 512)
            ps = psring.next()
            r = rs.next()
            rstd_from([xT[:, k, ts] for k in range(8)], ps[:], r[:], sq, D_MODEL, ccol("eps"))
            for k in range(8):
                xv = xT[:, k, ts]
                wv = ccol(wname, wcol0 + k)
                if not final:
                    STT("dve", hn[:, k, ts], xv, wv, r[:], ALU.mult, ALU.mult)
                else:
                    o = ost.next()
                    STT("dve", o[:], xv, wv, r[:], ALU.mult, ALU.mult)
                    P.dma("sp", d_outs[oi % 3], lambda e, o=o, k=k, ts=ts: e.dma_start(
                        out=outT_d[k * 128:(k + 1) * 128, ts], in_=o[:].ap),
                        reads=[o[:]], writes=[DramRes()], new_batch=True)
                    oi += 1
        AR.reset(m0_)

    d_wgu = [DSem(P, "d_wgu%d" % i) for i in range(2)]
    d_wd = [DSem(P, "d_wd%d" % i) for i in range(2)]
    FGROUPS = [(0, 4), (4, 4), (8, 4), (12, 4), (16, 3), (19, 3)]

    def ffn(l, which):
        pre = "ffn1" if which == 0 else "ffn2"
        Wg = Wd_[pre + "_w_gate"]
        Wu = Wd_[pre + "_w_up"]
        Wdn = Wd_[pre + "_w_down"]
        AR.reset()
        hn = hnb[0] = AR.alloc([8, SEQ], BF16)
        rmsnorm(pre + "_norm", l * 8)
        wg = [AR.alloc([8, 512], BF16) for _ in range(2)]
        wu = [AR.alloc([8, 512], BF16) for _ in range(2)]
        wd = [AR.alloc([4, 1024], BF16) for _ in range(2)]
        actb = [AR.alloc([4, SEQ], BF16) for _ in range(2)]
        stmp = Ring([AR.alloc([512], F32) for _ in range(3)])

        def load(gi):
            f0, nf = FGROUPS[gi]
            s = gi % 2
            d_wgu[s].new_batch()
            for (W, t) in ((Wg, wg[s]), (Wu, wu[s])):
                src = W[LW[l], :, f0 * 128:(f0 + nf) * 128].rearrange("(k p) f -> p k f", p=128)
                DMA("pool", d_wgu[s], t[:, :, 0:nf * 128], src, new_batch=False)
            src = Wdn[LW[l], f0 * 128:(f0 + nf) * 128, :].rearrange("(j p) d -> p j d", p=128)
            DMA("pool", d_wd[s], wd[s][:, 0:nf, :], src)

        load(0)
        for gi, (f0, nf) in enumerate(FGROUPS):
            s = gi % 2
            if gi + 1 < len(FGROUPS):
                load(gi + 1)
            ab = actb[s]
            for fj in range(nf):
                fs = slice(fj * 128, (fj + 1) * 128)
                for tt in range(NT):
                    ts = slice(tt * 512, (tt + 1) * 512)
                    gps = psring.next()
                    ups = psring.next()
                    for (pst, wt) in ((gps, wg[s]), (ups, wu[s])):
                        for k in range(8):
                            MM(pst[:], wt[:, k, fs], hn[:, k, ts], start=(k == 0), stop=(k == 7))
                    st = stmp.next()
                    ACTF(st[:], gps[:], AF.Silu)
                    TT("dve", ab[:, fj, ts], ups[:], st[:], ALU.mult)
            for dk in range(8):
                ds_ = slice(dk * 128, (dk + 1) * 128)
                for tt in range(NT):
                    ts = slice(tt * 512, (tt + 1) * 512)
                    ops = psring.next()
                    for fj in range(nf):
                        MM(ops[:], wd[s][:, fj, ds_], ab[:, fj, ts], start=(fj == 0), stop=(fj == nf - 1))
                    xv = xT[:, dk, ts]
                    STT("dve", xv, ops[:], 0.5, xv, ALU.mult, ALU.add)

    d_win = [DSem(P, "d_win%d" % i) for i in range(2)]
    d_wo = DSem(P, "d_wo")
    d_dft = [DSem(P, "d_dft%d" % i) for i in range(2)]
    d_misc = DSem(P, "d_misc")
    d_wba = DSem(P, "d_wba")
    win_i = [0]
    ohyb = [None]
    hy_mark = [0]

    def wout_apply(l, row0, nchunks, src_tile):
        wo = AR.alloc([nchunks, 1024], BF16)
        src = Wd_["w_out"][LW[l], row0:row0 + 128 * nchunks, :].rearrange("(m p) d -> p m d", p=128)
        DMA("pool", d_wo, wo[:], src)
        for dk in range(8):
            for tt in range(NT):
                ts = slice(tt * 512, (tt + 1) * 512)
                ps = psring.next()
                for m in range(nchunks):
                    MM(ps[:], wo[:, m, dk * 128:(dk + 1) * 128], src_tile[:, m, ts], start=(m == 0), stop=(m == nchunks - 1))
                xv = xT[:, dk, ts]
                TT("dve", xv, ps[:], xv, ALU.add)

    def project(l, col0, ncols, wslots, consume):
        hn = hnb[0]
        s = win_i[0] % 2
        win_i[0] += 1
        wt = wslots[s]
        src = Wd_["w_in"][LW[l], :, col0:col0 + ncols].rearrange("(k p) f -> p k f", p=128)
        DMA("pool", d_win[s], wt[:, :, 0:ncols], src)
        for m in range(ncols // 128):
            for tt in range(NT):
                ts = slice(tt * 512, (tt + 1) * 512)
                ps = psring.next()
                for k in range(8):
                    MM(ps[:], wt[:, k, m * 128:(m + 1) * 128], hn[:, k, ts], start=(k == 0), stop=(k == 7))
                consume(m, tt, ps)

    def hyena(l):
        AR.reset()
        h3T = AR.alloc([SEQ], F32)
        ohy = ohyb[0] = AR.alloc([4, SEQ], BF16)
        fcst = AR.alloc([8], F32)
        mark0 = AR.mark()
        featsT = AR.alloc([SEQ], F32)
        hA = AR.alloc([SEQ], F32)
        hB = AR.alloc([SEQ], F32)
        tmp = Ring([AR.alloc([512], F32) for _ in range(4)])
        tmpi = Ring([AR.alloc([512], I32) for _ in range(2)])
        DMA("sp", d_misc, featsT[0:33, :], featsT_d)
        freq = ccol("f_freq", l, rows=64)
        for li, bn in enumerate(("f_b1", "f_b2", "f_b3")):
            TS("dve", fcst[0:64, li:li + 1], ccol(bn, l, rows=64), freq, 17.0 * PI, ALU.mult, ALU.add)
        srcs = [featsT, hA, hB]
        dsts = [hA, hB, h3T]
        for li in range(3):
            kk = 33 if li == 0 else 64
            wv = cst[0:kk, CLAY["f_w%d" % (li + 1)][0] + l * 64: CLAY["f_w%d" % (li + 1)][0] + (l + 1) * 64]
            for tt in range(NT):
                ts = slice(tt * 512, (tt + 1) * 512)
                ps = psring.next()
                MM(ps[0:64, :], wv, srcs[li][0:kk, ts])
                t_ = tmp.next()
                TS("dve", t_[0:64, :], ps[0:64, :], freq, fcst[0:64, li:li + 1], ALU.mult, ALU.add)
                ka = tmp.next()
                ki = tmpi.next()
                TS("dve", ka[0:64, :], t_[0:64, :], 1.0 / (2.0 * PI), None, ALU.mult)
                CP("dve", ki[0:64, :], ka[0:64, :])
                CP("dve", ka[0:64, :], ki[0:64, :])
                STT("dve", t_[0:64, :], ka[0:64, :], -2.0 * PI, t_[0:64, :], ALU.mult, ALU.add)
                TS("dve", ka[0:64, :], t_[0:64, :], 0.0, 2.0 * PI, ALU.is_lt, ALU.mult)
                TT("dve", t_[0:64, :], t_[0:64, :], ka[0:64, :], ALU.add)
                ACTF(dsts[li][0:64, ts], t_[0:64, :], AF.Sin, bias=cst[0:64, CLAY["negpi"][0]:CLAY["negpi"][0] + 1])
        for hh in range(2):
            AR.reset(mark0)
            u = [AR.alloc([2, SEQ], BF16) for _ in range(3)]
            mark2 = AR.mark()
            hnb[0] = AR.alloc([8, SEQ], BF16)
            rmsnorm("mix_norm", l * 8)
            wslots = [AR.alloc([8, 256], BF16) for _ in range(2)]
            stage = AR.alloc([SEQ + 2], F32)
            acc = AR.alloc([SEQ], F32)
            MEMSET("pool", stage[:, 0:1], 0.0)
            MEMSET("pool", stage[:, SEQ + 1:SEQ + 2], 0.0)
            for s in range(3):
                def consume(m, tt, ps, s=s):
                    CP("act", stage[:, 1 + tt * 512:1 + (tt + 1) * 512], ps[:])
                    if tt == NT - 1:
                        ch = s * 4 + hh * 2 + m
                        w = [ccol("hy_conv", l * 36 + ch * 3 + k) for k in range(3)]
                        b = ccol("hy_bias", l * 12 + ch)
                        TS("dve", acc[:], stage[:, 0:SEQ], w[0], b, ALU.mult, ALU.add)
                        STT("dve", acc[:], stage[:, 1:SEQ + 1], w[1], acc[:], ALU.mult, ALU.add)
                        STT("dve", u[s][:, m, :], stage[:, 2:SEQ + 2], w[2], acc[:], ALU.mult, ALU.add)
                project(l, 2064 + s * 512 + hh * 256, 256, wslots, consume)
            for o in range(2):
                AR.reset(mark2)
                Kt = AR.alloc([32, 256], BF16)
                dft = [AR.alloc([2, 2048], BF16) for _ in range(2)]
                mark3 = AR.mark()
                woutT = AR.alloc([2, 256], F32)
                Pm = AR.alloc([16, 256], BF16)
                Mm = AR.alloc([16, 256], BF16)
                fw = Ring([AR.alloc([2, 256], F32) for _ in range(2)])
                wn = Ring([AR.alloc([256], F32) for _ in range(2)])
                d_misc.new_batch()
                for d in range(2):
                    c0 = d * 1024 + o * 512 + hh * 256
                    DMA("sp", d_misc, woutT[0:64, d, :], Wd_["hy_f_wout"][LW[l], :, c0:c0 + 256], new_batch=False)
                for i in range(16):
                    ps = psring.next()
                    for d in range(2):
                        MM(ps[:, d * 256:(d + 1) * 256], h3T[0:64, i * 128:(i + 1) * 128], woutT[0:64, d, :])
                    w_ = wn.next()
                    ACTF(w_[:], deltas[:, hh * 256:(hh + 1) * 256], AF.Exp, scale=ccol("negt", i))
                    f_ = fw.next()
                    psv = View(ps[:].ap.rearrange("p (a b) -> p a b", b=256), ps[:].pages)
                    wb = View(w_[:].ap.unsqueeze(1).to_broadcast([128, 2, 256]), w_[:].pages)
                    TT("dve", f_[:], psv, wb, ALU.mult)
                    if i == 0:
                        MEMSET("dve", f_[0:1, 1, :], 0.0)
                    TT("dve", Pm[:, i, :], f_[:, 0, :], f_[:, 1, :], ALU.add)
                    TT("dve", Mm[:, i, :], f_[:, 0, :], f_[:, 1, :], ALU.subtract)

                def load_fwd(j):
                    s = j % 2
                    d_dft[s].new_batch()
                    DMA("sp", d_dft[s], dft[s][:, 0, :], cf_d[j], new_batch=False)
                    DMA("sp", d_dft[s], dft[s][:, 1, :], sf_d[j], new_batch=False)

                load_fwd(0)
                for j in range(16):
                    s = j % 2
                    if j + 1 < 16:
                        load_fwd(j + 1)
                    psA = psring.next()
                    psB = psring.next()
                    for i in range(16):
                        MM(psA[:, 0:256], dft[s][:, 0, i * 128:(i + 1) * 128], Pm[:, i, :], start=(i == 0), stop=(i == 15))
                    for i in range(16):
                        MM(psB[:, 0:256], dft[s][:, 1, i * 128:(i + 1) * 128], Mm[:, i, :], start=(i == 0), stop=(i == 15))
                    CP("act", Kt[:, 2 * j, :], psA[:, 0:256])
                    CP("act", Kt[:, 2 * j + 1, :], psB[:, 0:256])
                    if j == 0:
                        psC = psring.next()
                        for i in range(16):
                            MM(psC[:, 0:256], dft[s][:, 1, i * 128:(i + 1) * 128], Pm[:, i, :], start=(i == 0), stop=(i == 15))
                        CP("act", Kt[0:1, 1, :], psC[0:1, 0:256])
                AR.reset(mark3)
                ztok = AR.alloc([16, 256], BF16)
                ctmp = Ring([AR.alloc([256], F32) for _ in range(4)])
                zT = u[0]
                gate = u[1 + o]
                for i in range(16):
                    ps = psring.next()
                    pbf = psbf[id(ps)]
                    for m in range(2):
                        src_v = zT[:, m, i * 128:(i + 1) * 128]
                        ov = pbf[:, m * 128:(m + 1) * 128]
                        P.pe(lambda e, ov=ov, src_v=src_v: e.transpose(ov.ap, src_v.ap, ident_bf[:].ap),
                             reads=[src_v, ident_bf[:]], writes=[ov])
                    CP("act", ztok[:, i, :], pbf[:, 0:256])
                load_fwd(0)
                for j in range(16):
                    s = j % 2
                    if j + 1 < 16:
                        load_fwd(j + 1)
                    psR = psring.next()
                    psI = psring.next()
                    for i in range(16):
                        MM(psR[:, 0:256], dft[s][:, 0, i * 128:(i + 1) * 128], ztok[:, i, :], start=(i == 0), stop=(i == 15))
                    for i in range(16):
                        MM(psI[:, 0:256], dft[s][:, 1, i * 128:(i + 1) * 128], ztok[:, i, :], start=(i == 0), stop=(i == 15))
                    kre = Kt[:, 2 * j, :]
                    kim = Kt[:, 2 * j + 1, :]
                    t1, t2, t3, t4 = (ctmp.next() for _ in range(4))
                    TT("dve", t1[:], psR[:, 0:256], kre, ALU.mult)
                    TT("dve", t2[:], psI[:, 0:256], kim, ALU.mult)
                    TT("dve", t3[:], psR[:, 0:256], kim, ALU.mult)
                    TT("dve", t4[:], psI[:, 0:256], kre, ALU.mult)
                    TT("dve", kre, t1[:], t2[:], ALU.subtract)
                    TT("dve", kim, t3[:], t4[:], ALU.add)
                    if j == 0:
                        CP("dve", Kt[0:1, 0, :], t1[0:1, :])
                        CP("dve", Kt[0:1, 1, :], t2[0:1, :])

                def load_inv(th, j):
                    s = j % 2
                    d_dft[s].new_batch()
                    DMA("sp", d_dft[s], dft[s][:, 0, 0:1024], ci_d[j * 128:(j + 1) * 128, th * 1024:(th + 1) * 1024], new_batch=False)
                    DMA("sp", d_dft[s], dft[s][:, 1, 0:1024], si_d[j * 128:(j + 1) * 128, th * 1024:(th + 1) * 1024], new_batch=False)

                for th in range(2):
                    accs = [[psring.next() for _ in range(2)] for _ in range(2)]
                    load_inv(th, 0)
                    for j in range(16):
                        s = j % 2
                        if j + 1 < 16:
                            load_inv(th, j + 1)
                        for m in range(2):
                            for tq in range(2):
                                MM(accs[m][tq][:], Kt[:, 2 * j, m * 128:(m + 1) * 128], dft[s][:, 0, tq * 512:(tq + 1) * 512],
                                   start=(j == 0), stop=False)
                                MM(accs[m][tq][:], Kt[:, 2 * j + 1, m * 128:(m + 1) * 128], dft[s][:, 1, tq * 512:(tq + 1) * 512],
                                   start=False, stop=(j == 15))
                    for m in range(2):
                        for tq in range(2):
                            tt = th * 2 + tq
                            ts = slice(tt * 512, (tt + 1) * 512)
                            sk = ccol("hy_skip", l * 8 + o * 4 + hh * 2 + m)
                            tv = accs[m][tq][:]
                            STT("dve", tv, zT[:, m, ts], sk, tv, ALU.mult, ALU.add)
                            dst = zT[:, m, ts] if o == 0 else ohy[:, hh * 2 + m, ts]
                            TT("dve", dst, tv, gate[:, m, ts], ALU.mult)
        AR.reset(mark0)
        sq = Ring([AR.alloc([512], BF16) for _ in range(3)])
        rs = Ring([AR.alloc([512], F32) for _ in range(2)])
        for tt in range(NT):
            ts = slice(tt * 512, (tt + 1) * 512)
            ps = psring.next()
            r = rs.next()
            rstd_from([ohy[:, m, ts] for m in range(4)], ps[:], r[:], sq, HYW, ccol("eps"))
            for m in range(4):
                STT("dve", ohy[:, m, ts], ohy[:, m, ts], ccol("hy_norm", l * 4 + m), r[:], ALU.mult, ALU.mult)
        hy_mark[0] = mark0

    def hyena_out(l):
        wout_apply(l, 512, 4, ohyb[0])

    def deltanet(l):
        AR.reset(hy_mark[0])
        hn = hnb[0] = AR.alloc([8, SEQ], BF16)
        rmsnorm("mix_norm", l * 8)
        wba = AR.alloc([8, 16], BF16)
        ba = AR.alloc([16, 16], F32)
        beta_t = AR.alloc([16, 8], F32)
        g_t = AR.alloc([16, 8], F32)
        gc_t = AR.alloc([16, 8], F32)
        ngc_t = AR.alloc([16, 8], F32)
        bw_t = AR.alloc([16, 8], F32)
        kds_t = AR.alloc([16, 8], F32)
        egl = AR.alloc([16, 16], F32)
        DMA("pool", d_wba, wba[:], Wd_["w_in"][LW[l], :, 2048:2064].rearrange("(k p) f -> p k f", p=128))
        ps = psring.next()
        for blk in range(16):
            for k in range(8):
                MM(ps[:, blk * 16:(blk + 1) * 16], hn[:, k, blk * 128:(blk + 1) * 128], wba[:, k, :], start=(k == 0), stop=(k == 7))
        bav = View(ba[:].ap.rearrange("p a b -> p (a b)"), ba[:].pages)
        CP("act", bav, ps[:, 0:256])
        dn_stop = cfg.get("dn_stop", 99)
        if dn_stop <= -3:
            return
        ACTF(beta_t[:], ba[:, :, 0:8], AF.Sigmoid)
        dtb = View(ccol("dtb", l * 8, 8).ap.unsqueeze(1).to_broadcast([128, 16, 8]), ccol("dtb", l * 8, 8).pages)
        nab = View(negA[:, l * 8:(l + 1) * 8].ap.unsqueeze(1).to_broadcast([128, 16, 8]), negA[:].pages)
        TT("dve", g_t[:], ba[:, :, 8:16], dtb, ALU.add)
        ACTF(g_t[:], g_t[:], AF.Exp)
        ACTF(g_t[:], g_t[:], AF.Ln, bias=ccol("one"))
        TT("dve", g_t[:], g_t[:], nab, ALU.mult)
        if dn_stop <= -2:
            return
        ps = psring.next()
        ps2 = psring.next()
        ps3 = psring.next()
        v3 = lambda v, b: View(v.ap.rearrange("p (a b) -> p a b", b=b), v.pages)
        MM(ps[:, 0:64], mat("cum_f"), g_t[:, :, 0:4])
        MM(ps[:, 64:128], mat("cum_b"), g_t[:, :, 4:8])
        MM(ps2[:, 0:128], mat("inda"), g_t[:])
        MM(ps2[:, 128:256], mat("indb"), g_t[:])
        MM(ps3[:, 0:128], mat("blk"), g_t[:])
        flat = lambda t: View(t[:].ap.rearrange("p a b -> p (a b)"), t[:].pages)
        if dn_stop <= -1:
            return
        nel = cfg.get("dn_nel", 99)
        elops = [
            lambda: CP("act", gc_t[:, :, 0:4], v3(ps[:, 0:64], 4)),
            lambda: CP("act", gc_t[:, :, 4:8], v3(ps[:, 64:128], 4)),
            lambda: TS("dve", flat(ngc_t), flat(gc_t), -1.0, None, ALU.mult),
            lambda: ACTF(egl[:, :, 0:8], v3(ps2[:, 0:128], 8), AF.Exp),
            lambda: ACTF(egl[:, :, 8:16], v3(ps2[:, 128:256], 8), AF.Exp),
            lambda: TT("dve", flat(kds_t), ps3[:, 0:128], flat(gc_t), ALU.subtract),
            lambda: ACTF(flat(kds_t), flat(kds_t), AF.Exp),
            lambda: ACTF(flat(bw_t), flat(gc_t), AF.Exp),
            lambda: TT("dve", flat(bw_t), flat(bw_t), flat(beta_t), ALU.mult),
        ]
        for f_ in elops[:nel]:
            f_()
        markh = AR.mark()
        dn_stop = cfg.get("dn_stop", 99)
        if dn_stop <= 0:
            return
        for h in range(cfg.get("dn_heads", 4)):
            AR.reset(markh)
            qkvz = [AR.alloc([SEQ], BF16) for _ in range(4)]
            qT, kT, vT, zg = qkvz
            ktok = AR.alloc([16, 128], BF16)
            vtok = AR.alloc([16, 128], BF16)
            oT = AR.alloc([SEQ], F32)
            S = [AR.alloc([128], F32) for _ in range(2)]
            Sb = [AR.alloc([128], BF16) for _ in range(2)]
            markp = AR.mark()
            wslots = [AR.alloc([8, 128], BF16) for _ in range(2)]
            stage = AR.alloc([SEQ + 4], F32)
            acc = AR.alloc([SEQ], F32)
            sq = Ring([AR.alloc([512], BF16) for _ in range(2)])
            rs = Ring([AR.alloc([512], F32) for _ in range(2)])
            MEMSET("pool", stage[:, 0:2], 0.0)
            MEMSET("pool", stage[:, SEQ + 2:SEQ + 4], 0.0)
            for s in range(4):
                def consume(m, tt, ps, s=s):
                    CP("act", stage[:, 2 + tt * 512:2 + (tt + 1) * 512], ps[:])
                    if tt != NT - 1:
                        return
                    if s == 3:
                        ACTF(zg[:], stage[:, 2:SEQ + 2], AF.Silu)
                        return
                    ch = s * 4 + h
                    w = [ccol("dn_conv", l * 60 + ch * 5 + k) for k in range(5)]
                    TS("dve", acc[:], stage[:, 0:SEQ], w[0], None, ALU.mult)
                    for k in range(1, 5):
                        STT("dve", acc[:], stage[:, k:SEQ + k], w[k], acc[:], ALU.mult, ALU.add)
                    if s == 2:
                        ACTF(vT[:], acc[:], AF.Silu)
                        return
                    ACTF(acc[:], acc[:], AF.Silu)
                    dst = qkvz[s]
                    for t2 in range(NT):
                        ts = slice(t2 * 512, (t2 + 1) * 512)
                        p2 = psring.next()
                        r = rs.next()
                        rstd_from([acc[:, ts]], p2[:], r[:], sq, 1.0, ccol("eps_l2"))
                        if s == 0:
                            STT("dve", dst[:, ts], acc[:, ts], 128.0 ** -0.5, r[:], ALU.mult, ALU.mult)
                        else:
                            TT("dve", dst[:, ts], acc[:, ts], r[:], ALU.mult)
                project(l, s * 512 + h * 128, 128, wslots, consume)
            for blk in range(16):
                bs = slice(blk * 128, (blk + 1) * 128)
                ps = psring.next()
                pbf = psbf[id(ps)]
                for m, srcT in enumerate((kT, vT)):
                    src_v = srcT[:, bs]
                    ov = pbf[:, m * 128:(m + 1) * 128]
                    P.pe(lambda e, ov=ov, src_v=src_v: e.transpose(ov.ap, src_v.ap, ident_bf[:].ap),
                         reads=[src_v, ident_bf[:]], writes=[ov])
                CP("act", ktok[:, blk, :], pbf[:, 0:128])
                CP("act", vtok[:, blk, :], pbf[:, 128:256])
            if dn_stop <= 1:
                continue
            AR.reset(markp)
            NF = 5
            NB = 15
            sets = [[{"f": [AR.alloc([128], F32) for _ in range(NF)], "b": [AR.alloc([128], BF16, align=256) for _ in range(NB)]}
                     for _ in range(2)] for _ in range(2)]
            MEMSET("pool", oT[:], 0.0)
            for d in range(2):
                MEMSET("pool", S[d][:], 0.0)
                MEMSET("pool", Sb[d][:], 0.0)
            ident = mat("ident")
            for step in range(cfg.get("dn_steps", 16)):
                for d in range(2):
                    blk = step if d == 0 else 15 - step
                    bs = slice(blk * 128, (blk + 1) * 128)
                    r_ = d * 4 + h
                    T = sets[d][step % 2]
                    dg, Ds, DTm, Er, Pm_ = T["f"]
                    Pb, kbg, vb, kdec, wT, attnT, qd, vn, N_, M_, Na, Ma, Nb_, Mb_, Pw = T["b"]
                    gcc = gc_t[:, blk, r_:r_ + 1]
                    ngc = ngc_t[:, blk, r_:r_ + 1]
                    bcol = beta_t[:, blk, r_:r_ + 1]
                    TS("dve", dg[:], ident, gcc, None, ALU.mult)
                    pR = psring.next()
                    MM(pR[:, 0:128], mat("ones"), dg[:])
                    STT("dve", Ds[:], pR[:, 0:128], -1.0, mat("neg_ls" if d == 0 else "neg_us"), ALU.mult, ALU.add)
                    ACTF(Ds[:], Ds[:], AF.Exp, bias=gcc)
                    TT("dve", DTm[:], pR[:, 0:128], mat("neg_ui" if d == 0 else "neg_li"), ALU.add)
                    ACTF(DTm[:], DTm[:], AF.Exp, bias=ngc)
                    ACTF(Er[:], pR[:, 0:128], AF.Exp)
                    pK = psring.next()
                    MM(pK[:, 0:128], kT[:, bs], kT[:, bs])
                    MM(pK[:, 128:256], kT[:, bs], qT[:, bs])
                    STT("dve", N_[:], pK[:, 0:128], bcol, Ds[:], ALU.mult, ALU.mult)
                    TT("dve", attnT[:], pK[:, 128:256], DTm[:], ALU.mult)
                    TT("dve", qd[:], qT[:, bs], Er[:], ALU.mult)
                    pre_stop = cfg.get("pre_stop", 99)
                    if pre_stop <= 2:
                        continue
                    pT = psring.next()
                    pTb = psbf[id(pT)]
                    P.pe(lambda e, o_=pTb[:, 0:128], i_=N_[:]: e.transpose(o_.ap, i_.ap, ident_bf[:].ap),
                         reads=[N_[:], ident_bf[:]], writes=[pTb[:, 0:128]])
                    CP("act", M_[:], pTb[:, 0:128])
                    STT("dve", Pm_[:], pTb[:, 0:128], -1.0, ident, ALU.mult, ALU.add)
                    STT("dve", Pw[:], pTb[:, 0:128], -1.0, ident, ALU.mult, ALU.add)
                    if pre_stop <= 3:
                        continue
                    Nc, Mc = N_, M_
                    nxt = [(Na, Ma), (Nb_, Mb_)]
                    for lvl in range(5):
                        N2, M2 = nxt[lvl % 2]
                        pq = psring.next()
                        MM(pq[:, 0:128], Mc[:], Nc[:])
                        if lvl < 4:
                            MM(pq[:, 128:256], Nc[:], Mc[:])
                        CP("act", N2[:], pq[:, 0:128])
                        if lvl < 4:
                            CP("act", M2[:], pq[:, 128:256])
                        pp = psring.next()
                        MM(pp[:, 0:128], N2[:], Pw[:])
                        if lvl < 4:
                            TT("dve", Pm_[:], Pm_[:], pp[:, 0:128], ALU.add)
                            CP("act", Pw[:], Pm_[:])
                        else:
                            TT("dve", Pb[:], Pm_[:], pp[:, 0:128], ALU.add)
                        Nc, Mc = N2, M2
                    if pre_stop <= 4:
                        continue
                    TS("dve", kbg[:], ktok[:, blk, :], bw_t[:, blk, r_:r_ + 1], None, ALU.mult)
                    TS("dve", vb[:], vtok[:, blk, :], bcol, None, ALU.mult)
                    TS("dve", kdec[:], ktok[:, blk, :], kds_t[:, blk, r_:r_ + 1], None, ALU.mult)
                    if pre_stop <= 5:
                        continue
                    pu = psring.next()
                    MM(pu[:, 0:128], Pb[:], vb[:])
                    MM(pu[:, 128:256], kbg[:], Pb[:])
                    u_ = dg
                    CP("act", u_[:], pu[:, 0:128])
                    CP("act", wT[:], pu[:, 128:256])
                    if dn_stop <= 2:
                        continue
                    for cc in ((0, 1) if d == 0 else (1, 0)):
                        r0 = cc * 64
                        rr = slice(r0, r0 + 64)
                        pw = psring.next()
                        MM(pw[rr, 0:128], wT[:, rr], Sb[d][:])
                        TT("dve", vn[rr, :], u_[rr, :], pw[rr, 0:128], ALU.subtract)
                        po = psring.next()
                        MM(po[:, 0:64], Sb[d][:], qd[:, rr], start=True, stop=False)
                        MM(po[:, 0:64], vn[rr, :], attnT[rr, rr], start=False, stop=True)
                        ov = oT[:, blk * 128 + r0:blk * 128 + r0 + 64]
                        TT("dve", ov, ov, po[:, 0:64], ALU.add)
                        pd = psring.next()
                        MM(pd[:, 0:128], kdec[rr, :], vn[rr, :])
                        STT("dve", S[d][:], S[d][:], egl[:, blk, cc * 8 + r_:cc * 8 + r_ + 1], pd[:, 0:128], ALU.mult, ALU.add)
                        CP("act", Sb[d][:], S[d][:])
            AR.reset(markp)
            sq = Ring([AR.alloc([512], BF16) for _ in range(2)])
            rs = Ring([AR.alloc([512], F32) for _ in range(2)])
            odn = AR.alloc([1, SEQ], BF16)
            for tt in range(NT):
                ts = slice(tt * 512, (tt + 1) * 512)
                p2 = psring.next()
                r = rs.next()
                rstd_from([oT[:, ts]], p2[:], r[:], sq, 128.0, ccol("eps"))
                STT("dve", oT[:, ts], oT[:, ts], ccol("dn_norm", l), r[:], ALU.mult, ALU.mult)
                TT("dve", odn[:, 0, ts], oT[:, ts], zg[:, ts], ALU.mult)
            wout_apply(l, h * 128, 1, odn)

    for l in layers:
        if "ffn1" in stages:
            ffn(l, 0)
        if "hy" in stages:
            hyena(l)
        else:
            AR.reset()
            hy_mark[0] = 0
        if "dn" in stages:
            deltanet(l)
        if "hy" in stages:
            hyena_out(l)
        if "ffn2" in stages:
            ffn(l, 1)
    if cfg.get("final_norm", True):
        AR.reset()
        rmsnorm("final_norm", 0, final=True)
    else:
        AR.reset()
        d_out.new_batch()
        for k in range(8):
            v = xT[:, k, :]
            P.dma("sp", d_out, lambda e, v=v, k=k: e.dma_start(out=outT_d[k * 128:(k + 1) * 128, :], in_=v.ap),
                  reads=[v], writes=[DramRes()], new_batch=False)
    for ds_ in [d_out] + d_outs:
        if ds_.batch is not None:
            P.final_batches.append(ds_.batch)
    stats = P.emit()
    P.es.close()
    return nc, stats


_CACHE = {}


def make_in_maps(inputs, cfg=None):
    stages = (cfg or {}).get("stages", ("ffn1", "dn", "hy", "ffn2"))
    st = static_consts()
    shared = {"cst": build_cst(inputs)}
    shared.update(st)
    for nm in ("ffn1_w_gate", "ffn1_w_up", "ffn1_w_down", "ffn2_w_gate", "ffn2_w_up", "ffn2_w_down",
               "w_in", "w_out", "hy_f_wout"):
        if nm.startswith("ffn") and nm[:4] not in stages:
            continue
        a_ = np.asarray(inputs[nm], dtype=np.float32)
        if cfg and "layers" in cfg:
            a_ = a_[list(cfg["layers"])]
        shared[nm] = np.ascontiguousarray(a_)
    maps = []
    xT_list = (cfg or {}).get("_xT")
    x = None if xT_list is not None else np.asarray(inputs["x"], dtype=np.float32)
    for b in range(8):
        m = dict(shared)
        m["xT"] = xT_list[b] if xT_list is not None else np.ascontiguousarray(x[b].T)
        maps.append(m)
    return maps


LAUNCH_GROUPS = ((0, 1, 2, 3),)


def _get_prog(cfg):
    key = repr(sorted((k, v) for k, v in cfg.items() if k != "_xT"))
    if key not in _CACHE:
        _CACHE[key] = build_program({k: v for k, v in cfg.items() if k != "_xT"})
    return _CACHE[key]


def kernel(**inputs):
    cfg = inputs.pop("_cfg", None)
    if cfg is not None:
        nc, stats = _get_prog(cfg)
        maps = make_in_maps(inputs, cfg)
        res = run_bass_kernel_spmd(nc, maps, core_ids=list(range(8)))
        out = np.stack([np.ascontiguousarray(r["outT"].T) for r in res.results], axis=0)
        return out.astype(np.float32)
    xT_list = None
    for gi, grp in enumerate(LAUNCH_GROUPS):
        last = gi == len(LAUNCH_GROUPS) - 1
        if len(LAUNCH_GROUPS) == 1:
            cfg = {}
        else:
            cfg = {"layers": tuple(grp), "final_norm": last}
        nc, stats = _get_prog(cfg)
        c2 = dict(cfg)
        if xT_list is not None:
            c2["_xT"] = xT_list
        maps = make_in_maps(inputs, c2)
        res = run_bass_kernel_spmd(nc, maps, core_ids=list(range(8)))
        xT_list = [np.ascontiguousarray(r["outT"]) for r in res.results]
    out = np.stack([np.ascontiguousarray(a.T) for a in xT_list], axis=0)
    return out.astype(np.float32)
```

```python
import math
import numpy as np
import ml_dtypes
from contextlib import ExitStack
import concourse.bass as bass
import concourse.mybir as mybir
from concourse.bass_utils import run_bass_kernel_spmd

F32 = mybir.dt.float32
BF16 = mybir.dt.bfloat16
ALU = mybir.AluOpType
AF = mybir.ActivationFunctionType

D_MODEL = 1024
SEQ = 2048
DEPTH = 4
D_FF = 2816
D_IN = 3600
HYW = 512
NT = 4
KC = 8
RMS_EPS = 1e-6

ENGS = ("pe", "act", "dve", "pool", "sp")
ENGATTR = {"pe": "tensor", "act": "scalar", "dve": "vector", "pool": "gpsimd", "sp": "sync"}
I32 = mybir.dt.int32
DTSIZE = {F32: 4, BF16: 2, I32: 4}


class Res:
    __slots__ = ("last_w", "readers", "psum")

    def __init__(self, psum=False):
        self.last_w = None
        self.readers = []
        self.psum = psum


class Batch:
    __slots__ = ("sem", "final")

    def __init__(self, sem):
        self.sem = sem
        self.final = 0


class DSem:
    def __init__(self, prog, name):
        self.prog = prog
        self.name = name
        self.gen = 0
        self.sem = prog.sem(name)
        self.count = 0
        self.batch = None
        self.prev = None

    def new_batch(self):
        self.prev = self.batch
        if self.count > 3000:
            self.gen += 1
            self.sem = self.prog.sem("%s_g%d" % (self.name, self.gen))
            self.count = 0
        self.batch = Batch(self.sem)
        self.batch.final = self.count
        return self.batch


class Instr:
    __slots__ = ("eng", "fn", "deps", "idx", "marked", "batch")

    def __init__(self, eng, fn):
        self.eng = eng
        self.fn = fn
        self.deps = []
        self.marked = False
        self.batch = None


class Mem:
    def __init__(self, prog, name, nbytes, page, space="sb"):
        self.name = name
        self.nbytes = nbytes
        self.page = page
        self.pages = [Res(psum=(space != "sb")) for _ in range((nbytes + page - 1) // page)]
        if space == "sb":
            self.t = prog.es.enter_context(prog.nc.sbuf_tensor(name, [128, nbytes // 4], F32))
        else:
            self.t = prog.es.enter_context(prog.nc.psum_tensor(name, [128, nbytes // 4], F32))
        self.bump = 0

    def reset(self, to=0):
        self.bump = to

    def mark(self):
        return self.bump

    def alloc(self, shape, dt, align=None):
        n = int(np.prod(shape)) * DTSIZE[dt]
        al = align or min(self.page, 512)
        base = (self.bump + al - 1) // al * al
        assert base + n <= self.nbytes, (self.name, base, n, self.nbytes)
        self.bump = base + n
        return Tile(self, base, shape, dt)


class View:
    __slots__ = ("ap", "pages")

    def __init__(self, ap, pages):
        self.ap = ap
        self.pages = pages


class Tile:
    def __init__(self, mem, base, shape, dt):
        self.mem = mem
        self.base = base
        self.shape = tuple(shape)
        self.dt = dt
        esz = DTSIZE[dt]
        n = int(np.prod(shape))
        assert base % 4 == 0 and (n * esz) % 4 == 0
        ap = mem.t[:, base // 4:(base + n * esz) // 4]
        if dt != F32:
            ap = ap.bitcast(dt)
        if len(shape) == 2:
            ap = ap.rearrange("p (a b) -> p a b", b=shape[1])
        elif len(shape) == 3:
            ap = ap.rearrange("p (a b c) -> p a b c", b=shape[1], c=shape[2])
        self.ap = ap
        self.strides = [int(np.prod(shape[i + 1:])) for i in range(len(shape))]

    def __getitem__(self, idx):
        if not isinstance(idx, tuple):
            idx = (idx,)
        ap = self.ap[idx]
        lo = 0
        hi = 0
        fidx = idx[1:]
        for d in range(len(self.shape)):
            if d < len(fidx):
                ix = fidx[d]
                if isinstance(ix, slice):
                    a = 0 if ix.start is None else ix.start
                    b = self.shape[d] if ix.stop is None else ix.stop
                else:
                    a, b = ix, ix + 1
            else:
                a, b = 0, self.shape[d]
            lo += a * self.strides[d]
            hi += (b - 1) * self.strides[d]
        esz = DTSIZE[self.dt]
        b0 = self.base + lo * esz
        b1 = self.base + (hi + 1) * esz
        pg = self.mem.page
        pages = self.mem.pages[b0 // pg:(b1 + pg - 1) // pg]
        return View(ap, pages)

    def all(self):
        return self[:]


class Ring:
    def __init__(self, items):
        self.items = items
        self.i = 0

    def next(self):
        it = self.items[self.i % len(self.items)]
        self.i += 1
        return it


class Prog:
    def __init__(self, nc, same_eng_sync=True):
        self.nc = nc
        self.es = ExitStack()
        self.streams = {e: [] for e in ENGS}
        self.same_eng_sync = same_eng_sync
        self.final_batches = []

    def sem(self, name):
        return self.es.enter_context(self.nc.semaphore(name))

    def op(self, eng, fn, reads=(), writes=(), batch=None, dsem=None):
        ins = Instr(eng, fn)
        st = self.streams[eng]
        ins.idx = len(st)
        st.append(ins)
        deps = ins.deps
        rp = []
        for r in reads:
            rp.extend(r.pages)
        wp = []
        for w in writes:
            wp.extend(w.pages)
        for r in rp:
            if r.last_w is not None:
                deps.append(r.last_w)
            if r.psum:
                for t_ in r.readers:
                    if t_[0] == "eng" and t_[1] != eng:
                        deps.append(t_)
        for w in wp:
            if w.last_w is not None:
                deps.append(w.last_w)
            deps.extend(w.readers)
        if batch is not None:
            ins.deps = deps = [d for d in deps if not (d[0] == "dma" and d[1] is batch)]
            ins.batch = batch
            dsem.count += 16
            batch.final = dsem.count
            tok = ("dma", batch)
        else:
            tok = ("eng", eng, ins.idx)
        for r in rp:
            r.readers.append(tok)
        for w in wp:
            w.last_w = tok
            w.readers = []
        return ins

    def pe(self, fn, reads=(), writes=()):
        return self.op("pe", fn, reads, writes)

    def act(self, fn, reads=(), writes=()):
        return self.op("act", fn, reads, writes)

    def dve(self, fn, reads=(), writes=()):
        return self.op("dve", fn, reads, writes)

    def pool(self, fn, reads=(), writes=()):
        return self.op("pool", fn, reads, writes)

    def dma(self, eng, dsem, fn, reads=(), writes=(), new_batch=True):
        if new_batch or dsem.batch is None:
            dsem.new_batch()
        ins = self.op(eng, fn, reads, writes, batch=dsem.batch, dsem=dsem)
        pv = getattr(dsem, "prev", None)
        if pv is not None:
            ins.deps.append(("dma", pv))
        return ins

    def emit(self):
        nc = self.nc
        sync_same = self.same_eng_sync
        for e in ENGS:
            for ins in self.streams[e]:
                for d in ins.deps:
                    if d[0] == "eng":
                        if d[1] == ins.eng and (ins.eng == "pe" or (not sync_same and ins.eng != "pool")):
                            continue
                        self.streams[d[1]][d[2]].marked = True
        cum = {}
        for e in ENGS:
            c = 0
            arr = []
            for ins in self.streams[e]:
                if ins.marked and ins.batch is None:
                    c += 1
                arr.append(c)
            cum[e] = arr
        SEG = 4000
        esem = {e: [self.sem("s_%s%d" % (e, k)) for k in range(max(1, (cum[e][-1] + SEG - 1) // SEG if cum[e] else 1))]
                for e in ENGS}
        stats = {"waits": 0, "incs": 0, "n": {e: len(self.streams[e]) for e in ENGS}}
        final_batches = self.final_batches
        with nc.Block() as block:
            for e in ENGS:
                stream = self.streams[e]

                def body(engine, e=e, stream=stream):
                    waited = {}
                    for ins in stream:
                        need = {}
                        for d in ins.deps:
                            if d[0] == "eng":
                                if d[1] == e and (e == "pe" or (not sync_same and e != "pool")):
                                    continue
                                c_ = cum[d[1]][d[2]]
                                k_ = (c_ - 1) // SEG
                                key = (d[1], k_)
                                sem = esem[d[1]][k_]
                                val = (c_ - 1) % SEG + 1
                            else:
                                key = id(d[1].sem)
                                sem = d[1].sem
                                val = d[1].final
                            if waited.get(key, 0) >= val:
                                continue
                            if key not in need or need[key][1] < val:
                                need[key] = (sem, val)
                        for key, (sem, val) in need.items():
                            engine.wait_ge(sem, val)
                            waited[key] = val
                            stats["waits"] += 1
                        bi = ins.fn(engine)
                        if ins.batch is not None:
                            bi.then_inc(ins.batch.sem, 16)
                        elif ins.marked:
                            bi.then_inc(esem[e][(cum[e][ins.idx] - 1) // SEG], 1)
                            stats["incs"] += 1
                    if e == "sp":
                        for b in final_batches:
                            engine.wait_ge(b.sem, b.final)

                getattr(block, ENGATTR[e])(body)
        return stats


class DramRes:
    def __init__(self):
        self.pages = [Res()]


PI = math.pi
CLAY = {}
NCST = 0


def _mk_layout():
    global NCST
    off = 0
    for name, n in (("ffn1_norm", DEPTH * 8), ("mix_norm", DEPTH * 8), ("ffn2_norm", DEPTH * 8), ("final_norm", 8),
                    ("eps", 1), ("eps_l2", 1), ("one", 1), ("negpi", 1),
                    ("dn_conv", DEPTH * 60), ("dn_norm", DEPTH), ("hy_conv", DEPTH * 36), ("hy_bias", DEPTH * 12),
                    ("hy_skip", DEPTH * 8), ("hy_norm", DEPTH * 4), ("dtb", DEPTH * 8), ("alog", DEPTH * 8),
                    ("f_b1", DEPTH), ("f_b2", DEPTH), ("f_b3", DEPTH), ("f_freq", DEPTH), ("negt", 16),
                    ("f_w1", DEPTH * 64), ("f_w2", DEPTH * 64), ("f_w3", DEPTH * 64)):
        CLAY[name] = (off, n)
        off += n
    NCST = off


_mk_layout()
MATS = ("ident", "ones", "neg_ls", "neg_us", "neg_li", "neg_ui", "cum_f", "cum_b", "inda", "indb", "blk")
NEGV = -30000.0


def _chunked(v):
    return np.ascontiguousarray(np.asarray(v, np.float32).reshape(-1, 128).T)


def build_cst(inp):
    cst = np.zeros((128, NCST), np.float32)

    def put(name, arr, rows=128):
        o, n = CLAY[name]
        arr = np.asarray(arr, np.float32)
        assert arr.shape == (rows, n), (name, arr.shape, n)
        cst[:rows, o:o + n] = arr

    for nm in ("ffn1_norm", "mix_norm", "ffn2_norm"):
        put(nm, np.concatenate([_chunked(inp[nm][l]) for l in range(DEPTH)], axis=1))
    put("final_norm", _chunked(inp["final_norm"]))
    cst[:, CLAY["eps"][0]] = RMS_EPS
    cst[:, CLAY["eps_l2"][0]] = 1e-6
    cst[:, CLAY["one"][0]] = 1.0
    cst[:, CLAY["negpi"][0]] = -math.pi
    a = np.asarray(inp["dn_conv"], np.float32).reshape(DEPTH, 5, 12, 128).transpose(3, 0, 2, 1).reshape(128, -1)
    put("dn_conv", a)
    put("dn_norm", np.asarray(inp["dn_norm"], np.float32).T)
    a = np.asarray(inp["hy_conv"], np.float32).reshape(DEPTH, 3, 12, 128).transpose(3, 0, 2, 1).reshape(128, -1)
    put("hy_conv", a)
    a = np.asarray(inp["hy_conv_bias"], np.float32).reshape(DEPTH, 12, 128).transpose(2, 0, 1).reshape(128, -1)
    put("hy_bias", a)
    a = np.asarray(inp["hy_skip"], np.float32).reshape(DEPTH, 2, 4, 128).transpose(3, 0, 1, 2).reshape(128, -1)
    put("hy_skip", a)
    a = np.asarray(inp["hy_norm"], np.float32).reshape(DEPTH, 4, 128).transpose(2, 0, 1).reshape(128, -1)
    put("hy_norm", a)
    put("dtb", np.broadcast_to(np.asarray(inp["dn_dt_bias"], np.float32).reshape(1, -1), (128, DEPTH * 8)))
    put("alog", np.broadcast_to(np.asarray(inp["dn_a_log"], np.float32).reshape(1, -1), (128, DEPTH * 8)))
    for nm, key in (("f_b1", "hy_f_b1"), ("f_b2", "hy_f_b2"), ("f_b3", "hy_f_b3"), ("f_freq", "hy_f_freq")):
        put(nm, np.asarray(inp[key], np.float32).T, rows=64)
    t = np.linspace(0.0, 1.0, SEQ, dtype=np.float32)
    put("negt", -t.reshape(16, 128).T)
    put("f_w1", np.asarray(inp["hy_f_w1"], np.float32).transpose(1, 0, 2).reshape(33, -1), rows=33)
    put("f_w2", np.asarray(inp["hy_f_w2"], np.float32).transpose(1, 0, 2).reshape(64, -1), rows=64)
    put("f_w3", np.asarray(inp["hy_f_w3"], np.float32).transpose(1, 0, 2).reshape(64, -1), rows=64)
    return cst


_STATIC = {}


def static_consts():
    if _STATIC:
        return _STATIC
    i = np.arange(128)
    same = (i[:, None] // 64) == (i[None, :] // 64)
    row, col = i[:, None], i[None, :]
    m = {}
    m["ident"] = np.eye(128)
    m["ones"] = np.ones((128, 128))
    m["neg_ls"] = np.where(same & (row > col), 0.0, NEGV)
    m["neg_us"] = np.where(same & (row < col), 0.0, NEGV)
    m["neg_li"] = np.where(same & (row >= col), 0.0, NEGV)
    m["neg_ui"] = np.where(same & (row <= col), 0.0, NEGV)
    m["cum_f"] = (same & (row <= col)).astype(np.float64)
    m["cum_b"] = (same & (row >= col)).astype(np.float64)
    m["inda"] = np.broadcast_to((i < 64)[:, None], (128, 128)).astype(np.float64)
    m["indb"] = np.broadcast_to((i >= 64)[:, None], (128, 128)).astype(np.float64)
    m["blk"] = same.astype(np.float64)
    _STATIC["mats"] = np.ascontiguousarray(np.concatenate([m[k] for k in MATS], axis=1), dtype=np.float32)
    l = SEQ
    t = np.linspace(0.0, 1.0, l, dtype=np.float32)[:, None]
    bands = 16
    ang = ((np.float32(2.0 * math.pi / l)) * np.arange(l, dtype=np.float32)[:, None]
           * np.linspace(1e-4, bands - 1, bands, dtype=np.float32)[None, :]).astype(np.float32)
    feats = np.concatenate([t, np.cos(ang), -np.sin(ang)], axis=-1).astype(np.float32)
    _STATIC["featsT"] = np.ascontiguousarray(feats.T)
    max_decay = math.log(1e-2) / 0.3
    min_decay = math.log(1e-2) / 1.5
    deltas = np.abs(np.linspace(min_decay, max_decay, HYW, dtype=np.float32))
    _STATIC["deltas"] = np.ascontiguousarray(np.broadcast_to(deltas[None, :], (128, HYW)), dtype=np.float32)
    n = np.arange(SEQ, dtype=np.float64)
    ang = 2.0 * np.pi * np.outer(n, n) / (2 * SEQ)
    cf = np.cos(ang)
    sf = -np.sin(ang)
    sf[:, 0] = (-1.0) ** n
    ci = (2.0 / (2 * SEQ)) * np.cos(ang)
    ci[0, :] = 1.0 / (2 * SEQ)
    si = -(2.0 / (2 * SEQ)) * np.sin(ang)
    si[0, :] = ((-1.0) ** n) / (2 * SEQ)
    bf = ml_dtypes.bfloat16

    def fwd_layout(a):
        return np.ascontiguousarray(a.reshape(16, 128, 16, 128).transpose(2, 1, 0, 3).reshape(16, 128, 2048)).astype(bf)

    _STATIC["dft_cf"] = fwd_layout(cf)
    _STATIC["dft_sf"] = fwd_layout(sf)
    _STATIC["dft_ci"] = np.ascontiguousarray(ci).astype(bf)
    _STATIC["dft_si"] = np.ascontiguousarray(si).astype(bf)
    return _STATIC


def build_program(cfg):
    n_layers = cfg.get("n_layers", DEPTH)
    layers = tuple(cfg.get("layers", tuple(range(n_layers))))
    LW = {l_: i_ for i_, l_ in enumerate(layers)}
    LD = len(layers) if "layers" in cfg else DEPTH
    if "layers" not in cfg:
        LW = {l_: l_ for l_ in range(DEPTH)}
    stages = cfg.get("stages", ("ffn1", "dn", "hy", "ffn2"))
    nc = bass.Bass("TRN2", target_bir_lowering=False)
    P = Prog(nc, same_eng_sync=cfg.get("same_eng_sync", True))

    def din(name, shape, dt=F32):
        return nc.dram_tensor(name, list(shape), dt, kind="ExternalInput").ap()

    xT_d = din("xT", [D_MODEL, SEQ])
    cst_d = din("cst", [128, NCST])
    mats_d = din("mats", [128, len(MATS) * 128])
    featsT_d = din("featsT", [33, SEQ])
    deltas_d = din("deltas", [128, HYW])
    cf_d = din("dft_cf", [16, 128, 2048], BF16)
    sf_d = din("dft_sf", [16, 128, 2048], BF16)
    ci_d = din("dft_ci", [2048, 2048], BF16)
    si_d = din("dft_si", [2048, 2048], BF16)
    Wd_ = {}
    for nm, shp in (("ffn1_w_gate", [LD, D_MODEL, D_FF]), ("ffn1_w_up", [LD, D_MODEL, D_FF]),
                    ("ffn1_w_down", [LD, D_FF, D_MODEL]), ("ffn2_w_gate", [LD, D_MODEL, D_FF]),
                    ("ffn2_w_up", [LD, D_MODEL, D_FF]), ("ffn2_w_down", [LD, D_FF, D_MODEL]),
                    ("w_in", [LD, D_MODEL, D_IN]), ("w_out", [LD, D_MODEL, D_MODEL]),
                    ("hy_f_wout", [LD, 64, 2048])):
        if nm.startswith("ffn") and nm[:4] not in stages:
            continue
        Wd_[nm] = din(nm, shp)
    outT_d = nc.dram_tensor("outT", [D_MODEL, SEQ], F32, kind="ExternalOutput").ap()

    CM = Mem(P, "cm", 15 * 1024, 256)
    AR = Mem(P, "arena", cfg.get("arena_kib", 128) * 1024, 512)
    XM = Mem(P, "xT_sb", 8 * SEQ * 4, 2048)
    xT = XM.alloc([8, SEQ], F32)
    hnb = [None]
    PS = [Mem(P, "ps%d" % i, 2048, 2048, space="ps") for i in range(8)]
    psb = [m.alloc([512], F32) for m in PS]
    psbf = {id(t): Tile(t.mem, 0, [256], BF16) for t in psb}
    psring = Ring(psb)

    cst = CM.alloc([NCST], F32)
    mats = CM.alloc([len(MATS), 128], F32)
    ones_bf = CM.alloc([128], BF16)
    ident_bf = CM.alloc([128], BF16)
    negA = CM.alloc([DEPTH * 8], F32)
    deltas = CM.alloc([HYW], F32)

    def mat(name):
        return mats[:, MATS.index(name), :]

    d_const = DSem(P, "d_const")
    d_x = DSem(P, "d_x")
    d_out = DSem(P, "d_out")
    d_outs = [DSem(P, "d_out%d" % i) for i in range(3)]

    def ccol(name, i=0, n=1, rows=None):
        o, _ = CLAY[name]
        if rows is None:
            return cst[:, o + i:o + i + n]
        return cst[0:rows, o + i:o + i + n]

    def MM(out, lhsT, rhs, start=True, stop=True):
        P.pe(lambda e: e.matmul(out.ap, lhsT=lhsT.ap, rhs=rhs.ap, start=start, stop=stop),
             reads=[lhsT, rhs], writes=[out])

    def ACTF(out, in_, func, bias=None, scale=1.0):
        rd = [in_]
        kw = {}
        if bias is not None:
            rd.append(bias)
            kw["bias"] = bias.ap
        if isinstance(scale, View):
            rd.append(scale)
            kw["scale"] = scale.ap
        else:
            kw["scale"] = scale
        P.act(lambda e: e.activation(out=out.ap, in_=in_.ap, func=func, **kw), reads=rd, writes=[out])

    def TT(eng, out, in0, in1, op):
        P.op(eng, lambda e: e.tensor_tensor(out=out.ap, in0=in0.ap, in1=in1.ap, op=op), reads=[in0, in1], writes=[out])

    def TS(eng, out, in0, s1, s2, op0, op1=None):
        rd = [in0]
        a1 = s1
        a2 = s2
        if isinstance(s1, View):
            rd.append(s1)
            a1 = s1.ap
        if isinstance(s2, View):
            rd.append(s2)
            a2 = s2.ap
        if op1 is None:
            P.op(eng, lambda e: e.tensor_scalar(out=out.ap, in0=in0.ap, scalar1=a1, scalar2=None, op0=op0),
                 reads=rd, writes=[out])
        else:
            P.op(eng, lambda e: e.tensor_scalar(out=out.ap, in0=in0.ap, scalar1=a1, scalar2=a2, op0=op0, op1=op1),
                 reads=rd, writes=[out])

    def STT(eng, out, in0, sc, in1, op0, op1):
        rd = [in0, in1]
        a = sc
        if isinstance(sc, View):
            rd.append(sc)
            a = sc.ap
        P.op(eng, lambda e: e.scalar_tensor_tensor(out=out.ap, in0=in0.ap, scalar=a, in1=in1.ap, op0=op0, op1=op1),
             reads=rd, writes=[out])

    def CP(eng, out, in_):
        if eng == "act":
            P.act(lambda e: e.copy(out=out.ap, in_=in_.ap), reads=[in_], writes=[out])
        else:
            P.op(eng, lambda e: e.tensor_copy(out=out.ap, in_=in_.ap), reads=[in_], writes=[out])

    def MEMSET(eng, out, val):
        P.op(eng, lambda e: e.memset(out.ap, val), writes=[out])

    def DMA(eng, dsem, out, src_ap, new_batch=True):
        P.dma(eng, dsem, lambda e: e.dma_start(out=out.ap, in_=src_ap), writes=[out], new_batch=new_batch)

    def bc(view, shape):
        return View(view.ap.to_broadcast(list(shape)), view.pages)

    DMA("sp", d_const, cst[:], cst_d)
    DMA("sp", d_const, mats[:], mats_d.rearrange("p (m c) -> p m c", c=128), new_batch=False)
    DMA("sp", d_const, deltas[:], deltas_d, new_batch=False)
    MEMSET("pool", ones_bf[:], 1.0)
    CP("dve", ident_bf[:], mat("ident"))
    ACTF(negA[:], ccol("alog", 0, DEPTH * 8), AF.Exp)
    TS("dve", negA[:], negA[:], -1.0, None, ALU.mult)
    d_x.new_batch()
    for k in range(8):
        DMA("sp", d_x, xT[:, k, :], xT_d[k * 128:(k + 1) * 128, :], new_batch=False)

    def rstd_from(chunks, ps, r, sqring, mean_div, eps_view):
        n = len(chunks)
        for k, cv in enumerate(chunks):
            s = sqring.next()
            sv = s[:, 0:cv.ap.shape[-1]]
            ACTF(sv, cv, AF.Square)
            MM(ps, ones_bf[:], sv, start=(k == 0), stop=(k == n - 1))
        ACTF(r, ps, AF.Sqrt, bias=eps_view, scale=1.0 / mean_div)
        P.dve(lambda e: e.reciprocal(out=r.ap, in_=r.ap), reads=[r], writes=[r])

    def rmsnorm(wname, wcol0, final=False):
        hn = hnb[0]
        m0_ = AR.mark()
        sq = Ring([AR.alloc([512], BF16) for _ in range(3)])
        rs = Ring([AR.alloc([512], F32) for _ in range(2)])
        ost = Ring([AR.alloc([512], F32) for _ in range(3)]) if final else None
        oi = 0
        for tt in range(NT):
            ts = slice(tt * 512, (tt + 1) * 512)
            ps = psring.next()
            r = rs.next()
            rstd_from([xT[:, k, ts] for k in range(8)], ps[:], r[:], sq, D_MODEL, ccol("eps"))
            for k in range(8):
                xv = xT[:, k, ts]
                wv = ccol(wname, wcol0 + k)
                if not final:
                    STT("dve", hn[:, k, ts], xv, wv, r[:], ALU.mult, ALU.mult)
                else:
                    o = ost.next()
                    STT("dve", o[:], xv, wv, r[:], ALU.mult, ALU.mult)
                    P.dma("sp", d_outs[oi % 3], lambda e, o=o, k=k, ts=ts: e.dma_start(
                        out=outT_d[k * 128:(k + 1) * 128, ts], in_=o[:].ap),
                        reads=[o[:]], writes=[DramRes()], new_batch=True)
                    oi += 1
        AR.reset(m0_)

    d_wgu = [DSem(P, "d_wgu%d" % i) for i in range(2)]
    d_wd = [DSem(P, "d_wd%d" % i) for i in range(2)]
    FGROUPS = [(0, 4), (4, 4), (8, 4), (12, 4), (16, 3), (19, 3)]

    def ffn(l, which):
        pre = "ffn1" if which == 0 else "ffn2"
        Wg = Wd_[pre + "_w_gate"]
        Wu = Wd_[pre + "_w_up"]
        Wdn = Wd_[pre + "_w_down"]
        AR.reset()
        hn = hnb[0] = AR.alloc([8, SEQ], BF16)
        rmsnorm(pre + "_norm", l * 8)
        wg = [AR.alloc([8, 512], BF16) for _ in range(2)]
        wu = [AR.alloc([8, 512], BF16) for _ in range(2)]
        wd = [AR.alloc([4, 1024], BF16) for _ in range(2)]
        actb = [AR.alloc([4, SEQ], BF16) for _ in range(2)]
        stmp = Ring([AR.alloc([512], F32) for _ in range(3)])

        def load(gi):
            f0, nf = FGROUPS[gi]
            s = gi % 2
            d_wgu[s].new_batch()
            for (W, t) in ((Wg, wg[s]), (Wu, wu[s])):
                src = W[LW[l], :, f0 * 128:(f0 + nf) * 128].rearrange("(k p) f -> p k f", p=128)
                DMA("pool", d_wgu[s], t[:, :, 0:nf * 128], src, new_batch=False)
            src = Wdn[LW[l], f0 * 128:(f0 + nf) * 128, :].rearrange("(j p) d -> p j d", p=128)
            DMA("pool", d_wd[s], wd[s][:, 0:nf, :], src)

        load(0)
        for gi, (f0, nf) in enumerate(FGROUPS):
            s = gi % 2
            if gi + 1 < len(FGROUPS):
                load(gi + 1)
            ab = actb[s]
            for fj in range(nf):
                fs = slice(fj * 128, (fj + 1) * 128)
                for tt in range(NT):
                    ts = slice(tt * 512, (tt + 1) * 512)
                    gps = psring.next()
                    ups = psring.next()
                    for (pst, wt) in ((gps, wg[s]), (ups, wu[s])):
                        for k in range(8):
                            MM(pst[:], wt[:, k, fs], hn[:, k, ts], start=(k == 0), stop=(k == 7))
                    st = stmp.next()
                    ACTF(st[:], gps[:], AF.Silu)
                    TT("dve", ab[:, fj, ts], ups[:], st[:], ALU.mult)
            for dk in range(8):
                ds_ = slice(dk * 128, (dk + 1) * 128)
                for tt in range(NT):
                    ts = slice(tt * 512, (tt + 1) * 512)
                    ops = psring.next()
                    for fj in range(nf):
                        MM(ops[:], wd[s][:, fj, ds_], ab[:, fj, ts], start=(fj == 0), stop=(fj == nf - 1))
                    xv = xT[:, dk, ts]
                    STT("dve", xv, ops[:], 0.5, xv, ALU.mult, ALU.add)

    d_win = [DSem(P, "d_win%d" % i) for i in range(2)]
    d_wo = DSem(P, "d_wo")
    d_dft = [DSem(P, "d_dft%d" % i) for i in range(2)]
    d_misc = DSem(P, "d_misc")
    d_wba = DSem(P, "d_wba")
    win_i = [0]
    ohyb = [None]
    hy_mark = [0]

    def wout_apply(l, row0, nchunks, src_tile):
        wo = AR.alloc([nchunks, 1024], BF16)
        src = Wd_["w_out"][LW[l], row0:row0 + 128 * nchunks, :].rearrange("(m p) d -> p m d", p=128)
        DMA("pool", d_wo, wo[:], src)
        for dk in range(8):
            for tt in range(NT):
                ts = slice(tt * 512, (tt + 1) * 512)
                ps = psring.next()
                for m in range(nchunks):
                    MM(ps[:], wo[:, m, dk * 128:(dk + 1) * 128], src_tile[:, m, ts], start=(m == 0), stop=(m == nchunks - 1))
                xv = xT[:, dk, ts]
                TT("dve", xv, ps[:], xv, ALU.add)

    def project(l, col0, ncols, wslots, consume):
        hn = hnb[0]
        s = win_i[0] % 2
        win_i[0] += 1
        wt = wslots[s]
        src = Wd_["w_in"][LW[l], :, col0:col0 + ncols].rearrange("(k p) f -> p k f", p=128)
        DMA("pool", d_win[s], wt[:, :, 0:ncols], src)
        for m in range(ncols // 128):
            for tt in range(NT):
                ts = slice(tt * 512, (tt + 1) * 512)
                ps = psring.next()
                for k in range(8):
                    MM(ps[:], wt[:, k, m * 128:(m + 1) * 128], hn[:, k, ts], start=(k == 0), stop=(k == 7))
                consume(m, tt, ps)

    def hyena(l):
        AR.reset()
        h3T = AR.alloc([SEQ], F32)
        ohy = ohyb[0] = AR.alloc([4, SEQ], BF16)
        fcst = AR.alloc([8], F32)
        mark0 = AR.mark()
        featsT = AR.alloc([SEQ], F32)
        hA = AR.alloc([SEQ], F32)
        hB = AR.alloc([SEQ], F32)
        tmp = Ring([AR.alloc([512], F32) for _ in range(4)])
        tmpi = Ring([AR.alloc([512], I32) for _ in range(2)])
        DMA("sp", d_misc, featsT[0:33, :], featsT_d)
        freq = ccol("f_freq", l, rows=64)
        for li, bn in enumerate(("f_b1", "f_b2", "f_b3")):
            TS("dve", fcst[0:64, li:li + 1], ccol(bn, l, rows=64), freq, 17.0 * PI, ALU.mult, ALU.add)
        srcs = [featsT, hA, hB]
        dsts = [hA, hB, h3T]
        for li in range(3):
            kk = 33 if li == 0 else 64
            wv = cst[0:kk, CLAY["f_w%d" % (li + 1)][0] + l * 64: CLAY["f_w%d" % (li + 1)][0] + (l + 1) * 64]
            for tt in range(NT):
                ts = slice(tt * 512, (tt + 1) * 512)
                ps = psring.next()
                MM(ps[0:64, :], wv, srcs[li][0:kk, ts])
                t_ = tmp.next()
                TS("dve", t_[0:64, :], ps[0:64, :], freq, fcst[0:64, li:li + 1], ALU.mult, ALU.add)
                ka = tmp.next()
                ki = tmpi.next()
                TS("dve", ka[0:64, :], t_[0:64, :], 1.0 / (2.0 * PI), None, ALU.mult)
                CP("dve", ki[0:64, :], ka[0:64, :])
                CP("dve", ka[0:64, :], ki[0:64, :])
                STT("dve", t_[0:64, :], ka[0:64, :], -2.0 * PI, t_[0:64, :], ALU.mult, ALU.add)
                TS("dve", ka[0:64, :], t_[0:64, :], 0.0, 2.0 * PI, ALU.is_lt, ALU.mult)
                TT("dve", t_[0:64, :], t_[0:64, :], ka[0:64, :], ALU.add)
                ACTF(dsts[li][0:64, ts], t_[0:64, :], AF.Sin, bias=cst[0:64, CLAY["negpi"][0]:CLAY["negpi"][0] + 1])
        for hh in range(2):
            AR.reset(mark0)
            u = [AR.alloc([2, SEQ], BF16) for _ in range(3)]
            mark2 = AR.mark()
            hnb[0] = AR.alloc([8, SEQ], BF16)
            rmsnorm("mix_norm", l * 8)
            wslots = [AR.alloc([8, 256], BF16) for _ in range(2)]
            stage = AR.alloc([SEQ + 2], F32)
            acc = AR.alloc([SEQ], F32)
            MEMSET("pool", stage[:, 0:1], 0.0)
            MEMSET("pool", stage[:, SEQ + 1:SEQ + 2], 0.0)
            for s in range(3):
                def consume(m, tt, ps, s=s):
                    CP("act", stage[:, 1 + tt * 512:1 + (tt + 1) * 512], ps[:])
                    if tt == NT - 1:
                        ch = s * 4 + hh * 2 + m
                        w = [ccol("hy_conv", l * 36 + ch * 3 + k) for k in range(3)]
                        b = ccol("hy_bias", l * 12 + ch)
                        TS("dve", acc[:], stage[:, 0:SEQ], w[0], b, ALU.mult, ALU.add)
                        STT("dve", acc[:], stage[:, 1:SEQ + 1], w[1], acc[:], ALU.mult, ALU.add)
                        STT("dve", u[s][:, m, :], stage[:, 2:SEQ + 2], w[2], acc[:], ALU.mult, ALU.add)
                project(l, 2064 + s * 512 + hh * 256, 256, wslots, consume)
            for o in range(2):
                AR.reset(mark2)
                Kt = AR.alloc([32, 256], BF16)
                dft = [AR.alloc([2, 2048], BF16) for _ in range(2)]
                mark3 = AR.mark()
                woutT = AR.alloc([2, 256], F32)
                Pm = AR.alloc([16, 256], BF16)
                Mm = AR.alloc([16, 256], BF16)
                fw = Ring([AR.alloc([2, 256], F32) for _ in range(2)])
                wn = Ring([AR.alloc([256], F32) for _ in range(2)])
                d_misc.new_batch()
                for d in range(2):
                    c0 = d * 1024 + o * 512 + hh * 256
                    DMA("sp", d_misc, woutT[0:64, d, :], Wd_["hy_f_wout"][LW[l], :, c0:c0 + 256], new_batch=False)
                for i in range(16):
                    ps = psring.next()
                    for d in range(2):
                        MM(ps[:, d * 256:(d + 1) * 256], h3T[0:64, i * 128:(i + 1) * 128], woutT[0:64, d, :])
                    w_ = wn.next()
                    ACTF(w_[:], deltas[:, hh * 256:(hh + 1) * 256], AF.Exp, scale=ccol("negt", i))
                    f_ = fw.next()
                    psv = View(ps[:].ap.rearrange("p (a b) -> p a b", b=256), ps[:].pages)
                    wb = View(w_[:].ap.unsqueeze(1).to_broadcast([128, 2, 256]), w_[:].pages)
                    TT("dve", f_[:], psv, wb, ALU.mult)
                    if i == 0:
                        MEMSET("dve", f_[0:1, 1, :], 0.0)
                    TT("dve", Pm[:, i, :], f_[:, 0, :], f_[:, 1, :], ALU.add)
                    TT("dve", Mm[:, i, :], f_[:, 0, :], f_[:, 1, :], ALU.subtract)

                def load_fwd(j):
                    s = j % 2
                    d_dft[s].new_batch()
                    DMA("sp", d_dft[s], dft[s][:, 0, :], cf_d[j], new_batch=False)
                    DMA("sp", d_dft[s], dft[s][:, 1, :], sf_d[j], new_batch=False)

                load_fwd(0)
                for j in range(16):
                    s = j % 2
                    if j + 1 < 16:
                        load_fwd(j + 1)
                    psA = psring.next()
                    psB = psring.next()
                    for i in range(16):
                        MM(psA[:, 0:256], dft[s][:, 0, i * 128:(i + 1) * 128], Pm[:, i, :], start=(i == 0), stop=(i == 15))
                    for i in range(16):
                        MM(psB[:, 0:256], dft[s][:, 1, i * 128:(i + 1) * 128], Mm[:, i, :], start=(i == 0), stop=(i == 15))
                    CP("act", Kt[:, 2 * j, :], psA[:, 0:256])
                    CP("act", Kt[:, 2 * j + 1, :], psB[:, 0:256])
                    if j == 0:
                        psC = psring.next()
                        for i in range(16):
                            MM(psC[:, 0:256], dft[s][:, 1, i * 128:(i + 1) * 128], Pm[:, i, :], start=(i == 0), stop=(i == 15))
                        CP("act", Kt[0:1, 1, :], psC[0:1, 0:256])
                AR.reset(mark3)
                ztok = AR.alloc([16, 256], BF16)
                ctmp = Ring([AR.alloc([256], F32) for _ in range(4)])
                zT = u[0]
                gate = u[1 + o]
                for i in range(16):
                    ps = psring.next()
                    pbf = psbf[id(ps)]
                    for m in range(2):
                        src_v = zT[:, m, i * 128:(i + 1) * 128]
                        ov = pbf[:, m * 128:(m + 1) * 128]
                        P.pe(lambda e, ov=ov, src_v=src_v: e.transpose(ov.ap, src_v.ap, ident_bf[:].ap),
                             reads=[src_v, ident_bf[:]], writes=[ov])
                    CP("act", ztok[:, i, :], pbf[:, 0:256])
                load_fwd(0)
                for j in range(16):
                    s = j % 2
                    if j + 1 < 16:
                        load_fwd(j + 1)
                    psR = psring.next()
                    psI = psring.next()
                    for i in range(16):
                        MM(psR[:, 0:256], dft[s][:, 0, i * 128:(i + 1) * 128], ztok[:, i, :], start=(i == 0), stop=(i == 15))
                    for i in range(16):
                        MM(psI[:, 0:256], dft[s][:, 1, i * 128:(i + 1) * 128], ztok[:, i, :], start=(i == 0), stop=(i == 15))
                    kre = Kt[:, 2 * j, :]
                    kim = Kt[:, 2 * j + 1, :]
                    t1, t2, t3, t4 = (ctmp.next() for _ in range(4))
                    TT("dve", t1[:], psR[:, 0:256], kre, ALU.mult)
                    TT("dve", t2[:], psI[:, 0:256], kim, ALU.mult)
                    TT("dve", t3[:], psR[:, 0:256], kim, ALU.mult)
                    TT("dve", t4[:], psI[:, 0:256], kre, ALU.mult)
                    TT("dve", kre, t1[:], t2[:], ALU.subtract)
                    TT("dve", kim, t3[:], t4[:], ALU.add)
                    if j == 0:
                        CP("dve", Kt[0:1, 0, :], t1[0:1, :])
                        CP("dve", Kt[0:1, 1, :], t2[0:1, :])

                def load_inv(th, j):
                    s = j % 2
                    d_dft[s].new_batch()
                    DMA("sp", d_dft[s], dft[s][:, 0, 0:1024], ci_d[j * 128:(j + 1) * 128, th * 1024:(th + 1) * 1024], new_batch=False)
                    DMA("sp", d_dft[s], dft[s][:, 1, 0:1024], si_d[j * 128:(j + 1) * 128, th * 1024:(th + 1) * 1024], new_batch=False)

                for th in range(2):
                    accs = [[psring.next() for _ in range(2)] for _ in range(2)]
                    load_inv(th, 0)
                    for j in range(16):
                        s = j % 2
                        if j + 1 < 16:
                            load_inv(th, j + 1)
                        for m in range(2):
                            for tq in range(2):
                                MM(accs[m][tq][:], Kt[:, 2 * j, m * 128:(m + 1) * 128], dft[s][:, 0, tq * 512:(tq + 1) * 512],
                                   start=(j == 0), stop=False)
                                MM(accs[m][tq][:], Kt[:, 2 * j + 1, m * 128:(m + 1) * 128], dft[s][:, 1, tq * 512:(tq + 1) * 512],
                                   start=False, stop=(j == 15))
                    for m in range(2):
                        for tq in range(2):
                            tt = th * 2 + tq
                            ts = slice(tt * 512, (tt + 1) * 512)
                            sk = ccol("hy_skip", l * 8 + o * 4 + hh * 2 + m)
                            tv = accs[m][tq][:]
                            STT("dve", tv, zT[:, m, ts], sk, tv, ALU.mult, ALU.add)
                            dst = zT[:, m, ts] if o == 0 else ohy[:, hh * 2 + m, ts]
                            TT("dve", dst, tv, gate[:, m, ts], ALU.mult)
        AR.reset(mark0)
        sq = Ring([AR.alloc([512], BF16) for _ in range(3)])
        rs = Ring([AR.alloc([512], F32) for _ in range(2)])
        for tt in range(NT):
            ts = slice(tt * 512, (tt + 1) * 512)
            ps = psring.next()
            r = rs.next()
            rstd_from([ohy[:, m, ts] for m in range(4)], ps[:], r[:], sq, HYW, ccol("eps"))
            for m in range(4):
                STT("dve", ohy[:, m, ts], ohy[:, m, ts], ccol("hy_norm", l * 4 + m), r[:], ALU.mult, ALU.mult)
        hy_mark[0] = mark0

    def hyena_out(l):
        wout_apply(l, 512, 4, ohyb[0])

    def deltanet(l):
        AR.reset(hy_mark[0])
        hn = hnb[0] = AR.alloc([8, SEQ], BF16)
        rmsnorm("mix_norm", l * 8)
        wba = AR.alloc([8, 16], BF16)
        ba = AR.alloc([16, 16], F32)
        beta_t = AR.alloc([16, 8], F32)
        g_t = AR.alloc([16, 8], F32)
        gc_t = AR.alloc([16, 8], F32)
        ngc_t = AR.alloc([16, 8], F32)
        bw_t = AR.alloc([16, 8], F32)
        kds_t = AR.alloc([16, 8], F32)
        egl = AR.alloc([16, 16], F32)
        DMA("pool", d_wba, wba[:], Wd_["w_in"][LW[l], :, 2048:2064].rearrange("(k p) f -> p k f", p=128))
        ps = psring.next()
        for blk in range(16):
            for k in range(8):
                MM(ps[:, blk * 16:(blk + 1) * 16], hn[:, k, blk * 128:(blk + 1) * 128], wba[:, k, :], start=(k == 0), stop=(k == 7))
        bav = View(ba[:].ap.rearrange("p a b -> p (a b)"), ba[:].pages)
        CP("act", bav, ps[:, 0:256])
        dn_stop = cfg.get("dn_stop", 99)
        if dn_stop <= -3:
            return
        ACTF(beta_t[:], ba[:, :, 0:8], AF.Sigmoid)
        dtb = View(ccol("dtb", l * 8, 8).ap.unsqueeze(1).to_broadcast([128, 16, 8]), ccol("dtb", l * 8, 8).pages)
        nab = View(negA[:, l * 8:(l + 1) * 8].ap.unsqueeze(1).to_broadcast([128, 16, 8]), negA[:].pages)
        TT("dve", g_t[:], ba[:, :, 8:16], dtb, ALU.add)
        ACTF(g_t[:], g_t[:], AF.Exp)
        ACTF(g_t[:], g_t[:], AF.Ln, bias=ccol("one"))
        TT("dve", g_t[:], g_t[:], nab, ALU.mult)
        if dn_stop <= -2:
            return
        ps = psring.next()
        ps2 = psring.next()
        ps3 = psring.next()
        v3 = lambda v, b: View(v.ap.rearrange("p (a b) -> p a b", b=b), v.pages)
        MM(ps[:, 0:64], mat("cum_f"), g_t[:, :, 0:4])
        MM(ps[:, 64:128], mat("cum_b"), g_t[:, :, 4:8])
        MM(ps2[:, 0:128], mat("inda"), g_t[:])
        MM(ps2[:, 128:256], mat("indb"), g_t[:])
        MM(ps3[:, 0:128], mat("blk"), g_t[:])
        flat = lambda t: View(t[:].ap.rearrange("p a b -> p (a b)"), t[:].pages)
        if dn_stop <= -1:
            return
        nel = cfg.get("dn_nel", 99)
        elops = [
            lambda: CP("act", gc_t[:, :, 0:4], v3(ps[:, 0:64], 4)),
            lambda: CP("act", gc_t[:, :, 4:8], v3(ps[:, 64:128], 4)),
            lambda: TS("dve", flat(ngc_t), flat(gc_t), -1.0, None, ALU.mult),
            lambda: ACTF(egl[:, :, 0:8], v3(ps2[:, 0:128], 8), AF.Exp),
            lambda: ACTF(egl[:, :, 8:16], v3(ps2[:, 128:256], 8), AF.Exp),
            lambda: TT("dve", flat(kds_t), ps3[:, 0:128], flat(gc_t), ALU.subtract),
            lambda: ACTF(flat(kds_t), flat(kds_t), AF.Exp),
            lambda: ACTF(flat(bw_t), flat(gc_t), AF.Exp),
            lambda: TT("dve", flat(bw_t), flat(bw_t), flat(beta_t), ALU.mult),
        ]
        for f_ in elops[:nel]:
            f_()
        markh = AR.mark()
        dn_stop = cfg.get("dn_stop", 99)
        if dn_stop <= 0:
            return
        for h in range(cfg.get("dn_heads", 4)):
            AR.reset(markh)
            qkvz = [AR.alloc([SEQ], BF16) for _ in range(4)]
            qT, kT, vT, zg = qkvz
            ktok = AR.alloc([16, 128], BF16)
            vtok = AR.alloc([16, 128], BF16)
            oT = AR.alloc([SEQ], F32)
            S = [AR.alloc([128], F32) for _ in range(2)]
            Sb = [AR.alloc([128], BF16) for _ in range(2)]
            markp = AR.mark()
            wslots = [AR.alloc([8, 128], BF16) for _ in range(2)]
            stage = AR.alloc([SEQ + 4], F32)
            acc = AR.alloc([SEQ], F32)
            sq = Ring([AR.alloc([512], BF16) for _ in range(2)])
            rs = Ring([AR.alloc([512], F32) for _ in range(2)])
            MEMSET("pool", stage[:, 0:2], 0.0)
            MEMSET("pool", stage[:, SEQ + 2:SEQ + 4], 0.0)
            for s in range(4):
                def consume(m, tt, ps, s=s):
                    CP("act", stage[:, 2 + tt * 512:2 + (tt + 1) * 512], ps[:])
                    if tt != NT - 1:
                        return
                    if s == 3:
                        ACTF(zg[:], stage[:, 2:SEQ + 2], AF.Silu)
                        return
                    ch = s * 4 + h
                    w = [ccol("dn_conv", l * 60 + ch * 5 + k) for k in range(5)]
                    TS("dve", acc[:], stage[:, 0:SEQ], w[0], None, ALU.mult)
                    for k in range(1, 5):
                        STT("dve", acc[:], stage[:, k:SEQ + k], w[k], acc[:], ALU.mult, ALU.add)
                    if s == 2:
                        ACTF(vT[:], acc[:], AF.Silu)
                        return
                    ACTF(acc[:], acc[:], AF.Silu)
                    dst = qkvz[s]
                    for t2 in range(NT):
                        ts = slice(t2 * 512, (t2 + 1) * 512)
                        p2 = psring.next()
                        r = rs.next()
                        rstd_from([acc[:, ts]], p2[:], r[:], sq, 1.0, ccol("eps_l2"))
                        if s == 0:
                            STT("dve", dst[:, ts], acc[:, ts], 128.0 ** -0.5, r[:], ALU.mult, ALU.mult)
                        else:
                            TT("dve", dst[:, ts], acc[:, ts], r[:], ALU.mult)
                project(l, s * 512 + h * 128, 128, wslots, consume)
            for blk in range(16):
                bs = slice(blk * 128, (blk + 1) * 128)
                ps = psring.next()
                pbf = psbf[id(ps)]
                for m, srcT in enumerate((kT, vT)):
                    src_v = srcT[:, bs]
                    ov = pbf[:, m * 128:(m + 1) * 128]
                    P.pe(lambda e, ov=ov, src_v=src_v: e.transpose(ov.ap, src_v.ap, ident_bf[:].ap),
                         reads=[src_v, ident_bf[:]], writes=[ov])
                CP("act", ktok[:, blk, :], pbf[:, 0:128])
                CP("act", vtok[:, blk, :], pbf[:, 128:256])
            if dn_stop <= 1:
                continue
            AR.reset(markp)
            NF = 5
            NB = 15
            sets = [[{"f": [AR.alloc([128], F32) for _ in range(NF)], "b": [AR.alloc([128], BF16, align=256) for _ in range(NB)]}
                     for _ in range(2)] for _ in range(2)]
            MEMSET("pool", oT[:], 0.0)
            for d in range(2):
                MEMSET("pool", S[d][:], 0.0)
                MEMSET("pool", Sb[d][:], 0.0)
            ident = mat("ident")
            def unit(step, d):
                blk = step if d == 0 else 15 - step
                bs = slice(blk * 128, (blk + 1) * 128)
                r_ = d * 4 + h
                T = sets[d][step % 2]
                dg, Ds, DTm, Er, Pm_ = T["f"]
                Pb, kbg, vb, kdec, wT, attnT, qd, vn, N_, M_, Na, Ma, Nb_, Mb_, Pw = T["b"]
                gcc = gc_t[:, blk, r_:r_ + 1]
                ngc = ngc_t[:, blk, r_:r_ + 1]
                bcol = beta_t[:, blk, r_:r_ + 1]
                TS("dve", dg[:], ident, gcc, None, ALU.mult)
                yield
                pR = psring.next()
                MM(pR[:, 0:128], mat("ones"), dg[:])
                yield
                STT("dve", Ds[:], pR[:, 0:128], -1.0, mat("neg_ls" if d == 0 else "neg_us"), ALU.mult, ALU.add)
                yield
                ACTF(Ds[:], Ds[:], AF.Exp, bias=gcc)
                yield
                TT("dve", DTm[:], pR[:, 0:128], mat("neg_ui" if d == 0 else "neg_li"), ALU.add)
                yield
                ACTF(DTm[:], DTm[:], AF.Exp, bias=ngc)
                yield
                ACTF(Er[:], pR[:, 0:128], AF.Exp)
                yield
                pK = psring.next()
                MM(pK[:, 0:128], kT[:, bs], kT[:, bs])
                yield
                MM(pK[:, 128:256], kT[:, bs], qT[:, bs])
                yield
                STT("dve", N_[:], pK[:, 0:128], bcol, Ds[:], ALU.mult, ALU.mult)
                yield
                TT("dve", attnT[:], pK[:, 128:256], DTm[:], ALU.mult)
                yield
                TT("dve", qd[:], qT[:, bs], Er[:], ALU.mult)
                yield
                pre_stop = cfg.get("pre_stop", 99)
                if pre_stop <= 2:
                    return
                pT = psring.next()
                pTb = psbf[id(pT)]
                P.pe(lambda e, o_=pTb[:, 0:128], i_=N_[:]: e.transpose(o_.ap, i_.ap, ident_bf[:].ap),
                     reads=[N_[:], ident_bf[:]], writes=[pTb[:, 0:128]])
                CP("act", M_[:], pTb[:, 0:128])
                yield
                STT("dve", Pm_[:], pTb[:, 0:128], -1.0, ident, ALU.mult, ALU.add)
                yield
                STT("dve", Pw[:], pTb[:, 0:128], -1.0, ident, ALU.mult, ALU.add)
                yield
                if pre_stop <= 3:
                    return
                Nc, Mc = N_, M_
                nxt = [(Na, Ma), (Nb_, Mb_)]
                for lvl in range(5):
                    N2, M2 = nxt[lvl % 2]
                    pq = psring.next()
                    MM(pq[:, 0:128], Mc[:], Nc[:])
                    if lvl < 4:
                        MM(pq[:, 128:256], Nc[:], Mc[:])
                    CP("act", N2[:], pq[:, 0:128])
                    yield
                    if lvl < 4:
                        CP("act", M2[:], pq[:, 128:256])
                        yield
                    pp = psring.next()
                    MM(pp[:, 0:128], N2[:], Pw[:])
                    yield
                    if lvl < 4:
                        TT("dve", Pm_[:], Pm_[:], pp[:, 0:128], ALU.add)
                        yield
                        CP("act", Pw[:], Pm_[:])
                        yield
                    else:
                        TT("dve", Pb[:], Pm_[:], pp[:, 0:128], ALU.add)
                        yield
                    Nc, Mc = N2, M2
                if pre_stop <= 4:
                    return
                TS("dve", kbg[:], ktok[:, blk, :], bw_t[:, blk, r_:r_ + 1], None, ALU.mult)
                yield
                TS("dve", vb[:], vtok[:, blk, :], bcol, None, ALU.mult)
                yield
                TS("dve", kdec[:], ktok[:, blk, :], kds_t[:, blk, r_:r_ + 1], None, ALU.mult)
                yield
                if pre_stop <= 5:
                    return
                pu = psring.next()
                MM(pu[:, 0:128], Pb[:], vb[:])
                yield
                MM(pu[:, 128:256], kbg[:], Pb[:])
                yield
                u_ = dg
                CP("act", u_[:], pu[:, 0:128])
                yield
                CP("act", wT[:], pu[:, 128:256])
                yield
                if dn_stop <= 2:
                    return
                for cc in ((0, 1) if d == 0 else (1, 0)):
                    r0 = cc * 64
                    rr = slice(r0, r0 + 64)
                    pw = psring.next()
                    MM(pw[rr, 0:128], wT[:, rr], Sb[d][:])
                    yield
                    TT("dve", vn[rr, :], u_[rr, :], pw[rr, 0:128], ALU.subtract)
                    yield
                    po = psring.next()
                    MM(po[:, 0:64], Sb[d][:], qd[:, rr], start=True, stop=False)
                    yield
                    MM(po[:, 0:64], vn[rr, :], attnT[rr, rr], start=False, stop=True)
                    yield
                    ov = oT[:, blk * 128 + r0:blk * 128 + r0 + 64]
                    TT("dve", ov, ov, po[:, 0:64], ALU.add)
                    yield
                    pd = psring.next()
                    MM(pd[:, 0:128], kdec[rr, :], vn[rr, :])
                    yield
                    STT("dve", S[d][:], S[d][:], egl[:, blk, cc * 8 + r_:cc * 8 + r_ + 1], pd[:, 0:128], ALU.mult, ALU.add)
                    yield
                    CP("act", Sb[d][:], S[d][:])
                    yield

                yield

            for step in range(cfg.get("dn_steps", 16)):
                gens = [unit(step, 0), unit(step, 1)]
                while gens:
                    for g_ in list(gens):
                        try:
                            next(g_)
                        except StopIteration:
                            gens.remove(g_)
            AR.reset(markp)
            sq = Ring([AR.alloc([512], BF16) for _ in range(2)])
            rs = Ring([AR.alloc([512], F32) for _ in range(2)])
            odn = AR.alloc([1, SEQ], BF16)
            for tt in range(NT):
                ts = slice(tt * 512, (tt + 1) * 512)
                p2 = psring.next()
                r = rs.next()
                rstd_from([oT[:, ts]], p2[:], r[:], sq, 128.0, ccol("eps"))
                STT("dve", oT[:, ts], oT[:, ts], ccol("dn_norm", l), r[:], ALU.mult, ALU.mult)
                TT("dve", odn[:, 0, ts], oT[:, ts], zg[:, ts], ALU.mult)
            wout_apply(l, h * 128, 1, odn)

    for l in layers:
        if "ffn1" in stages:
            ffn(l, 0)
        if "hy" in stages:
            hyena(l)
        else:
            AR.reset()
            hy_mark[0] = 0
        if "dn" in stages:
            deltanet(l)
        if "hy" in stages:
            hyena_out(l)
        if "ffn2" in stages:
            ffn(l, 1)
    if cfg.get("final_norm", True):
        AR.reset()
        rmsnorm("final_norm", 0, final=True)
    else:
        AR.reset()
        d_out.new_batch()
        for k in range(8):
            v = xT[:, k, :]
            P.dma("sp", d_out, lambda e, v=v, k=k: e.dma_start(out=outT_d[k * 128:(k + 1) * 128, :], in_=v.ap),
                  reads=[v], writes=[DramRes()], new_batch=False)
    for ds_ in [d_out] + d_outs:
        if ds_.batch is not None:
            P.final_batches.append(ds_.batch)
    stats = P.emit()
    P.es.close()
    return nc, stats


_CACHE = {}


def make_in_maps(inputs, cfg=None):
    stages = (cfg or {}).get("stages", ("ffn1", "dn", "hy", "ffn2"))
    st = static_consts()
    shared = {"cst": build_cst(inputs)}
    shared.update(st)
    for nm in ("ffn1_w_gate", "ffn1_w_up", "ffn1_w_down", "ffn2_w_gate", "ffn2_w_up", "ffn2_w_down",
               "w_in", "w_out", "hy_f_wout"):
        if nm.startswith("ffn") and nm[:4] not in stages:
            continue
        a_ = np.asarray(inputs[nm], dtype=np.float32)
        if cfg and "layers" in cfg:
            a_ = a_[list(cfg["layers"])]
        shared[nm] = np.ascontiguousarray(a_)
    maps = []
    xT_list = (cfg or {}).get("_xT")
    x = None if xT_list is not None else np.asarray(inputs["x"], dtype=np.float32)
    for b in range(8):
        m = dict(shared)
        m["xT"] = xT_list[b] if xT_list is not None else np.ascontiguousarray(x[b].T)
        maps.append(m)
    return maps


LAUNCH_GROUPS = ((0, 1, 2, 3),)


def _get_prog(cfg):
    key = repr(sorted((k, v) for k, v in cfg.items() if k != "_xT"))
    if key not in _CACHE:
        _CACHE[key] = build_program({k: v for k, v in cfg.items() if k != "_xT"})
    return _CACHE[key]


def kernel(**inputs):
    cfg = inputs.pop("_cfg", None)
    if cfg is not None:
        nc, stats = _get_prog(cfg)
        maps = make_in_maps(inputs, cfg)
        res = run_bass_kernel_spmd(nc, maps, core_ids=list(range(8)))
        out = np.stack([np.ascontiguousarray(r["outT"].T) for r in res.results], axis=0)
        return out.astype(np.float32)
    xT_list = None
    for gi, grp in enumerate(LAUNCH_GROUPS):
        last = gi == len(LAUNCH_GROUPS) - 1
        if len(LAUNCH_GROUPS) == 1:
            cfg = {}
        else:
            cfg = {"layers": tuple(grp), "final_norm": last}
        nc, stats = _get_prog(cfg)
        c2 = dict(cfg)
        if xT_list is not None:
            c2["_xT"] = xT_list
        maps = make_in_maps(inputs, c2)
        res = run_bass_kernel_spmd(nc, maps, core_ids=list(range(8)))
        xT_list = [np.ascontiguousarray(r["outT"]) for r in res.results]
    out = np.stack([np.ascontiguousarray(a.T) for a in xT_list], axis=0)
    return out.astype(np.float32)
```

```python
import math
import numpy as np
import ml_dtypes
from contextlib import ExitStack
import concourse.bass as bass
import concourse.mybir as mybir
from concourse.bass_utils import run_bass_kernel_spmd

F32 = mybir.dt.float32
BF16 = mybir.dt.bfloat16
ALU = mybir.AluOpType
AF = mybir.ActivationFunctionType

D_MODEL = 1024
SEQ = 2048
DEPTH = 4
D_FF = 2816
D_IN = 3600
HYW = 512
NT = 4
KC = 8
RMS_EPS = 1e-6

ENGS = ("pe", "act", "dve", "pool", "sp")
ENGATTR = {"pe": "tensor", "act": "scalar", "dve": "vector", "pool": "gpsimd", "sp": "sync"}
I32 = mybir.dt.int32
DTSIZE = {F32: 4, BF16: 2, I32: 4}


class Res:
    __slots__ = ("last_w", "readers", "psum")

    def __init__(self, psum=False):
        self.last_w = None
        self.readers = []
        self.psum = psum


class Batch:
    __slots__ = ("sem", "final")

    def __init__(self, sem):
        self.sem = sem
        self.final = 0


class DSem:
    def __init__(self, prog, name):
        self.prog = prog
        self.name = name
        self.gen = 0
        self.sem = prog.sem(name)
        self.count = 0
        self.batch = None
        self.prev = None

    def new_batch(self):
        self.prev = self.batch
        if self.count > 3000:
            self.gen += 1
            self.sem = self.prog.sem("%s_g%d" % (self.name, self.gen))
            self.count = 0
        self.batch = Batch(self.sem)
        self.batch.final = self.count
        return self.batch


class Instr:
    __slots__ = ("eng", "fn", "deps", "idx", "marked", "batch")

    def __init__(self, eng, fn):
        self.eng = eng
        self.fn = fn
        self.deps = []
        self.marked = False
        self.batch = None


class Mem:
    def __init__(self, prog, name, nbytes, page, space="sb"):
        self.name = name
        self.nbytes = nbytes
        self.page = page
        self.pages = [Res(psum=(space != "sb")) for _ in range((nbytes + page - 1) // page)]
        if space == "sb":
            self.t = prog.es.enter_context(prog.nc.sbuf_tensor(name, [128, nbytes // 4], F32))
        else:
            self.t = prog.es.enter_context(prog.nc.psum_tensor(name, [128, nbytes // 4], F32))
        self.bump = 0

    def reset(self, to=0):
        self.bump = to

    def mark(self):
        return self.bump

    def alloc(self, shape, dt, align=None):
        n = int(np.prod(shape)) * DTSIZE[dt]
        al = align or min(self.page, 512)
        base = (self.bump + al - 1) // al * al
        assert base + n <= self.nbytes, (self.name, base, n, self.nbytes)
        self.bump = base + n
        return Tile(self, base, shape, dt)


class View:
    __slots__ = ("ap", "pages")

    def __init__(self, ap, pages):
        self.ap = ap
        self.pages = pages


class Tile:
    def __init__(self, mem, base, shape, dt):
        self.mem = mem
        self.base = base
        self.shape = tuple(shape)
        self.dt = dt
        esz = DTSIZE[dt]
        n = int(np.prod(shape))
        assert base % 4 == 0 and (n * esz) % 4 == 0
        ap = mem.t[:, base // 4:(base + n * esz) // 4]
        if dt != F32:
            ap = ap.bitcast(dt)
        if len(shape) == 2:
            ap = ap.rearrange("p (a b) -> p a b", b=shape[1])
        elif len(shape) == 3:
            ap = ap.rearrange("p (a b c) -> p a b c", b=shape[1], c=shape[2])
        self.ap = ap
        self.strides = [int(np.prod(shape[i + 1:])) for i in range(len(shape))]

    def __getitem__(self, idx):
        if not isinstance(idx, tuple):
            idx = (idx,)
        ap = self.ap[idx]
        lo = 0
        hi = 0
        fidx = idx[1:]
        for d in range(len(self.shape)):
            if d < len(fidx):
                ix = fidx[d]
                if isinstance(ix, slice):
                    a = 0 if ix.start is None else ix.start
                    b = self.shape[d] if ix.stop is None else ix.stop
                else:
                    a, b = ix, ix + 1
            else:
                a, b = 0, self.shape[d]
            lo += a * self.strides[d]
            hi += (b - 1) * self.strides[d]
        esz = DTSIZE[self.dt]
        b0 = self.base + lo * esz
        b1 = self.base + (hi + 1) * esz
        pg = self.mem.page
        pages = self.mem.pages[b0 // pg:(b1 + pg - 1) // pg]
        return View(ap, pages)

    def all(self):
        return self[:]


class Ring:
    def __init__(self, items):
        self.items = items
        self.i = 0

    def next(self):
        it = self.items[self.i % len(self.items)]
        self.i += 1
        return it


class Prog:
    def __init__(self, nc, same_eng_sync=True):
        self.nc = nc
        self.es = ExitStack()
        self.streams = {e: [] for e in ENGS}
        self.same_eng_sync = same_eng_sync
        self.final_batches = []

    def sem(self, name):
        return self.es.enter_context(self.nc.semaphore(name))

    def op(self, eng, fn, reads=(), writes=(), batch=None, dsem=None):
        ins = Instr(eng, fn)
        st = self.streams[eng]
        ins.idx = len(st)
        st.append(ins)
        deps = ins.deps
        rp = []
        for r in reads:
            rp.extend(r.pages)
        wp = []
        for w in writes:
            wp.extend(w.pages)
        for r in rp:
            if r.last_w is not None:
                deps.append(r.last_w)
            if r.psum:
                for t_ in r.readers:
                    if t_[0] == "eng" and t_[1] != eng:
                        deps.append(t_)
        for w in wp:
            if w.last_w is not None:
                deps.append(w.last_w)
            deps.extend(w.readers)
        if batch is not None:
            ins.deps = deps = [d for d in deps if not (d[0] == "dma" and d[1] is batch)]
            ins.batch = batch
            dsem.count += 16
            batch.final = dsem.count
            tok = ("dma", batch)
        else:
            tok = ("eng", eng, ins.idx)
        for r in rp:
            r.readers.append(tok)
        for w in wp:
            w.last_w = tok
            w.readers = []
        return ins

    def pe(self, fn, reads=(), writes=()):
        return self.op("pe", fn, reads, writes)

    def act(self, fn, reads=(), writes=()):
        return self.op("act", fn, reads, writes)

    def dve(self, fn, reads=(), writes=()):
        return self.op("dve", fn, reads, writes)

    def pool(self, fn, reads=(), writes=()):
        return self.op("pool", fn, reads, writes)

    def dma(self, eng, dsem, fn, reads=(), writes=(), new_batch=True):
        if new_batch or dsem.batch is None:
            dsem.new_batch()
        ins = self.op(eng, fn, reads, writes, batch=dsem.batch, dsem=dsem)
        pv = getattr(dsem, "prev", None)
        if pv is not None:
            ins.deps.append(("dma", pv))
        return ins

    def emit(self):
        nc = self.nc
        sync_same = self.same_eng_sync
        for e in ENGS:
            for ins in self.streams[e]:
                for d in ins.deps:
                    if d[0] == "eng":
                        if d[1] == ins.eng and (ins.eng == "pe" or (not sync_same and ins.eng != "pool")):
                            continue
                        self.streams[d[1]][d[2]].marked = True
        cum = {}
        for e in ENGS:
            c = 0
            arr = []
            for ins in self.streams[e]:
                if ins.marked and ins.batch is None:
                    c += 1
                arr.append(c)
            cum[e] = arr
        SEG = 4000
        esem = {e: [self.sem("s_%s%d" % (e, k)) for k in range(max(1, (cum[e][-1] + SEG - 1) // SEG if cum[e] else 1))]
                for e in ENGS}
        stats = {"waits": 0, "incs": 0, "n": {e: len(self.streams[e]) for e in ENGS}}
        final_batches = self.final_batches
        with nc.Block() as block:
            for e in ENGS:
                stream = self.streams[e]

                def body(engine, e=e, stream=stream):
                    waited = {}
                    for ins in stream:
                        need = {}
                        for d in ins.deps:
                            if d[0] == "eng":
                                if d[1] == e and (e == "pe" or (not sync_same and e != "pool")):
                                    continue
                                c_ = cum[d[1]][d[2]]
                                k_ = (c_ - 1) // SEG
                                key = (d[1], k_)
                                sem = esem[d[1]][k_]
                                val = (c_ - 1) % SEG + 1
                            else:
                                key = id(d[1].sem)
                                sem = d[1].sem
                                val = d[1].final
                            if waited.get(key, 0) >= val:
                                continue
                            if key not in need or need[key][1] < val:
                                need[key] = (sem, val)
                        for key, (sem, val) in need.items():
                            engine.wait_ge(sem, val)
                            waited[key] = val
                            stats["waits"] += 1
                        bi = ins.fn(engine)
                        if ins.batch is not None:
                            bi.then_inc(ins.batch.sem, 16)
                        elif ins.marked:
                            bi.then_inc(esem[e][(cum[e][ins.idx] - 1) // SEG], 1)
                            stats["incs"] += 1
                    if e == "sp":
                        for b in final_batches:
                            engine.wait_ge(b.sem, b.final)

                getattr(block, ENGATTR[e])(body)
        return stats


class DramRes:
    def __init__(self):
        self.pages = [Res()]


PI = math.pi
CLAY = {}
NCST = 0


def _mk_layout():
    global NCST
    off = 0
    for name, n in (("ffn1_norm", DEPTH * 8), ("mix_norm", DEPTH * 8), ("ffn2_norm", DEPTH * 8), ("final_norm", 8),
                    ("eps", 1), ("eps_l2", 1), ("one", 1), ("negpi", 1),
                    ("dn_conv", DEPTH * 60), ("dn_norm", DEPTH), ("hy_conv", DEPTH * 36), ("hy_bias", DEPTH * 12),
                    ("hy_skip", DEPTH * 8), ("hy_norm", DEPTH * 4), ("dtb", DEPTH * 8), ("alog", DEPTH * 8),
                    ("f_b1", DEPTH), ("f_b2", DEPTH), ("f_b3", DEPTH), ("f_freq", DEPTH), ("negt", 16),
                    ("f_w1", DEPTH * 64), ("f_w2", DEPTH * 64), ("f_w3", DEPTH * 64)):
        CLAY[name] = (off, n)
        off += n
    NCST = off


_mk_layout()
MATS = ("ident", "ones", "neg_ls", "neg_us", "neg_li", "neg_ui", "cum_f", "cum_b", "inda", "indb", "blk")
NEGV = -30000.0


def _chunked(v):
    return np.ascontiguousarray(np.asarray(v, np.float32).reshape(-1, 128).T)


def build_cst(inp):
    cst = np.zeros((128, NCST), np.float32)

    def put(name, arr, rows=128):
        o, n = CLAY[name]
        arr = np.asarray(arr, np.float32)
        assert arr.shape == (rows, n), (name, arr.shape, n)
        cst[:rows, o:o + n] = arr

    for nm in ("ffn1_norm", "mix_norm", "ffn2_norm"):
        put(nm, np.concatenate([_chunked(inp[nm][l]) for l in range(DEPTH)], axis=1))
    put("final_norm", _chunked(inp["final_norm"]))
    cst[:, CLAY["eps"][0]] = RMS_EPS
    cst[:, CLAY["eps_l2"][0]] = 1e-6
    cst[:, CLAY["one"][0]] = 1.0
    cst[:, CLAY["negpi"][0]] = -math.pi
    a = np.asarray(inp["dn_conv"], np.float32).reshape(DEPTH, 5, 12, 128).transpose(3, 0, 2, 1).reshape(128, -1)
    put("dn_conv", a)
    put("dn_norm", np.asarray(inp["dn_norm"], np.float32).T)
    a = np.asarray(inp["hy_conv"], np.float32).reshape(DEPTH, 3, 12, 128).transpose(3, 0, 2, 1).reshape(128, -1)
    put("hy_conv", a)
    a = np.asarray(inp["hy_conv_bias"], np.float32).reshape(DEPTH, 12, 128).transpose(2, 0, 1).reshape(128, -1)
    put("hy_bias", a)
    a = np.asarray(inp["hy_skip"], np.float32).reshape(DEPTH, 2, 4, 128).transpose(3, 0, 1, 2).reshape(128, -1)
    put("hy_skip", a)
    a = np.asarray(inp["hy_norm"], np.float32).reshape(DEPTH, 4, 128).transpose(2, 0, 1).reshape(128, -1)
    put("hy_norm", a)
    put("dtb", np.broadcast_to(np.asarray(inp["dn_dt_bias"], np.float32).reshape(1, -1), (128, DEPTH * 8)))
    put("alog", np.broadcast_to(np.asarray(inp["dn_a_log"], np.float32).reshape(1, -1), (128, DEPTH * 8)))
    for nm, key in (("f_b1", "hy_f_b1"), ("f_b2", "hy_f_b2"), ("f_b3", "hy_f_b3"), ("f_freq", "hy_f_freq")):
        put(nm, np.asarray(inp[key], np.float32).T, rows=64)
    t = np.linspace(0.0, 1.0, SEQ, dtype=np.float32)
    put("negt", -t.reshape(16, 128).T)
    put("f_w1", np.asarray(inp["hy_f_w1"], np.float32).transpose(1, 0, 2).reshape(33, -1), rows=33)
    put("f_w2", np.asarray(inp["hy_f_w2"], np.float32).transpose(1, 0, 2).reshape(64, -1), rows=64)
    put("f_w3", np.asarray(inp["hy_f_w3"], np.float32).transpose(1, 0, 2).reshape(64, -1), rows=64)
    return cst


_STATIC = {}


def static_consts():
    if _STATIC:
        return _STATIC
    i = np.arange(128)
    same = (i[:, None] // 64) == (i[None, :] // 64)
    row, col = i[:, None], i[None, :]
    m = {}
    m["ident"] = np.eye(128)
    m["ones"] = np.ones((128, 128))
    m["neg_ls"] = np.where(same & (row > col), 0.0, NEGV)
    m["neg_us"] = np.where(same & (row < col), 0.0, NEGV)
    m["neg_li"] = np.where(same & (row >= col), 0.0, NEGV)
    m["neg_ui"] = np.where(same & (row <= col), 0.0, NEGV)
    m["cum_f"] = (same & (row <= col)).astype(np.float64)
    m["cum_b"] = (same & (row >= col)).astype(np.float64)
    m["inda"] = np.broadcast_to((i < 64)[:, None], (128, 128)).astype(np.float64)
    m["indb"] = np.broadcast_to((i >= 64)[:, None], (128, 128)).astype(np.float64)
    m["blk"] = same.astype(np.float64)
    _STATIC["mats"] = np.ascontiguousarray(np.concatenate([m[k] for k in MATS], axis=1), dtype=np.float32)
    l = SEQ
    t = np.linspace(0.0, 1.0, l, dtype=np.float32)[:, None]
    bands = 16
    ang = ((np.float32(2.0 * math.pi / l)) * np.arange(l, dtype=np.float32)[:, None]
           * np.linspace(1e-4, bands - 1, bands, dtype=np.float32)[None, :]).astype(np.float32)
    feats = np.concatenate([t, np.cos(ang), -np.sin(ang)], axis=-1).astype(np.float32)
    _STATIC["featsT"] = np.ascontiguousarray(feats.T)
    max_decay = math.log(1e-2) / 0.3
    min_decay = math.log(1e-2) / 1.5
    deltas = np.abs(np.linspace(min_decay, max_decay, HYW, dtype=np.float32))
    _STATIC["deltas"] = np.ascontiguousarray(np.broadcast_to(deltas[None, :], (128, HYW)), dtype=np.float32)
    n = np.arange(SEQ, dtype=np.float64)
    ang = 2.0 * np.pi * np.outer(n, n) / (2 * SEQ)
    cf = np.cos(ang)
    sf = -np.sin(ang)
    sf[:, 0] = (-1.0) ** n
    ci = (2.0 / (2 * SEQ)) * np.cos(ang)
    ci[0, :] = 1.0 / (2 * SEQ)
    si = -(2.0 / (2 * SEQ)) * np.sin(ang)
    si[0, :] = ((-1.0) ** n) / (2 * SEQ)
    bf = ml_dtypes.bfloat16

    def fwd_layout(a):
        return np.ascontiguousarray(a.reshape(16, 128, 16, 128).transpose(2, 1, 0, 3).reshape(16, 128, 2048)).astype(bf)

    _STATIC["dft_cf"] = fwd_layout(cf)
    _STATIC["dft_sf"] = fwd_layout(sf)
    _STATIC["dft_ci"] = np.ascontiguousarray(ci).astype(bf)
    _STATIC["dft_si"] = np.ascontiguousarray(si).astype(bf)
    return _STATIC


def build_program(cfg):
    n_layers = cfg.get("n_layers", DEPTH)
    layers = tuple(cfg.get("layers", tuple(range(n_layers))))
    LW = {l_: i_ for i_, l_ in enumerate(layers)}
    LD = len(layers) if "layers" in cfg else DEPTH
    if "layers" not in cfg:
        LW = {l_: l_ for l_ in range(DEPTH)}
    stages = cfg.get("stages", ("ffn1", "dn", "hy", "ffn2"))
    nc = bass.Bass("TRN2", target_bir_lowering=False)
    P = Prog(nc, same_eng_sync=cfg.get("same_eng_sync", True))

    def din(name, shape, dt=F32):
        return nc.dram_tensor(name, list(shape), dt, kind="ExternalInput").ap()

    xT_d = din("xT", [D_MODEL, SEQ])
    cst_d = din("cst", [128, NCST])
    mats_d = din("mats", [128, len(MATS) * 128])
    featsT_d = din("featsT", [33, SEQ])
    deltas_d = din("deltas", [128, HYW])
    cf_d = din("dft_cf", [16, 128, 2048], BF16)
    sf_d = din("dft_sf", [16, 128, 2048], BF16)
    ci_d = din("dft_ci", [2048, 2048], BF16)
    si_d = din("dft_si", [2048, 2048], BF16)
    Wd_ = {}
    for nm, shp in (("ffn1_w_gate", [LD, D_MODEL, D_FF]), ("ffn1_w_up", [LD, D_MODEL, D_FF]),
                    ("ffn1_w_down", [LD, D_FF, D_MODEL]), ("ffn2_w_gate", [LD, D_MODEL, D_FF]),
                    ("ffn2_w_up", [LD, D_MODEL, D_FF]), ("ffn2_w_down", [LD, D_FF, D_MODEL]),
                    ("w_in", [LD, D_MODEL, D_IN]), ("w_out", [LD, D_MODEL, D_MODEL]),
                    ("hy_f_wout", [LD, 64, 2048])):
        if nm.startswith("ffn") and nm[:4] not in stages:
            continue
        Wd_[nm] = din(nm, shp)
    outT_d = nc.dram_tensor("outT", [D_MODEL, SEQ], F32, kind="ExternalOutput").ap()

    CM = Mem(P, "cm", 15 * 1024, 256)
    AR = Mem(P, "arena", cfg.get("arena_kib", 128) * 1024, 512)
    XM = Mem(P, "xT_sb", 8 * SEQ * 4, 2048)
    xT = XM.alloc([8, SEQ], F32)
    hnb = [None]
    PS = [Mem(P, "ps%d" % i, 2048, 2048, space="ps") for i in range(8)]
    psb = [m.alloc([512], F32) for m in PS]
    psbf = {id(t): Tile(t.mem, 0, [256], BF16) for t in psb}
    psring = Ring(psb)

    cst = CM.alloc([NCST], F32)
    mats = CM.alloc([len(MATS), 128], F32)
    ones_bf = CM.alloc([128], BF16)
    ident_bf = CM.alloc([128], BF16)
    negA = CM.alloc([DEPTH * 8], F32)
    deltas = CM.alloc([HYW], F32)

    def mat(name):
        return mats[:, MATS.index(name), :]

    d_const = DSem(P, "d_const")
    d_x = DSem(P, "d_x")
    d_out = DSem(P, "d_out")
    d_outs = [DSem(P, "d_out%d" % i) for i in range(3)]

    def ccol(name, i=0, n=1, rows=None):
        o, _ = CLAY[name]
        if rows is None:
            return cst[:, o + i:o + i + n]
        return cst[0:rows, o + i:o + i + n]

    def MM(out, lhsT, rhs, start=True, stop=True):
        P.pe(lambda e: e.matmul(out.ap, lhsT=lhsT.ap, rhs=rhs.ap, start=start, stop=stop),
             reads=[lhsT, rhs], writes=[out])

    def ACTF(out, in_, func, bias=None, scale=1.0):
        rd = [in_]
        kw = {}
        if bias is not None:
            rd.append(bias)
            kw["bias"] = bias.ap
        if isinstance(scale, View):
            rd.append(scale)
            kw["scale"] = scale.ap
        else:
            kw["scale"] = scale
        P.act(lambda e: e.activation(out=out.ap, in_=in_.ap, func=func, **kw), reads=rd, writes=[out])

    def TT(eng, out, in0, in1, op):
        P.op(eng, lambda e: e.tensor_tensor(out=out.ap, in0=in0.ap, in1=in1.ap, op=op), reads=[in0, in1], writes=[out])

    def TS(eng, out, in0, s1, s2, op0, op1=None):
        rd = [in0]
        a1 = s1
        a2 = s2
        if isinstance(s1, View):
            rd.append(s1)
            a1 = s1.ap
        if isinstance(s2, View):
            rd.append(s2)
            a2 = s2.ap
        if op1 is None:
            P.op(eng, lambda e: e.tensor_scalar(out=out.ap, in0=in0.ap, scalar1=a1, scalar2=None, op0=op0),
                 reads=rd, writes=[out])
        else:
            P.op(eng, lambda e: e.tensor_scalar(out=out.ap, in0=in0.ap, scalar1=a1, scalar2=a2, op0=op0, op1=op1),
                 reads=rd, writes=[out])

    def STT(eng, out, in0, sc, in1, op0, op1):
        rd = [in0, in1]
        a = sc
        if isinstance(sc, View):
            rd.append(sc)
            a = sc.ap
        P.op(eng, lambda e: e.scalar_tensor_tensor(out=out.ap, in0=in0.ap, scalar=a, in1=in1.ap, op0=op0, op1=op1),
             reads=rd, writes=[out])

    def CP(eng, out, in_):
        if eng == "act":
            P.act(lambda e: e.copy(out=out.ap, in_=in_.ap), reads=[in_], writes=[out])
        else:
            P.op(eng, lambda e: e.tensor_copy(out=out.ap, in_=in_.ap), reads=[in_], writes=[out])

    def MEMSET(eng, out, val):
        P.op(eng, lambda e: e.memset(out.ap, val), writes=[out])

    def DMA(eng, dsem, out, src_ap, new_batch=True):
        P.dma(eng, dsem, lambda e: e.dma_start(out=out.ap, in_=src_ap), writes=[out], new_batch=new_batch)

    def bc(view, shape):
        return View(view.ap.to_broadcast(list(shape)), view.pages)

    DMA("sp", d_const, cst[:], cst_d)
    DMA("sp", d_const, mats[:], mats_d.rearrange("p (m c) -> p m c", c=128), new_batch=False)
    DMA("sp", d_const, deltas[:], deltas_d, new_batch=False)
    MEMSET("pool", ones_bf[:], 1.0)
    CP("dve", ident_bf[:], mat("ident"))
    ACTF(negA[:], ccol("alog", 0, DEPTH * 8), AF.Exp)
    TS("dve", negA[:], negA[:], -1.0, None, ALU.mult)
    d_x.new_batch()
    for k in range(8):
        DMA("sp", d_x, xT[:, k, :], xT_d[k * 128:(k + 1) * 128, :], new_batch=False)

    def rstd_from(chunks, ps, r, sqring, mean_div, eps_view):
        n = len(chunks)
        for k, cv in enumerate(chunks):
            s = sqring.next()
            sv = s[:, 0:cv.ap.shape[-1]]
            ACTF(sv, cv, AF.Square)
            MM(ps, ones_bf[:], sv, start=(k == 0), stop=(k == n - 1))
        ACTF(r, ps, AF.Sqrt, bias=eps_view, scale=1.0 / mean_div)
        P.dve(lambda e: e.reciprocal(out=r.ap, in_=r.ap), reads=[r], writes=[r])

    def rmsnorm(wname, wcol0, final=False):
        hn = hnb[0]
        m0_ = AR.mark()
        sq = Ring([AR.alloc([512], BF16) for _ in range(3)])
        rs = Ring([AR.alloc([512], F32) for _ in range(2)])
        ost = Ring([AR.alloc([512], F32) for _ in range(3)]) if final else None
        oi = 0
        for tt in range(NT):
            ts = slice(tt * 512, (tt + 1) * 512)
            ps = psring.next()
            r = rs.next()
            rstd_from([xT[:, k, ts] for k in range(8)], ps[:], r[:], sq, D_MODEL, ccol("eps"))
            for k in range(8):
                xv = xT[:, k, ts]
                wv = ccol(wname, wcol0 + k)
                if not final:
                    STT("dve", hn[:, k, ts], xv, wv, r[:], ALU.mult, ALU.mult)
                else:
                    o = ost.next()
                    STT("dve", o[:], xv, wv, r[:], ALU.mult, ALU.mult)
                    P.dma("sp", d_outs[oi % 3], lambda e, o=o, k=k, ts=ts: e.dma_start(
                        out=outT_d[k * 128:(k + 1) * 128, ts], in_=o[:].ap),
                        reads=[o[:]], writes=[DramRes()], new_batch=True)
                    oi += 1
        AR.reset(m0_)

    d_wgu = [DSem(P, "d_wgu%d" % i) for i in range(2)]
    d_wd = [DSem(P, "d_wd%d" % i) for i in range(2)]
    FGROUPS = [(0, 4), (4, 4), (8, 4), (12, 4), (16, 3), (19, 3)]

    def ffn(l, which):
        pre = "ffn1" if which == 0 else "ffn2"
        Wg = Wd_[pre + "_w_gate"]
        Wu = Wd_[pre + "_w_up"]
        Wdn = Wd_[pre + "_w_down"]
        AR.reset()
        hn = hnb[0] = AR.alloc([8, SEQ], BF16)
        rmsnorm(pre + "_norm", l * 8)
        wg = [AR.alloc([8, 512], BF16) for _ in range(2)]
        wu = [AR.alloc([8, 512], BF16) for _ in range(2)]
        wd = [AR.alloc([4, 1024], BF16) for _ in range(2)]
        actb = [AR.alloc([4, SEQ], BF16) for _ in range(2)]
        stmp = Ring([AR.alloc([512], F32) for _ in range(3)])

        def load(gi):
            f0, nf = FGROUPS[gi]
            s = gi % 2
            d_wgu[s].new_batch()
            for (W, t) in ((Wg, wg[s]), (Wu, wu[s])):
                src = W[LW[l], :, f0 * 128:(f0 + nf) * 128].rearrange("(k p) f -> p k f", p=128)
                DMA("pool", d_wgu[s], t[:, :, 0:nf * 128], src, new_batch=False)
            src = Wdn[LW[l], f0 * 128:(f0 + nf) * 128, :].rearrange("(j p) d -> p j d", p=128)
            DMA("pool", d_wd[s], wd[s][:, 0:nf, :], src)

        load(0)
        for gi, (f0, nf) in enumerate(FGROUPS):
            s = gi % 2
            if gi + 1 < len(FGROUPS):
                load(gi + 1)
            ab = actb[s]
            for fj in range(nf):
                fs = slice(fj * 128, (fj + 1) * 128)
                for tt in range(NT):
                    ts = slice(tt * 512, (tt + 1) * 512)
                    gps = psring.next()
                    ups = psring.next()
                    for (pst, wt) in ((gps, wg[s]), (ups, wu[s])):
                        for k in range(8):
                            MM(pst[:], wt[:, k, fs], hn[:, k, ts], start=(k == 0), stop=(k == 7))
                    st = stmp.next()
                    ACTF(st[:], gps[:], AF.Silu)
                    TT("dve", ab[:, fj, ts], ups[:], st[:], ALU.mult)
            for dk in range(8):
                ds_ = slice(dk * 128, (dk + 1) * 128)
                for tt in range(NT):
                    ts = slice(tt * 512, (tt + 1) * 512)
                    ops = psring.next()
                    for fj in range(nf):
                        MM(ops[:], wd[s][:, fj, ds_], ab[:, fj, ts], start=(fj == 0), stop=(fj == nf - 1))
                    xv = xT[:, dk, ts]
                    STT("dve", xv, ops[:], 0.5, xv, ALU.mult, ALU.add)

    d_win = [DSem(P, "d_win%d" % i) for i in range(2)]
    d_wo = DSem(P, "d_wo")
    d_dft = [DSem(P, "d_dft%d" % i) for i in range(2)]
    d_misc = DSem(P, "d_misc")
    d_wba = DSem(P, "d_wba")
    win_i = [0]
    ohyb = [None]
    hy_mark = [0]

    def wout_apply(l, row0, nchunks, src_tile):
        wo = AR.alloc([nchunks, 1024], BF16)
        src = Wd_["w_out"][LW[l], row0:row0 + 128 * nchunks, :].rearrange("(m p) d -> p m d", p=128)
        DMA("pool", d_wo, wo[:], src)
        for dk in range(8):
            for tt in range(NT):
                ts = slice(tt * 512, (tt + 1) * 512)
                ps = psring.next()
                for m in range(nchunks):
                    MM(ps[:], wo[:, m, dk * 128:(dk + 1) * 128], src_tile[:, m, ts], start=(m == 0), stop=(m == nchunks - 1))
                xv = xT[:, dk, ts]
                TT("dve", xv, ps[:], xv, ALU.add)

    def project(l, col0, ncols, wslots, consume):
        hn = hnb[0]
        s = win_i[0] % 2
        win_i[0] += 1
        wt = wslots[s]
        src = Wd_["w_in"][LW[l], :, col0:col0 + ncols].rearrange("(k p) f -> p k f", p=128)
        DMA("pool", d_win[s], wt[:, :, 0:ncols], src)
        for m in range(ncols // 128):
            for tt in range(NT):
                ts = slice(tt * 512, (tt + 1) * 512)
                ps = psring.next()
                for k in range(8):
                    MM(ps[:], wt[:, k, m * 128:(m + 1) * 128], hn[:, k, ts], start=(k == 0), stop=(k == 7))
                consume(m, tt, ps)

    def hyena(l):
        AR.reset()
        h3T = AR.alloc([SEQ], F32)
        ohy = ohyb[0] = AR.alloc([4, SEQ], BF16)
        fcst = AR.alloc([8], F32)
        mark0 = AR.mark()
        featsT = AR.alloc([SEQ], F32)
        hA = AR.alloc([SEQ], F32)
        hB = AR.alloc([SEQ], F32)
        tmp = Ring([AR.alloc([512], F32) for _ in range(4)])
        tmpi = Ring([AR.alloc([512], I32) for _ in range(2)])
        DMA("sp", d_misc, featsT[0:33, :], featsT_d)
        freq = ccol("f_freq", l, rows=64)
        for li, bn in enumerate(("f_b1", "f_b2", "f_b3")):
            TS("dve", fcst[0:64, li:li + 1], ccol(bn, l, rows=64), freq, 17.0 * PI, ALU.mult, ALU.add)
        srcs = [featsT, hA, hB]
        dsts = [hA, hB, h3T]
        for li in range(3):
            kk = 33 if li == 0 else 64
            wv = cst[0:kk, CLAY["f_w%d" % (li + 1)][0] + l * 64: CLAY["f_w%d" % (li + 1)][0] + (l + 1) * 64]
            for tt in range(NT):
                ts = slice(tt * 512, (tt + 1) * 512)
                ps = psring.next()
                MM(ps[0:64, :], wv, srcs[li][0:kk, ts])
                t_ = tmp.next()
                TS("dve", t_[0:64, :], ps[0:64, :], freq, fcst[0:64, li:li + 1], ALU.mult, ALU.add)
                ka = tmp.next()
                ki = tmpi.next()
                TS("dve", ka[0:64, :], t_[0:64, :], 1.0 / (2.0 * PI), None, ALU.mult)
                CP("dve", ki[0:64, :], ka[0:64, :])
                CP("dve", ka[0:64, :], ki[0:64, :])
                STT("dve", t_[0:64, :], ka[0:64, :], -2.0 * PI, t_[0:64, :], ALU.mult, ALU.add)
                TS("dve", ka[0:64, :], t_[0:64, :], 0.0, 2.0 * PI, ALU.is_lt, ALU.mult)
                TT("dve", t_[0:64, :], t_[0:64, :], ka[0:64, :], ALU.add)
                ACTF(dsts[li][0:64, ts], t_[0:64, :], AF.Sin, bias=cst[0:64, CLAY["negpi"][0]:CLAY["negpi"][0] + 1])
        for hh in range(2):
            AR.reset(mark0)
            u = [AR.alloc([2, SEQ], BF16) for _ in range(3)]
            mark2 = AR.mark()
            hnb[0] = AR.alloc([8, SEQ], BF16)
            rmsnorm("mix_norm", l * 8)
            wslots = [AR.alloc([8, 256], BF16) for _ in range(2)]
            stage = AR.alloc([SEQ + 2], F32)
            acc = AR.alloc([SEQ], F32)
            MEMSET("pool", stage[:, 0:1], 0.0)
            MEMSET("pool", stage[:, SEQ + 1:SEQ + 2], 0.0)
            for s in range(3):
                def consume(m, tt, ps, s=s):
                    CP("act", stage[:, 1 + tt * 512:1 + (tt + 1) * 512], ps[:])
                    if tt == NT - 1:
                        ch = s * 4 + hh * 2 + m
                        w = [ccol("hy_conv", l * 36 + ch * 3 + k) for k in range(3)]
                        b = ccol("hy_bias", l * 12 + ch)
                        TS("dve", acc[:], stage[:, 0:SEQ], w[0], b, ALU.mult, ALU.add)
                        STT("dve", acc[:], stage[:, 1:SEQ + 1], w[1], acc[:], ALU.mult, ALU.add)
                        STT("dve", u[s][:, m, :], stage[:, 2:SEQ + 2], w[2], acc[:], ALU.mult, ALU.add)
                project(l, 2064 + s * 512 + hh * 256, 256, wslots, consume)
            for o in range(2):
                AR.reset(mark2)
                Kt = AR.alloc([32, 256], BF16)
                dft = [AR.alloc([2, 2048], BF16) for _ in range(2)]
                mark3 = AR.mark()
                woutT = AR.alloc([2, 256], F32)
                Pm = AR.alloc([16, 256], BF16)
                Mm = AR.alloc([16, 256], BF16)
                fw = Ring([AR.alloc([2, 256], F32) for _ in range(2)])
                wn = Ring([AR.alloc([256], F32) for _ in range(2)])
                d_misc.new_batch()
                for d in range(2):
                    c0 = d * 1024 + o * 512 + hh * 256
                    DMA("sp", d_misc, woutT[0:64, d, :], Wd_["hy_f_wout"][LW[l], :, c0:c0 + 256], new_batch=False)
                for i in range(16):
                    ps = psring.next()
                    for d in range(2):
                        MM(ps[:, d * 256:(d + 1) * 256], h3T[0:64, i * 128:(i + 1) * 128], woutT[0:64, d, :])
                    w_ = wn.next()
                    ACTF(w_[:], deltas[:, hh * 256:(hh + 1) * 256], AF.Exp, scale=ccol("negt", i))
                    f_ = fw.next()
                    psv = View(ps[:].ap.rearrange("p (a b) -> p a b", b=256), ps[:].pages)
                    wb = View(w_[:].ap.unsqueeze(1).to_broadcast([128, 2, 256]), w_[:].pages)
                    TT("dve", f_[:], psv, wb, ALU.mult)
                    if i == 0:
                        MEMSET("dve", f_[0:1, 1, :], 0.0)
                    TT("dve", Pm[:, i, :], f_[:, 0, :], f_[:, 1, :], ALU.add)
                    TT("dve", Mm[:, i, :], f_[:, 0, :], f_[:, 1, :], ALU.subtract)

                def load_fwd(j):
                    s = j % 2
                    d_dft[s].new_batch()
                    DMA("sp", d_dft[s], dft[s][:, 0, :], cf_d[j], new_batch=False)
                    DMA("sp", d_dft[s], dft[s][:, 1, :], sf_d[j], new_batch=False)

                load_fwd(0)
                for j in range(16):
                    s = j % 2
                    if j + 1 < 16:
                        load_fwd(j + 1)
                    psA = psring.next()
                    psB = psring.next()
                    for i in range(16):
                        MM(psA[:, 0:256], dft[s][:, 0, i * 128:(i + 1) * 128], Pm[:, i, :], start=(i == 0), stop=(i == 15))
                    for i in range(16):
                        MM(psB[:, 0:256], dft[s][:, 1, i * 128:(i + 1) * 128], Mm[:, i, :], start=(i == 0), stop=(i == 15))
                    CP("act", Kt[:, 2 * j, :], psA[:, 0:256])
                    CP("act", Kt[:, 2 * j + 1, :], psB[:, 0:256])
                    if j == 0:
                        psC = psring.next()
                        for i in range(16):
                            MM(psC[:, 0:256], dft[s][:, 1, i * 128:(i + 1) * 128], Pm[:, i, :], start=(i == 0), stop=(i == 15))
                        CP("act", Kt[0:1, 1, :], psC[0:1, 0:256])
                AR.reset(mark3)
                ztok = AR.alloc([16, 256], BF16)
                ctmp = Ring([AR.alloc([256], F32) for _ in range(4)])
                zT = u[0]
                gate = u[1 + o]
                for i in range(16):
                    ps = psring.next()
                    pbf = psbf[id(ps)]
                    for m in range(2):
                        src_v = zT[:, m, i * 128:(i + 1) * 128]
                        ov = pbf[:, m * 128:(m + 1) * 128]
                        P.pe(lambda e, ov=ov, src_v=src_v: e.transpose(ov.ap, src_v.ap, ident_bf[:].ap),
                             reads=[src_v, ident_bf[:]], writes=[ov])
                    CP("act", ztok[:, i, :], pbf[:, 0:256])
                load_fwd(0)
                for j in range(16):
                    s = j % 2
                    if j + 1 < 16:
                        load_fwd(j + 1)
                    psR = psring.next()
                    psI = psring.next()
                    for i in range(16):
                        MM(psR[:, 0:256], dft[s][:, 0, i * 128:(i + 1) * 128], ztok[:, i, :], start=(i == 0), stop=(i == 15))
                    for i in range(16):
                        MM(psI[:, 0:256], dft[s][:, 1, i * 128:(i + 1) * 128], ztok[:, i, :], start=(i == 0), stop=(i == 15))
                    kre = Kt[:, 2 * j, :]
                    kim = Kt[:, 2 * j + 1, :]
                    t1, t2, t3, t4 = (ctmp.next() for _ in range(4))
                    TT("dve", t1[:], psR[:, 0:256], kre, ALU.mult)
                    TT("dve", t2[:], psI[:, 0:256], kim, ALU.mult)
                    TT("dve", t3[:], psR[:, 0:256], kim, ALU.mult)
                    TT("dve", t4[:], psI[:, 0:256], kre, ALU.mult)
                    TT("dve", kre, t1[:], t2[:], ALU.subtract)
                    TT("dve", kim, t3[:], t4[:], ALU.add)
                    if j == 0:
                        CP("dve", Kt[0:1, 0, :], t1[0:1, :])
                        CP("dve", Kt[0:1, 1, :], t2[0:1, :])

                def load_inv(th, j):
                    s = j % 2
                    d_dft[s].new_batch()
                    DMA("sp", d_dft[s], dft[s][:, 0, 0:1024], ci_d[j * 128:(j + 1) * 128, th * 1024:(th + 1) * 1024], new_batch=False)
                    DMA("sp", d_dft[s], dft[s][:, 1, 0:1024], si_d[j * 128:(j + 1) * 128, th * 1024:(th + 1) * 1024], new_batch=False)

                for th in range(2):
                    accs = [[psring.next() for _ in range(2)] for _ in range(2)]
                    load_inv(th, 0)
                    for j in range(16):
                        s = j % 2
                        if j + 1 < 16:
                            load_inv(th, j + 1)
                        for m in range(2):
                            for tq in range(2):
                                MM(accs[m][tq][:], Kt[:, 2 * j, m * 128:(m + 1) * 128], dft[s][:, 0, tq * 512:(tq + 1) * 512],
                                   start=(j == 0), stop=False)
                                MM(accs[m][tq][:], Kt[:, 2 * j + 1, m * 128:(m + 1) * 128], dft[s][:, 1, tq * 512:(tq + 1) * 512],
                                   start=False, stop=(j == 15))
                    for m in range(2):
                        for tq in range(2):
                            tt = th * 2 + tq
                            ts = slice(tt * 512, (tt + 1) * 512)
                            sk = ccol("hy_skip", l * 8 + o * 4 + hh * 2 + m)
                            tv = accs[m][tq][:]
                            STT("dve", tv, zT[:, m, ts], sk, tv, ALU.mult, ALU.add)
                            dst = zT[:, m, ts] if o == 0 else ohy[:, hh * 2 + m, ts]
                            TT("dve", dst, tv, gate[:, m, ts], ALU.mult)
        AR.reset(mark0)
        sq = Ring([AR.alloc([512], BF16) for _ in range(3)])
        rs = Ring([AR.alloc([512], F32) for _ in range(2)])
        for tt in range(NT):
            ts = slice(tt * 512, (tt + 1) * 512)
            ps = psring.next()
            r = rs.next()
            rstd_from([ohy[:, m, ts] for m in range(4)], ps[:], r[:], sq, HYW, ccol("eps"))
            for m in range(4):
                STT("dve", ohy[:, m, ts], ohy[:, m, ts], ccol("hy_norm", l * 4 + m), r[:], ALU.mult, ALU.mult)
        hy_mark[0] = mark0

    def hyena_out(l):
        wout_apply(l, 512, 4, ohyb[0])

    def deltanet(l):
        AR.reset(hy_mark[0])
        hn = hnb[0] = AR.alloc([8, SEQ], BF16)
        rmsnorm("mix_norm", l * 8)
        wba = AR.alloc([8, 16], BF16)
        ba = AR.alloc([16, 16], F32)
        beta_t = AR.alloc([16, 8], F32)
        g_t = AR.alloc([16, 8], F32)
        gc_t = AR.alloc([16, 8], F32)
        ngc_t = AR.alloc([16, 8], F32)
        bw_t = AR.alloc([16, 8], F32)
        kds_t = AR.alloc([16, 8], F32)
        egl = AR.alloc([16, 16], F32)
        DMA("pool", d_wba, wba[:], Wd_["w_in"][LW[l], :, 2048:2064].rearrange("(k p) f -> p k f", p=128))
        ps = psring.next()
        for blk in range(16):
            for k in range(8):
                MM(ps[:, blk * 16:(blk + 1) * 16], hn[:, k, blk * 128:(blk + 1) * 128], wba[:, k, :], start=(k == 0), stop=(k == 7))
        bav = View(ba[:].ap.rearrange("p a b -> p (a b)"), ba[:].pages)
        CP("act", bav, ps[:, 0:256])
        dn_stop = cfg.get("dn_stop", 99)
        if dn_stop <= -3:
            return
        ACTF(beta_t[:], ba[:, :, 0:8], AF.Sigmoid)
        dtb = View(ccol("dtb", l * 8, 8).ap.unsqueeze(1).to_broadcast([128, 16, 8]), ccol("dtb", l * 8, 8).pages)
        nab = View(negA[:, l * 8:(l + 1) * 8].ap.unsqueeze(1).to_broadcast([128, 16, 8]), negA[:].pages)
        TT("dve", g_t[:], ba[:, :, 8:16], dtb, ALU.add)
        ACTF(g_t[:], g_t[:], AF.Exp)
        ACTF(g_t[:], g_t[:], AF.Ln, bias=ccol("one"))
        TT("dve", g_t[:], g_t[:], nab, ALU.mult)
        if dn_stop <= -2:
            return
        ps = psring.next()
        ps2 = psring.next()
        ps3 = psring.next()
        v3 = lambda v, b: View(v.ap.rearrange("p (a b) -> p a b", b=b), v.pages)
        MM(ps[:, 0:64], mat("cum_f"), g_t[:, :, 0:4])
        MM(ps[:, 64:128], mat("cum_b"), g_t[:, :, 4:8])
        MM(ps2[:, 0:128], mat("inda"), g_t[:])
        MM(ps2[:, 128:256], mat("indb"), g_t[:])
        MM(ps3[:, 0:128], mat("blk"), g_t[:])
        flat = lambda t: View(t[:].ap.rearrange("p a b -> p (a b)"), t[:].pages)
        if dn_stop <= -1:
            return
        nel = cfg.get("dn_nel", 99)
        elops = [
            lambda: CP("act", gc_t[:, :, 0:4], v3(ps[:, 0:64], 4)),
            lambda: CP("act", gc_t[:, :, 4:8], v3(ps[:, 64:128], 4)),
            lambda: TS("dve", flat(ngc_t), flat(gc_t), -1.0, None, ALU.mult),
            lambda: ACTF(egl[:, :, 0:8], v3(ps2[:, 0:128], 8), AF.Exp),
            lambda: ACTF(egl[:, :, 8:16], v3(ps2[:, 128:256], 8), AF.Exp),
            lambda: TT("dve", flat(kds_t), ps3[:, 0:128], flat(gc_t), ALU.subtract),
            lambda: ACTF(flat(kds_t), flat(kds_t), AF.Exp),
            lambda: ACTF(flat(bw_t), flat(gc_t), AF.Exp),
            lambda: TT("dve", flat(bw_t), flat(bw_t), flat(beta_t), ALU.mult),
        ]
        for f_ in elops[:nel]:
            f_()
        markh = AR.mark()
        dn_stop = cfg.get("dn_stop", 99)
        if dn_stop <= 0:
            return
        for h in range(cfg.get("dn_heads", 4)):
            AR.reset(markh)
            qkvz = [AR.alloc([SEQ], BF16) for _ in range(4)]
            qT, kT, vT, zg = qkvz
            ktok = AR.alloc([16, 128], BF16)
            vtok = AR.alloc([16, 128], BF16)
            oT = AR.alloc([SEQ], F32)
            S = [AR.alloc([128], F32) for _ in range(2)]
            Sb = [AR.alloc([128], BF16) for _ in range(2)]
            markp = AR.mark()
            wslots = [AR.alloc([8, 128], BF16) for _ in range(2)]
            stage = AR.alloc([SEQ + 4], F32)
            acc = AR.alloc([SEQ], F32)
            sq = Ring([AR.alloc([512], BF16) for _ in range(2)])
            rs = Ring([AR.alloc([512], F32) for _ in range(2)])
            MEMSET("pool", stage[:, 0:2], 0.0)
            MEMSET("pool", stage[:, SEQ + 2:SEQ + 4], 0.0)
            for s in range(4):
                def consume(m, tt, ps, s=s):
                    CP("act", stage[:, 2 + tt * 512:2 + (tt + 1) * 512], ps[:])
                    if tt != NT - 1:
                        return
                    if s == 3:
                        ACTF(zg[:], stage[:, 2:SEQ + 2], AF.Silu)
                        return
                    ch = s * 4 + h
                    w = [ccol("dn_conv", l * 60 + ch * 5 + k) for k in range(5)]
                    TS("dve", acc[:], stage[:, 0:SEQ], w[0], None, ALU.mult)
                    for k in range(1, 5):
                        STT("dve", acc[:], stage[:, k:SEQ + k], w[k], acc[:], ALU.mult, ALU.add)
                    if s == 2:
                        ACTF(vT[:], acc[:], AF.Silu)
                        return
                    ACTF(acc[:], acc[:], AF.Silu)
                    dst = qkvz[s]
                    for t2 in range(NT):
                        ts = slice(t2 * 512, (t2 + 1) * 512)
                        p2 = psring.next()
                        r = rs.next()
                        rstd_from([acc[:, ts]], p2[:], r[:], sq, 1.0, ccol("eps_l2"))
                        if s == 0:
                            STT("dve", dst[:, ts], acc[:, ts], 128.0 ** -0.5, r[:], ALU.mult, ALU.mult)
                        else:
                            TT("dve", dst[:, ts], acc[:, ts], r[:], ALU.mult)
                project(l, s * 512 + h * 128, 128, wslots, consume)
            for blk in range(16):
                bs = slice(blk * 128, (blk + 1) * 128)
                ps = psring.next()
                pbf = psbf[id(ps)]
                for m, srcT in enumerate((kT, vT)):
                    src_v = srcT[:, bs]
                    ov = pbf[:, m * 128:(m + 1) * 128]
                    P.pe(lambda e, ov=ov, src_v=src_v: e.transpose(ov.ap, src_v.ap, ident_bf[:].ap),
                         reads=[src_v, ident_bf[:]], writes=[ov])
                CP("act", ktok[:, blk, :], pbf[:, 0:128])
                CP("act", vtok[:, blk, :], pbf[:, 128:256])
            if dn_stop <= 1:
                continue
            AR.reset(markp)
            NF = 5
            NB = 15
            sets = [[{"f": [AR.alloc([128], F32) for _ in range(NF)], "b": [AR.alloc([128], BF16, align=256) for _ in range(NB)]}
                     for _ in range(2)] for _ in range(2)]
            MEMSET("pool", oT[:], 0.0)
            for d in range(2):
                MEMSET("pool", S[d][:], 0.0)
                MEMSET("pool", Sb[d][:], 0.0)
            ident = mat("ident")
            def unit(step, d):
                blk = step if d == 0 else 15 - step
                bs = slice(blk * 128, (blk + 1) * 128)
                r_ = d * 4 + h
                T = sets[d][step % 2]
                dg, Ds, DTm, Er, Pm_ = T["f"]
                Pb, kbg, vb, kdec, wT, attnT, qd, vn, N_, M_, Na, Ma, Nb_, Mb_, Pw = T["b"]
                gcc = gc_t[:, blk, r_:r_ + 1]
                ngc = ngc_t[:, blk, r_:r_ + 1]
                bcol = beta_t[:, blk, r_:r_ + 1]
                TS("dve", dg[:], ident, gcc, None, ALU.mult)
                yield
                pR = psring.next()
                MM(pR[:, 0:128], mat("ones"), dg[:])
                yield
                STT("dve", Ds[:], pR[:, 0:128], -1.0, mat("neg_ls" if d == 0 else "neg_us"), ALU.mult, ALU.add)
                yield
                ACTF(Ds[:], Ds[:], AF.Exp, bias=gcc)
                yield
                TT("dve", DTm[:], pR[:, 0:128], mat("neg_ui" if d == 0 else "neg_li"), ALU.add)
                yield
                ACTF(DTm[:], DTm[:], AF.Exp, bias=ngc)
                yield
                ACTF(Er[:], pR[:, 0:128], AF.Exp)
                yield
                pK = psring.next()
                MM(pK[:, 0:128], kT[:, bs], kT[:, bs])
                yield
                MM(pK[:, 128:256], kT[:, bs], qT[:, bs])
                yield
                STT("dve", N_[:], pK[:, 0:128], bcol, Ds[:], ALU.mult, ALU.mult)
                yield
                TT("dve", attnT[:], pK[:, 128:256], DTm[:], ALU.mult)
                yield
                TT("dve", qd[:], qT[:, bs], Er[:], ALU.mult)
                yield
                pre_stop = cfg.get("pre_stop", 99)
                if pre_stop <= 2:
                    return
                pT = psring.next()
                pTb = psbf[id(pT)]
                P.pe(lambda e, o_=pTb[:, 0:128], i_=N_[:]: e.transpose(o_.ap, i_.ap, ident_bf[:].ap),
                     reads=[N_[:], ident_bf[:]], writes=[pTb[:, 0:128]])
                CP("act", M_[:], pTb[:, 0:128])
                yield
                STT("dve", Pm_[:], pTb[:, 0:128], -1.0, ident, ALU.mult, ALU.add)
                yield
                STT("dve", Pw[:], pTb[:, 0:128], -1.0, ident, ALU.mult, ALU.add)
                yield
                if pre_stop <= 3:
                    return
                Nc, Mc = N_, M_
                nxt = [(Na, Ma), (Nb_, Mb_)]
                for lvl in range(5):
                    N2, M2 = nxt[lvl % 2]
                    pq = psring.next()
                    MM(pq[:, 0:128], Mc[:], Nc[:])
                    if lvl < 4:
                        MM(pq[:, 128:256], Nc[:], Mc[:])
                    CP("act", N2[:], pq[:, 0:128])
                    yield
                    if lvl < 4:
                        CP("act", M2[:], pq[:, 128:256])
                        yield
                    pp = psring.next()
                    MM(pp[:, 0:128], N2[:], Pw[:])
                    yield
                    if lvl < 4:
                        TT("dve", Pm_[:], Pm_[:], pp[:, 0:128], ALU.add)
                        yield
                        CP("act", Pw[:], Pm_[:])
                        yield
                    else:
                        TT("dve", Pb[:], Pm_[:], pp[:, 0:128], ALU.add)
                        yield
                    Nc, Mc = N2, M2
                if pre_stop <= 4:
                    return
                TS("dve", kbg[:], ktok[:, blk, :], bw_t[:, blk, r_:r_ + 1], None, ALU.mult)
                yield
                TS("dve", vb[:], vtok[:, blk, :], bcol, None, ALU.mult)
                yield
                TS("dve", kdec[:], ktok[:, blk, :], kds_t[:, blk, r_:r_ + 1], None, ALU.mult)
                yield
                if pre_stop <= 5:
                    return
                pu = psring.next()
                MM(pu[:, 0:128], Pb[:], vb[:])
                yield
                MM(pu[:, 128:256], kbg[:], Pb[:])
                yield
                u_ = dg
                CP("act", u_[:], pu[:, 0:128])
                yield
                CP("act", wT[:], pu[:, 128:256])
                yield
                if dn_stop <= 2:
                    return
                yield "SPLIT"
                for cc in ((0, 1) if d == 0 else (1, 0)):
                    r0 = cc * 64
                    rr = slice(r0, r0 + 64)
                    pw = psring.next()
                    MM(pw[rr, 0:128], wT[:, rr], Sb[d][:])
                    yield
                    TT("dve", vn[rr, :], u_[rr, :], pw[rr, 0:128], ALU.subtract)
                    yield
                    po = psring.next()
                    MM(po[:, 0:64], Sb[d][:], qd[:, rr], start=True, stop=False)
                    yield
                    MM(po[:, 0:64], vn[rr, :], attnT[rr, rr], start=False, stop=True)
                    yield
                    ov = oT[:, blk * 128 + r0:blk * 128 + r0 + 64]
                    TT("dve", ov, ov, po[:, 0:64], ALU.add)
                    yield
                    pd = psring.next()
                    MM(pd[:, 0:128], kdec[rr, :], vn[rr, :])
                    yield
                    STT("dve", S[d][:], S[d][:], egl[:, blk, cc * 8 + r_:cc * 8 + r_ + 1], pd[:, 0:128], ALU.mult, ALU.add)
                    yield
                    CP("act", Sb[d][:], S[d][:])
                    yield

                yield

            nsteps = cfg.get("dn_steps", 16)

            def drive(active):
                while active:
                    for it in list(active):
                        try:
                            v_ = next(it[0])
                        except StopIteration:
                            active.remove(it)
                            continue
                        if v_ == "SPLIT" and it[1] == "pre":
                            active.remove(it)

            cur = [unit(0, 0), unit(0, 1)]
            drive([[g_, "pre"] for g_ in cur])
            for step in range(nsteps):
                nxt = [unit(step + 1, 0), unit(step + 1, 1)] if step + 1 < nsteps else []
                drive([[g_, "rec"] for g_ in cur] + [[g_, "pre"] for g_ in nxt])
                cur = nxt

            AR.reset(markp)
            sq = Ring([AR.alloc([512], BF16) for _ in range(2)])
            rs = Ring([AR.alloc([512], F32) for _ in range(2)])
            odn = AR.alloc([1, SEQ], BF16)
            for tt in range(NT):
                ts = slice(tt * 512, (tt + 1) * 512)
                p2 = psring.next()
                r = rs.next()
                rstd_from([oT[:, ts]], p2[:], r[:], sq, 128.0, ccol("eps"))
                STT("dve", oT[:, ts], oT[:, ts], ccol("dn_norm", l), r[:], ALU.mult, ALU.mult)
                TT("dve", odn[:, 0, ts], oT[:, ts], zg[:, ts], ALU.mult)
            wout_apply(l, h * 128, 1, odn)

    for l in layers:
        if "ffn1" in stages:
            ffn(l, 0)
        if "hy" in stages:
            hyena(l)
        else:
            AR.reset()
            hy_mark[0] = 0
        if "dn" in stages:
            deltanet(l)
        if "hy" in stages:
            hyena_out(l)
        if "ffn2" in stages:
            ffn(l, 1)
    if cfg.get("final_norm", True):
        AR.reset()
        rmsnorm("final_norm", 0, final=True)
    else:
        AR.reset()
        d_out.new_batch()
        for k in range(8):
            v = xT[:, k, :]
            P.dma("sp", d_out, lambda e, v=v, k=k: e.dma_start(out=outT_d[k * 128:(k + 1) * 128, :], in_=v.ap),
                  reads=[v], writes=[DramRes()], new_batch=False)
    for ds_ in [d_out] + d_outs:
        if ds_.batch is not None:
            P.final_batches.append(ds_.batch)
    stats = P.emit()
    P.es.close()
    return nc, stats


_CACHE = {}


def make_in_maps(inputs, cfg=None):
    stages = (cfg or {}).get("stages", ("ffn1", "dn", "hy", "ffn2"))
    st = static_consts()
    shared = {"cst": build_cst(inputs)}
    shared.update(st)
    for nm in ("ffn1_w_gate", "ffn1_w_up", "ffn1_w_down", "ffn2_w_gate", "ffn2_w_up", "ffn2_w_down",
               "w_in", "w_out", "hy_f_wout"):
        if nm.startswith("ffn") and nm[:4] not in stages:
            continue
        a_ = np.asarray(inputs[nm], dtype=np.float32)
        if cfg and "layers" in cfg:
            a_ = a_[list(cfg["layers"])]
        shared[nm] = np.ascontiguousarray(a_)
    maps = []
    xT_list = (cfg or {}).get("_xT")
    x = None if xT_list is not None else np.asarray(inputs["x"], dtype=np.float32)
    for b in range(8):
        m = dict(shared)
        m["xT"] = xT_list[b] if xT_list is not None else np.ascontiguousarray(x[b].T)
        maps.append(m)
    return maps


LAUNCH_GROUPS = ((0, 1, 2, 3),)


def _get_prog(cfg):
    key = repr(sorted((k, v) for k, v in cfg.items() if k != "_xT"))
    if key not in _CACHE:
        _CACHE[key] = build_program({k: v for k, v in cfg.items() if k != "_xT"})
    return _CACHE[key]


def kernel(**inputs):
    cfg = inputs.pop("_cfg", None)
    if cfg is not None:
        nc, stats = _get_prog(cfg)
        maps = make_in_maps(inputs, cfg)
        res = run_bass_kernel_spmd(nc, maps, core_ids=list(range(8)))
        out = np.stack([np.ascontiguousarray(r["outT"].T) for r in res.results], axis=0)
        return out.astype(np.float32)
    xT_list = None
    for gi, grp in enumerate(LAUNCH_GROUPS):
        last = gi == len(LAUNCH_GROUPS) - 1
        if len(LAUNCH_GROUPS) == 1:
            cfg = {}
        else:
            cfg = {"layers": tuple(grp), "final_norm": last}
        nc, stats = _get_prog(cfg)
        c2 = dict(cfg)
        if xT_list is not None:
            c2["_xT"] = xT_list
        maps = make_in_maps(inputs, c2)
        res = run_bass_kernel_spmd(nc, maps, core_ids=list(range(8)))
        xT_list = [np.ascontiguousarray(r["outT"]) for r in res.results]
    out = np.stack([np.ascontiguousarray(a.T) for a in xT_list], axis=0)
    return out.astype(np.float32)
```

```python
import math
import numpy as np
import ml_dtypes
from contextlib import ExitStack
import concourse.bass as bass
import concourse.mybir as mybir
from concourse.bass_utils import run_bass_kernel_spmd

F32 = mybir.dt.float32
BF16 = mybir.dt.bfloat16
ALU = mybir.AluOpType
AF = mybir.ActivationFunctionType

D_MODEL = 1024
SEQ = 2048
DEPTH = 4
D_FF = 2816
D_IN = 3600
HYW = 512
NT = 4
KC = 8
RMS_EPS = 1e-6

ENGS = ("pe", "act", "dve", "pool", "sp")
ENGATTR = {"pe": "tensor", "act": "scalar", "dve": "vector", "pool": "gpsimd", "sp": "sync"}
I32 = mybir.dt.int32
DTSIZE = {F32: 4, BF16: 2, I32: 4}


class Res:
    __slots__ = ("last_w", "readers", "psum")

    def __init__(self, psum=False):
        self.last_w = None
        self.readers = []
        self.psum = psum


class Batch:
    __slots__ = ("sem", "final")

    def __init__(self, sem):
        self.sem = sem
        self.final = 0


class DSem:
    def __init__(self, prog, name):
        self.prog = prog
        self.name = name
        self.gen = 0
        self.sem = prog.sem(name)
        self.count = 0
        self.batch = None
        self.prev = None

    def new_batch(self):
        self.prev = self.batch
        if self.count > 3000:
            self.gen += 1
            self.sem = self.prog.sem("%s_g%d" % (self.name, self.gen))
            self.count = 0
        self.batch = Batch(self.sem)
        self.batch.final = self.count
        return self.batch


class Instr:
    __slots__ = ("eng", "fn", "deps", "idx", "marked", "batch")

    def __init__(self, eng, fn):
        self.eng = eng
        self.fn = fn
        self.deps = []
        self.marked = False
        self.batch = None


class Mem:
    def __init__(self, prog, name, nbytes, page, space="sb"):
        self.name = name
        self.nbytes = nbytes
        self.page = page
        self.pages = [Res(psum=(space != "sb")) for _ in range((nbytes + page - 1) // page)]
        if space == "sb":
            self.t = prog.es.enter_context(prog.nc.sbuf_tensor(name, [128, nbytes // 4], F32))
        else:
            self.t = prog.es.enter_context(prog.nc.psum_tensor(name, [128, nbytes // 4], F32))
        self.bump = 0

    def reset(self, to=0):
        self.bump = to

    def mark(self):
        return self.bump

    def alloc(self, shape, dt, align=None):
        n = int(np.prod(shape)) * DTSIZE[dt]
        al = align or min(self.page, 512)
        base = (self.bump + al - 1) // al * al
        assert base + n <= self.nbytes, (self.name, base, n, self.nbytes)
        self.bump = base + n
        return Tile(self, base, shape, dt)


class View:
    __slots__ = ("ap", "pages")

    def __init__(self, ap, pages):
        self.ap = ap
        self.pages = pages


class Tile:
    def __init__(self, mem, base, shape, dt):
        self.mem = mem
        self.base = base
        self.shape = tuple(shape)
        self.dt = dt
        esz = DTSIZE[dt]
        n = int(np.prod(shape))
        assert base % 4 == 0 and (n * esz) % 4 == 0
        ap = mem.t[:, base // 4:(base + n * esz) // 4]
        if dt != F32:
            ap = ap.bitcast(dt)
        if len(shape) == 2:
            ap = ap.rearrange("p (a b) -> p a b", b=shape[1])
        elif len(shape) == 3:
            ap = ap.rearrange("p (a b c) -> p a b c", b=shape[1], c=shape[2])
        self.ap = ap
        self.strides = [int(np.prod(shape[i + 1:])) for i in range(len(shape))]

    def __getitem__(self, idx):
        if not isinstance(idx, tuple):
            idx = (idx,)
        ap = self.ap[idx]
        lo = 0
        hi = 0
        fidx = idx[1:]
        for d in range(len(self.shape)):
            if d < len(fidx):
                ix = fidx[d]
                if isinstance(ix, slice):
                    a = 0 if ix.start is None else ix.start
                    b = self.shape[d] if ix.stop is None else ix.stop
                else:
                    a, b = ix, ix + 1
            else:
                a, b = 0, self.shape[d]
            lo += a * self.strides[d]
            hi += (b - 1) * self.strides[d]
        esz = DTSIZE[self.dt]
        b0 = self.base + lo * esz
        b1 = self.base + (hi + 1) * esz
        pg = self.mem.page
        pages = self.mem.pages[b0 // pg:(b1 + pg - 1) // pg]
        return View(ap, pages)

    def all(self):
        return self[:]


class Ring:
    def __init__(self, items):
        self.items = items
        self.i = 0

    def next(self):
        it = self.items[self.i % len(self.items)]
        self.i += 1
        return it


class Prog:
    def __init__(self, nc, same_eng_sync=True):
        self.nc = nc
        self.es = ExitStack()
        self.streams = {e: [] for e in ENGS}
        self.same_eng_sync = same_eng_sync
        self.final_batches = []

    def sem(self, name):
        return self.es.enter_context(self.nc.semaphore(name))

    def op(self, eng, fn, reads=(), writes=(), batch=None, dsem=None):
        ins = Instr(eng, fn)
        st = self.streams[eng]
        ins.idx = len(st)
        st.append(ins)
        deps = ins.deps
        rp = []
        for r in reads:
            rp.extend(r.pages)
        wp = []
        for w in writes:
            wp.extend(w.pages)
        for r in rp:
            if r.last_w is not None:
                deps.append(r.last_w)
            if r.psum:
                for t_ in r.readers:
                    if t_[0] == "eng" and t_[1] != eng:
                        deps.append(t_)
        for w in wp:
            if w.last_w is not None:
                deps.append(w.last_w)
            deps.extend(w.readers)
        if batch is not None:
            ins.deps = deps = [d for d in deps if not (d[0] == "dma" and d[1] is batch)]
            ins.batch = batch
            dsem.count += 16
            batch.final = dsem.count
            tok = ("dma", batch)
        else:
            tok = ("eng", eng, ins.idx)
        for r in rp:
            r.readers.append(tok)
        for w in wp:
            w.last_w = tok
            w.readers = []
        return ins

    def pe(self, fn, reads=(), writes=()):
        return self.op("pe", fn, reads, writes)

    def act(self, fn, reads=(), writes=()):
        return self.op("act", fn, reads, writes)

    def dve(self, fn, reads=(), writes=()):
        return self.op("dve", fn, reads, writes)

    def pool(self, fn, reads=(), writes=()):
        return self.op("pool", fn, reads, writes)

    def dma(self, eng, dsem, fn, reads=(), writes=(), new_batch=True):
        if new_batch or dsem.batch is None:
            dsem.new_batch()
        ins = self.op(eng, fn, reads, writes, batch=dsem.batch, dsem=dsem)
        pv = getattr(dsem, "prev", None)
        if pv is not None:
            ins.deps.append(("dma", pv))
        return ins

    def emit(self):
        nc = self.nc
        sync_same = self.same_eng_sync
        for e in ENGS:
            for ins in self.streams[e]:
                for d in ins.deps:
                    if d[0] == "eng":
                        if d[1] == ins.eng and (ins.eng == "pe" or (not sync_same and ins.eng != "pool")):
                            continue
                        self.streams[d[1]][d[2]].marked = True
        cum = {}
        for e in ENGS:
            c = 0
            arr = []
            for ins in self.streams[e]:
                if ins.marked and ins.batch is None:
                    c += 1
                arr.append(c)
            cum[e] = arr
        SEG = 4000
        esem = {e: [self.sem("s_%s%d" % (e, k)) for k in range(max(1, (cum[e][-1] + SEG - 1) // SEG if cum[e] else 1))]
                for e in ENGS}
        stats = {"waits": 0, "incs": 0, "n": {e: len(self.streams[e]) for e in ENGS}}
        final_batches = self.final_batches
        with nc.Block() as block:
            for e in ENGS:
                stream = self.streams[e]

                def body(engine, e=e, stream=stream):
                    waited = {}
                    for ins in stream:
                        need = {}
                        for d in ins.deps:
                            if d[0] == "eng":
                                if d[1] == e and (e == "pe" or (not sync_same and e != "pool")):
                                    continue
                                c_ = cum[d[1]][d[2]]
                                k_ = (c_ - 1) // SEG
                                key = (d[1], k_)
                                sem = esem[d[1]][k_]
                                val = (c_ - 1) % SEG + 1
                            else:
                                key = id(d[1].sem)
                                sem = d[1].sem
                                val = d[1].final
                            if waited.get(key, 0) >= val:
                                continue
                            if key not in need or need[key][1] < val:
                                need[key] = (sem, val)
                        for key, (sem, val) in need.items():
                            engine.wait_ge(sem, val)
                            waited[key] = val
                            stats["waits"] += 1
                        bi = ins.fn(engine)
                        if ins.batch is not None:
                            bi.then_inc(ins.batch.sem, 16)
                        elif ins.marked:
                            bi.then_inc(esem[e][(cum[e][ins.idx] - 1) // SEG], 1)
                            stats["incs"] += 1
                    if e == "sp":
                        for b in final_batches:
                            engine.wait_ge(b.sem, b.final)

                getattr(block, ENGATTR[e])(body)
        return stats


class DramRes:
    def __init__(self):
        self.pages = [Res()]


PI = math.pi
CLAY = {}
NCST = 0


def _mk_layout():
    global NCST
    off = 0
    for name, n in (("ffn1_norm", DEPTH * 8), ("mix_norm", DEPTH * 8), ("ffn2_norm", DEPTH * 8), ("final_norm", 8),
                    ("eps", 1), ("eps_l2", 1), ("one", 1), ("negpi", 1),
                    ("dn_conv", DEPTH * 60), ("dn_norm", DEPTH), ("hy_conv", DEPTH * 36), ("hy_bias", DEPTH * 12),
                    ("hy_skip", DEPTH * 8), ("hy_norm", DEPTH * 4), ("dtb", DEPTH * 8), ("alog", DEPTH * 8),
                    ("f_b1", DEPTH), ("f_b2", DEPTH), ("f_b3", DEPTH), ("f_freq", DEPTH), ("negt", 16),
                    ("f_w1", DEPTH * 64), ("f_w2", DEPTH * 64), ("f_w3", DEPTH * 64)):
        CLAY[name] = (off, n)
        off += n
    NCST = off


_mk_layout()
MATS = ("ident", "ones", "neg_ls", "neg_us", "neg_li", "neg_ui", "cum_f", "cum_b", "inda", "indb", "blk")
NEGV = -30000.0


def _chunked(v):
    return np.ascontiguousarray(np.asarray(v, np.float32).reshape(-1, 128).T)


def build_cst(inp):
    cst = np.zeros((128, NCST), np.float32)

    def put(name, arr, rows=128):
        o, n = CLAY[name]
        arr = np.asarray(arr, np.float32)
        assert arr.shape == (rows, n), (name, arr.shape, n)
        cst[:rows, o:o + n] = arr

    for nm in ("ffn1_norm", "mix_norm", "ffn2_norm"):
        put(nm, np.concatenate([_chunked(inp[nm][l]) for l in range(DEPTH)], axis=1))
    put("final_norm", _chunked(inp["final_norm"]))
    cst[:, CLAY["eps"][0]] = RMS_EPS
    cst[:, CLAY["eps_l2"][0]] = 1e-6
    cst[:, CLAY["one"][0]] = 1.0
    cst[:, CLAY["negpi"][0]] = -math.pi
    a = np.asarray(inp["dn_conv"], np.float32).reshape(DEPTH, 5, 12, 128).transpose(3, 0, 2, 1).reshape(128, -1)
    put("dn_conv", a)
    put("dn_norm", np.asarray(inp["dn_norm"], np.float32).T)
    a = np.asarray(inp["hy_conv"], np.float32).reshape(DEPTH, 3, 12, 128).transpose(3, 0, 2, 1).reshape(128, -1)
    put("hy_conv", a)
    a = np.asarray(inp["hy_conv_bias"], np.float32).reshape(DEPTH, 12, 128).transpose(2, 0, 1).reshape(128, -1)
    put("hy_bias", a)
    a = np.asarray(inp["hy_skip"], np.float32).reshape(DEPTH, 2, 4, 128).transpose(3, 0, 1, 2).reshape(128, -1)
    put("hy_skip", a)
    a = np.asarray(inp["hy_norm"], np.float32).reshape(DEPTH, 4, 128).transpose(2, 0, 1).reshape(128, -1)
    put("hy_norm", a)
    put("dtb", np.broadcast_to(np.asarray(inp["dn_dt_bias"], np.float32).reshape(1, -1), (128, DEPTH * 8)))
    put("alog", np.broadcast_to(np.asarray(inp["dn_a_log"], np.float32).reshape(1, -1), (128, DEPTH * 8)))
    for nm, key in (("f_b1", "hy_f_b1"), ("f_b2", "hy_f_b2"), ("f_b3", "hy_f_b3"), ("f_freq", "hy_f_freq")):
        put(nm, np.asarray(inp[key], np.float32).T, rows=64)
    t = np.linspace(0.0, 1.0, SEQ, dtype=np.float32)
    put("negt", -t.reshape(16, 128).T)
    put("f_w1", np.asarray(inp["hy_f_w1"], np.float32).transpose(1, 0, 2).reshape(33, -1), rows=33)
    put("f_w2", np.asarray(inp["hy_f_w2"], np.float32).transpose(1, 0, 2).reshape(64, -1), rows=64)
    put("f_w3", np.asarray(inp["hy_f_w3"], np.float32).transpose(1, 0, 2).reshape(64, -1), rows=64)
    return cst


_STATIC = {}


def static_consts():
    if _STATIC:
        return _STATIC
    i = np.arange(128)
    same = (i[:, None] // 64) == (i[None, :] // 64)
    row, col = i[:, None], i[None, :]
    m = {}
    m["ident"] = np.eye(128)
    m["ones"] = np.ones((128, 128))
    m["neg_ls"] = np.where(same & (row > col), 0.0, NEGV)
    m["neg_us"] = np.where(same & (row < col), 0.0, NEGV)
    m["neg_li"] = np.where(same & (row >= col), 0.0, NEGV)
    m["neg_ui"] = np.where(same & (row <= col), 0.0, NEGV)
    m["cum_f"] = (same & (row <= col)).astype(np.float64)
    m["cum_b"] = (same & (row >= col)).astype(np.float64)
    m["inda"] = np.broadcast_to((i < 64)[:, None], (128, 128)).astype(np.float64)
    m["indb"] = np.broadcast_to((i >= 64)[:, None], (128, 128)).astype(np.float64)
    m["blk"] = same.astype(np.float64)
    _STATIC["mats"] = np.ascontiguousarray(np.concatenate([m[k] for k in MATS], axis=1), dtype=np.float32)
    l = SEQ
    t = np.linspace(0.0, 1.0, l, dtype=np.float32)[:, None]
    bands = 16
    ang = ((np.float32(2.0 * math.pi / l)) * np.arange(l, dtype=np.float32)[:, None]
           * np.linspace(1e-4, bands - 1, bands, dtype=np.float32)[None, :]).astype(np.float32)
    feats = np.concatenate([t, np.cos(ang), -np.sin(ang)], axis=-1).astype(np.float32)
    _STATIC["featsT"] = np.ascontiguousarray(feats.T)
    max_decay = math.log(1e-2) / 0.3
    min_decay = math.log(1e-2) / 1.5
    deltas = np.abs(np.linspace(min_decay, max_decay, HYW, dtype=np.float32))
    _STATIC["deltas"] = np.ascontiguousarray(np.broadcast_to(deltas[None, :], (128, HYW)), dtype=np.float32)
    n = np.arange(SEQ, dtype=np.float64)
    ang = 2.0 * np.pi * np.outer(n, n) / (2 * SEQ)
    cf = np.cos(ang)
    sf = -np.sin(ang)
    sf[:, 0] = (-1.0) ** n
    ci = (2.0 / (2 * SEQ)) * np.cos(ang)
    ci[0, :] = 1.0 / (2 * SEQ)
    si = -(2.0 / (2 * SEQ)) * np.sin(ang)
    si[0, :] = ((-1.0) ** n) / (2 * SEQ)
    bf = ml_dtypes.bfloat16

    def fwd_layout(a):
        return np.ascontiguousarray(a.reshape(16, 128, 16, 128).transpose(2, 1, 0, 3).reshape(16, 128, 2048)).astype(bf)

    _STATIC["dft_cf"] = fwd_layout(cf)
    _STATIC["dft_sf"] = fwd_layout(sf)
    _STATIC["dft_ci"] = np.ascontiguousarray(ci).astype(bf)
    _STATIC["dft_si"] = np.ascontiguousarray(si).astype(bf)
    return _STATIC


def build_program(cfg):
    n_layers = cfg.get("n_layers", DEPTH)
    layers = tuple(cfg.get("layers", tuple(range(n_layers))))
    LW = {l_: i_ for i_, l_ in enumerate(layers)}
    LD = len(layers) if "layers" in cfg else DEPTH
    if "layers" not in cfg:
        LW = {l_: l_ for l_ in range(DEPTH)}
    stages = cfg.get("stages", ("ffn1", "dn", "hy", "ffn2"))
    nc = bass.Bass("TRN2", target_bir_lowering=False)
    P = Prog(nc, same_eng_sync=cfg.get("same_eng_sync", True))

    def din(name, shape, dt=F32):
        return nc.dram_tensor(name, list(shape), dt, kind="ExternalInput").ap()

    xT_d = din("xT", [D_MODEL, SEQ])
    cst_d = din("cst", [128, NCST])
    mats_d = din("mats", [128, len(MATS) * 128])
    featsT_d = din("featsT", [33, SEQ])
    deltas_d = din("deltas", [128, HYW])
    cf_d = din("dft_cf", [16, 128, 2048], BF16)
    sf_d = din("dft_sf", [16, 128, 2048], BF16)
    ci_d = din("dft_ci", [2048, 2048], BF16)
    si_d = din("dft_si", [2048, 2048], BF16)
    Wd_ = {}
    for nm, shp in (("ffn1_w_gate", [LD, D_MODEL, D_FF]), ("ffn1_w_up", [LD, D_MODEL, D_FF]),
                    ("ffn1_w_down", [LD, D_FF, D_MODEL]), ("ffn2_w_gate", [LD, D_MODEL, D_FF]),
                    ("ffn2_w_up", [LD, D_MODEL, D_FF]), ("ffn2_w_down", [LD, D_FF, D_MODEL]),
                    ("w_in", [LD, D_MODEL, D_IN]), ("w_out", [LD, D_MODEL, D_MODEL]),
                    ("hy_f_wout", [LD, 64, 2048])):
        if nm.startswith("ffn") and nm[:4] not in stages:
            continue
        Wd_[nm] = din(nm, shp)
    outT_d = nc.dram_tensor("outT", [D_MODEL, SEQ], F32, kind="ExternalOutput").ap()

    CM = Mem(P, "cm", 15 * 1024, 256)
    AR = Mem(P, "arena", cfg.get("arena_kib", 128) * 1024, 512)
    XM = Mem(P, "xT_sb", 8 * SEQ * 4, 2048)
    xT = XM.alloc([8, SEQ], F32)
    hnb = [None]
    PS = [Mem(P, "ps%d" % i, 2048, 2048, space="ps") for i in range(8)]
    psb = [m.alloc([512], F32) for m in PS]
    psbf = {id(t): Tile(t.mem, 0, [256], BF16) for t in psb}
    psring = Ring(psb)

    cst = CM.alloc([NCST], F32)
    mats = CM.alloc([len(MATS), 128], F32)
    ones_bf = CM.alloc([128], BF16)
    ident_bf = CM.alloc([128], BF16)
    negA = CM.alloc([DEPTH * 8], F32)
    deltas = CM.alloc([HYW], F32)

    def mat(name):
        return mats[:, MATS.index(name), :]

    d_const = DSem(P, "d_const")
    d_x = DSem(P, "d_x")
    d_out = DSem(P, "d_out")
    d_outs = [DSem(P, "d_out%d" % i) for i in range(3)]

    def ccol(name, i=0, n=1, rows=None):
        o, _ = CLAY[name]
        if rows is None:
            return cst[:, o + i:o + i + n]
        return cst[0:rows, o + i:o + i + n]

    def MM(out, lhsT, rhs, start=True, stop=True):
        P.pe(lambda e: e.matmul(out.ap, lhsT=lhsT.ap, rhs=rhs.ap, start=start, stop=stop),
             reads=[lhsT, rhs], writes=[out])

    def ACTF(out, in_, func, bias=None, scale=1.0):
        rd = [in_]
        kw = {}
        if bias is not None:
            rd.append(bias)
            kw["bias"] = bias.ap
        if isinstance(scale, View):
            rd.append(scale)
            kw["scale"] = scale.ap
        else:
            kw["scale"] = scale
        P.act(lambda e: e.activation(out=out.ap, in_=in_.ap, func=func, **kw), reads=rd, writes=[out])

    def TT(eng, out, in0, in1, op):
        P.op(eng, lambda e: e.tensor_tensor(out=out.ap, in0=in0.ap, in1=in1.ap, op=op), reads=[in0, in1], writes=[out])

    def TS(eng, out, in0, s1, s2, op0, op1=None):
        rd = [in0]
        a1 = s1
        a2 = s2
        if isinstance(s1, View):
            rd.append(s1)
            a1 = s1.ap
        if isinstance(s2, View):
            rd.append(s2)
            a2 = s2.ap
        if op1 is None:
            P.op(eng, lambda e: e.tensor_scalar(out=out.ap, in0=in0.ap, scalar1=a1, scalar2=None, op0=op0),
                 reads=rd, writes=[out])
        else:
            P.op(eng, lambda e: e.tensor_scalar(out=out.ap, in0=in0.ap, scalar1=a1, scalar2=a2, op0=op0, op1=op1),
                 reads=rd, writes=[out])

    def STT(eng, out, in0, sc, in1, op0, op1):
        rd = [in0, in1]
        a = sc
        if isinstance(sc, View):
            rd.append(sc)
            a = sc.ap
        P.op(eng, lambda e: e.scalar_tensor_tensor(out=out.ap, in0=in0.ap, scalar=a, in1=in1.ap, op0=op0, op1=op1),
             reads=rd, writes=[out])

    def CP(eng, out, in_):
        if eng == "act":
            P.act(lambda e: e.copy(out=out.ap, in_=in_.ap), reads=[in_], writes=[out])
        else:
            P.op(eng, lambda e: e.tensor_copy(out=out.ap, in_=in_.ap), reads=[in_], writes=[out])

    def MEMSET(eng, out, val):
        P.op(eng, lambda e: e.memset(out.ap, val), writes=[out])

    def DMA(eng, dsem, out, src_ap, new_batch=True):
        P.dma(eng, dsem, lambda e: e.dma_start(out=out.ap, in_=src_ap), writes=[out], new_batch=new_batch)

    def bc(view, shape):
        return View(view.ap.to_broadcast(list(shape)), view.pages)

    DMA("sp", d_const, cst[:], cst_d)
    DMA("sp", d_const, mats[:], mats_d.rearrange("p (m c) -> p m c", c=128), new_batch=False)
    DMA("sp", d_const, deltas[:], deltas_d, new_batch=False)
    MEMSET("pool", ones_bf[:], 1.0)
    CP("dve", ident_bf[:], mat("ident"))
    ACTF(negA[:], ccol("alog", 0, DEPTH * 8), AF.Exp)
    TS("dve", negA[:], negA[:], -1.0, None, ALU.mult)
    d_x.new_batch()
    for k in range(8):
        DMA("sp", d_x, xT[:, k, :], xT_d[k * 128:(k + 1) * 128, :], new_batch=False)

    def rstd_from(chunks, ps, r, sqring, mean_div, eps_view):
        n = len(chunks)
        for k, cv in enumerate(chunks):
            s = sqring.next()
            sv = s[:, 0:cv.ap.shape[-1]]
            ACTF(sv, cv, AF.Square)
            MM(ps, ones_bf[:], sv, start=(k == 0), stop=(k == n - 1))
        ACTF(r, ps, AF.Sqrt, bias=eps_view, scale=1.0 / mean_div)
        P.dve(lambda e: e.reciprocal(out=r.ap, in_=r.ap), reads=[r], writes=[r])

    def rmsnorm(wname, wcol0, final=False):
        hn = hnb[0]
        m0_ = AR.mark()
        sq = Ring([AR.alloc([512], BF16) for _ in range(3)])
        rs = Ring([AR.alloc([512], F32) for _ in range(2)])
        ost = Ring([AR.alloc([512], F32) for _ in range(3)]) if final else None
        oi = 0
        for tt in range(NT):
            ts = slice(tt * 512, (tt + 1) * 512)
            ps = psring.next()
            r = rs.next()
            rstd_from([xT[:, k, ts] for k in range(8)], ps[:], r[:], sq, D_MODEL, ccol("eps"))
            for k in range(8):
                xv = xT[:, k, ts]
                wv = ccol(wname, wcol0 + k)
                if not final:
                    STT("dve", hn[:, k, ts], xv, wv, r[:], ALU.mult, ALU.mult)
                else:
                    o = ost.next()
                    STT("dve", o[:], xv, wv, r[:], ALU.mult, ALU.mult)
                    P.dma("sp", d_outs[oi % 3], lambda e, o=o, k=k, ts=ts: e.dma_start(
                        out=outT_d[k * 128:(k + 1) * 128, ts], in_=o[:].ap),
                        reads=[o[:]], writes=[DramRes()], new_batch=True)
                    oi += 1
        AR.reset(m0_)

    d_wgu = [DSem(P, "d_wgu%d" % i) for i in range(2)]
    d_wd = [DSem(P, "d_wd%d" % i) for i in range(2)]
    FGROUPS = [(0, 4), (4, 4), (8, 4), (12, 4), (16, 3), (19, 3)]

    def ffn(l, which):
        pre = "ffn1" if which == 0 else "ffn2"
        Wg = Wd_[pre + "_w_gate"]
        Wu = Wd_[pre + "_w_up"]
        Wdn = Wd_[pre + "_w_down"]
        AR.reset()
        hn = hnb[0] = AR.alloc([8, SEQ], BF16)
        rmsnorm(pre + "_norm", l * 8)
        wg = [AR.alloc([8, 512], BF16) for _ in range(2)]
        wu = [AR.alloc([8, 512], BF16) for _ in range(2)]
        wd = [AR.alloc([4, 1024], BF16) for _ in range(2)]
        actb = [AR.alloc([4, SEQ], BF16) for _ in range(2)]
        stmp = Ring([AR.alloc([512], F32) for _ in range(3)])

        def load(gi):
            f0, nf = FGROUPS[gi]
            s = gi % 2
            d_wgu[s].new_batch()
            for (W, t) in ((Wg, wg[s]), (Wu, wu[s])):
                src = W[LW[l], :, f0 * 128:(f0 + nf) * 128].rearrange("(k p) f -> p k f", p=128)
                DMA("pool", d_wgu[s], t[:, :, 0:nf * 128], src, new_batch=False)
            src = Wdn[LW[l], f0 * 128:(f0 + nf) * 128, :].rearrange("(j p) d -> p j d", p=128)
            DMA("pool", d_wd[s], wd[s][:, 0:nf, :], src)

        load(0)
        for gi, (f0, nf) in enumerate(FGROUPS):
            s = gi % 2
            if gi + 1 < len(FGROUPS):
                load(gi + 1)
            ab = actb[s]
            for fj in range(nf):
                fs = slice(fj * 128, (fj + 1) * 128)
                for tt in range(NT):
                    ts = slice(tt * 512, (tt + 1) * 512)
                    gps = psring.next()
                    ups = psring.next()
                    for (pst, wt) in ((gps, wg[s]), (ups, wu[s])):
                        for k in range(8):
                            MM(pst[:], wt[:, k, fs], hn[:, k, ts], start=(k == 0), stop=(k == 7))
                    st = stmp.next()
                    ACTF(st[:], gps[:], AF.Silu)
                    TT("dve", ab[:, fj, ts], ups[:], st[:], ALU.mult)
            for dk in range(8):
                ds_ = slice(dk * 128, (dk + 1) * 128)
                for tt in range(NT):
                    ts = slice(tt * 512, (tt + 1) * 512)
                    ops = psring.next()
                    for fj in range(nf):
                        MM(ops[:], wd[s][:, fj, ds_], ab[:, fj, ts], start=(fj == 0), stop=(fj == nf - 1))
                    xv = xT[:, dk, ts]
                    STT("dve", xv, ops[:], 0.5, xv, ALU.mult, ALU.add)

    d_win = [DSem(P, "d_win%d" % i) for i in range(2)]
    d_wo = DSem(P, "d_wo")
    d_dft = [DSem(P, "d_dft%d" % i) for i in range(2)]
    d_misc = DSem(P, "d_misc")
    d_wba = DSem(P, "d_wba")
    win_i = [0]
    ohyb = [None]
    hy_mark = [0]

    def wout_apply(l, row0, nchunks, src_tile):
        wo = AR.alloc([nchunks, 1024], BF16)
        src = Wd_["w_out"][LW[l], row0:row0 + 128 * nchunks, :].rearrange("(m p) d -> p m d", p=128)
        DMA("pool", d_wo, wo[:], src)
        for dk in range(8):
            for tt in range(NT):
                ts = slice(tt * 512, (tt + 1) * 512)
                ps = psring.next()
                for m in range(nchunks):
                    MM(ps[:], wo[:, m, dk * 128:(dk + 1) * 128], src_tile[:, m, ts], start=(m == 0), stop=(m == nchunks - 1))
                xv = xT[:, dk, ts]
                TT("dve", xv, ps[:], xv, ALU.add)

    def project(l, col0, ncols, wslots, consume):
        hn = hnb[0]
        s = win_i[0] % 2
        win_i[0] += 1
        wt = wslots[s]
        src = Wd_["w_in"][LW[l], :, col0:col0 + ncols].rearrange("(k p) f -> p k f", p=128)
        DMA("pool", d_win[s], wt[:, :, 0:ncols], src)
        for m in range(ncols // 128):
            for tt in range(NT):
                ts = slice(tt * 512, (tt + 1) * 512)
                ps = psring.next()
                for k in range(8):
                    MM(ps[:], wt[:, k, m * 128:(m + 1) * 128], hn[:, k, ts], start=(k == 0), stop=(k == 7))
                consume(m, tt, ps)

    def hyena(l):
        AR.reset()
        h3T = AR.alloc([SEQ], F32)
        ohy = ohyb[0] = AR.alloc([4, SEQ], BF16)
        fcst = AR.alloc([8], F32)
        mark0 = AR.mark()
        featsT = AR.alloc([SEQ], F32)
        hA = AR.alloc([SEQ], F32)
        hB = AR.alloc([SEQ], F32)
        tmp = Ring([AR.alloc([512], F32) for _ in range(4)])
        tmpi = Ring([AR.alloc([512], I32) for _ in range(2)])
        DMA("sp", d_misc, featsT[0:33, :], featsT_d)
        freq = ccol("f_freq", l, rows=64)
        for li, bn in enumerate(("f_b1", "f_b2", "f_b3")):
            TS("dve", fcst[0:64, li:li + 1], ccol(bn, l, rows=64), freq, 17.0 * PI, ALU.mult, ALU.add)
        srcs = [featsT, hA, hB]
        dsts = [hA, hB, h3T]
        for li in range(3):
            kk = 33 if li == 0 else 64
            wv = cst[0:kk, CLAY["f_w%d" % (li + 1)][0] + l * 64: CLAY["f_w%d" % (li + 1)][0] + (l + 1) * 64]
            for tt in range(NT):
                ts = slice(tt * 512, (tt + 1) * 512)
                ps = psring.next()
                MM(ps[0:64, :], wv, srcs[li][0:kk, ts])
                t_ = tmp.next()
                TS("dve", t_[0:64, :], ps[0:64, :], freq, fcst[0:64, li:li + 1], ALU.mult, ALU.add)
                ka = tmp.next()
                ki = tmpi.next()
                TS("dve", ka[0:64, :], t_[0:64, :], 1.0 / (2.0 * PI), None, ALU.mult)
                CP("dve", ki[0:64, :], ka[0:64, :])
                CP("dve", ka[0:64, :], ki[0:64, :])
                STT("dve", t_[0:64, :], ka[0:64, :], -2.0 * PI, t_[0:64, :], ALU.mult, ALU.add)
                TS("dve", ka[0:64, :], t_[0:64, :], 0.0, 2.0 * PI, ALU.is_lt, ALU.mult)
                TT("dve", t_[0:64, :], t_[0:64, :], ka[0:64, :], ALU.add)
                ACTF(dsts[li][0:64, ts], t_[0:64, :], AF.Sin, bias=cst[0:64, CLAY["negpi"][0]:CLAY["negpi"][0] + 1])
        for hh in range(2):
            AR.reset(mark0)
            u = [AR.alloc([2, SEQ], BF16) for _ in range(3)]
            mark2 = AR.mark()
            hnb[0] = AR.alloc([8, SEQ], BF16)
            rmsnorm("mix_norm", l * 8)
            wslots = [AR.alloc([8, 256], BF16) for _ in range(2)]
            stage = AR.alloc([SEQ + 2], F32)
            acc = AR.alloc([SEQ], F32)
            MEMSET("pool", stage[:, 0:1], 0.0)
            MEMSET("pool", stage[:, SEQ + 1:SEQ + 2], 0.0)
            for s in range(3):
                def consume(m, tt, ps, s=s):
                    CP("act", stage[:, 1 + tt * 512:1 + (tt + 1) * 512], ps[:])
                    if tt == NT - 1:
                        ch = s * 4 + hh * 2 + m
                        w = [ccol("hy_conv", l * 36 + ch * 3 + k) for k in range(3)]
                        b = ccol("hy_bias", l * 12 + ch)
                        TS("dve", acc[:], stage[:, 0:SEQ], w[0], b, ALU.mult, ALU.add)
                        STT("dve", acc[:], stage[:, 1:SEQ + 1], w[1], acc[:], ALU.mult, ALU.add)
                        STT("dve", u[s][:, m, :], stage[:, 2:SEQ + 2], w[2], acc[:], ALU.mult, ALU.add)
                project(l, 2064 + s * 512 + hh * 256, 256, wslots, consume)
            for o in range(2):
                AR.reset(mark2)
                Kt = AR.alloc([32, 256], BF16)
                dft = [AR.alloc([2, 2048], BF16) for _ in range(2)]
                mark3 = AR.mark()
                woutT = AR.alloc([2, 256], F32)
                Pm = AR.alloc([16, 256], BF16)
                Mm = AR.alloc([16, 256], BF16)
                fw = Ring([AR.alloc([2, 256], F32) for _ in range(2)])
                wn = Ring([AR.alloc([256], F32) for _ in range(2)])
                d_misc.new_batch()
                for d in range(2):
                    c0 = d * 1024 + o * 512 + hh * 256
                    DMA("sp", d_misc, woutT[0:64, d, :], Wd_["hy_f_wout"][LW[l], :, c0:c0 + 256], new_batch=False)
                for i in range(16):
                    ps = psring.next()
                    for d in range(2):
                        MM(ps[:, d * 256:(d + 1) * 256], h3T[0:64, i * 128:(i + 1) * 128], woutT[0:64, d, :])
                    w_ = wn.next()
                    ACTF(w_[:], deltas[:, hh * 256:(hh + 1) * 256], AF.Exp, scale=ccol("negt", i))
                    f_ = fw.next()
                    psv = View(ps[:].ap.rearrange("p (a b) -> p a b", b=256), ps[:].pages)
                    wb = View(w_[:].ap.unsqueeze(1).to_broadcast([128, 2, 256]), w_[:].pages)
                    TT("dve", f_[:], psv, wb, ALU.mult)
                    if i == 0:
                        MEMSET("dve", f_[0:1, 1, :], 0.0)
                    TT("dve", Pm[:, i, :], f_[:, 0, :], f_[:, 1, :], ALU.add)
                    TT("dve", Mm[:, i, :], f_[:, 0, :], f_[:, 1, :], ALU.subtract)

                def load_fwd(j):
                    s = j % 2
                    d_dft[s].new_batch()
                    DMA("sp", d_dft[s], dft[s][:, 0, :], cf_d[j], new_batch=False)
                    DMA("sp", d_dft[s], dft[s][:, 1, :], sf_d[j], new_batch=False)

                ztok = AR.alloc([16, 256], BF16)
                ctmp = Ring([AR.alloc([256], F32) for _ in range(4)])
                zT = u[0]
                gate = u[1 + o]
                for i in range(16):
                    ps = psring.next()
                    pbf = psbf[id(ps)]
                    for m in range(2):
                        src_v = zT[:, m, i * 128:(i + 1) * 128]
                        ov = pbf[:, m * 128:(m + 1) * 128]
                        P.pe(lambda e, ov=ov, src_v=src_v: e.transpose(ov.ap, src_v.ap, ident_bf[:].ap),
                             reads=[src_v, ident_bf[:]], writes=[ov])
                    CP("act", ztok[:, i, :], pbf[:, 0:256])
                load_fwd(0)
                for j in range(16):
                    s = j % 2
                    if j + 1 < 16:
                        load_fwd(j + 1)
                    psA = psring.next()
                    psB = psring.next()
                    for i in range(16):
                        MM(psA[:, 0:256], dft[s][:, 0, i * 128:(i + 1) * 128], Pm[:, i, :], start=(i == 0), stop=(i == 15))
                    for i in range(16):
                        MM(psB[:, 0:256], dft[s][:, 1, i * 128:(i + 1) * 128], Mm[:, i, :], start=(i == 0), stop=(i == 15))
                    CP("act", Kt[:, 2 * j, :], psA[:, 0:256])
                    CP("act", Kt[:, 2 * j + 1, :], psB[:, 0:256])
                    if j == 0:
                        psC = psring.next()
                        for i in range(16):
                            MM(psC[:, 0:256], dft[s][:, 1, i * 128:(i + 1) * 128], Pm[:, i, :], start=(i == 0), stop=(i == 15))
                        CP("act", Kt[0:1, 1, :], psC[0:1, 0:256])
                    psR = psring.next()
                    psI = psring.next()
                    for i in range(16):
                        MM(psR[:, 0:256], dft[s][:, 0, i * 128:(i + 1) * 128], ztok[:, i, :], start=(i == 0), stop=(i == 15))
                    for i in range(16):
                        MM(psI[:, 0:256], dft[s][:, 1, i * 128:(i + 1) * 128], ztok[:, i, :], start=(i == 0), stop=(i == 15))
                    kre = Kt[:, 2 * j, :]
                    kim = Kt[:, 2 * j + 1, :]
                    t1, t2, t3, t4 = (ctmp.next() for _ in range(4))
                    TT("dve", t1[:], psR[:, 0:256], kre, ALU.mult)
                    TT("dve", t2[:], psI[:, 0:256], kim, ALU.mult)
                    TT("dve", t3[:], psR[:, 0:256], kim, ALU.mult)
                    TT("dve", t4[:], psI[:, 0:256], kre, ALU.mult)
                    TT("dve", kre, t1[:], t2[:], ALU.subtract)
                    TT("dve", kim, t3[:], t4[:], ALU.add)
                    if j == 0:
                        CP("dve", Kt[0:1, 0, :], t1[0:1, :])
                        CP("dve", Kt[0:1, 1, :], t2[0:1, :])

                def load_inv(th, j):
                    s = j % 2
                    d_dft[s].new_batch()
                    DMA("sp", d_dft[s], dft[s][:, 0, 0:1024], ci_d[j * 128:(j + 1) * 128, th * 1024:(th + 1) * 1024], new_batch=False)
                    DMA("sp", d_dft[s], dft[s][:, 1, 0:1024], si_d[j * 128:(j + 1) * 128, th * 1024:(th + 1) * 1024], new_batch=False)

                for th in range(2):
                    accs = [[psring.next() for _ in range(2)] for _ in range(2)]
                    load_inv(th, 0)
                    for j in range(16):
                        s = j % 2
                        if j + 1 < 16:
                            load_inv(th, j + 1)
                        for m in range(2):
                            for tq in range(2):
                                MM(accs[m][tq][:], Kt[:, 2 * j, m * 128:(m + 1) * 128], dft[s][:, 0, tq * 512:(tq + 1) * 512],
                                   start=(j == 0), stop=False)
                                MM(accs[m][tq][:], Kt[:, 2 * j + 1, m * 128:(m + 1) * 128], dft[s][:, 1, tq * 512:(tq + 1) * 512],
                                   start=False, stop=(j == 15))
                    for m in range(2):
                        for tq in range(2):
                            tt = th * 2 + tq
                            ts = slice(tt * 512, (tt + 1) * 512)
                            sk = ccol("hy_skip", l * 8 + o * 4 + hh * 2 + m)
                            tv = accs[m][tq][:]
                            STT("dve", tv, zT[:, m, ts], sk, tv, ALU.mult, ALU.add)
                            dst = zT[:, m, ts] if o == 0 else ohy[:, hh * 2 + m, ts]
                            TT("dve", dst, tv, gate[:, m, ts], ALU.mult)
        AR.reset(mark0)
        sq = Ring([AR.alloc([512], BF16) for _ in range(3)])
        rs = Ring([AR.alloc([512], F32) for _ in range(2)])
        for tt in range(NT):
            ts = slice(tt * 512, (tt + 1) * 512)
            ps = psring.next()
            r = rs.next()
            rstd_from([ohy[:, m, ts] for m in range(4)], ps[:], r[:], sq, HYW, ccol("eps"))
            for m in range(4):
                STT("dve", ohy[:, m, ts], ohy[:, m, ts], ccol("hy_norm", l * 4 + m), r[:], ALU.mult, ALU.mult)
        hy_mark[0] = mark0

    def hyena_out(l):
        wout_apply(l, 512, 4, ohyb[0])

    def deltanet(l):
        AR.reset(hy_mark[0])
        hn = hnb[0] = AR.alloc([8, SEQ], BF16)
        rmsnorm("mix_norm", l * 8)
        wba = AR.alloc([8, 16], BF16)
        ba = AR.alloc([16, 16], F32)
        beta_t = AR.alloc([16, 8], F32)
        g_t = AR.alloc([16, 8], F32)
        gc_t = AR.alloc([16, 8], F32)
        ngc_t = AR.alloc([16, 8], F32)
        bw_t = AR.alloc([16, 8], F32)
        kds_t = AR.alloc([16, 8], F32)
        egl = AR.alloc([16, 16], F32)
        DMA("pool", d_wba, wba[:], Wd_["w_in"][LW[l], :, 2048:2064].rearrange("(k p) f -> p k f", p=128))
        ps = psring.next()
        for blk in range(16):
            for k in range(8):
                MM(ps[:, blk * 16:(blk + 1) * 16], hn[:, k, blk * 128:(blk + 1) * 128], wba[:, k, :], start=(k == 0), stop=(k == 7))
        bav = View(ba[:].ap.rearrange("p a b -> p (a b)"), ba[:].pages)
        CP("act", bav, ps[:, 0:256])
        dn_stop = cfg.get("dn_stop", 99)
        if dn_stop <= -3:
            return
        ACTF(beta_t[:], ba[:, :, 0:8], AF.Sigmoid)
        dtb = View(ccol("dtb", l * 8, 8).ap.unsqueeze(1).to_broadcast([128, 16, 8]), ccol("dtb", l * 8, 8).pages)
        nab = View(negA[:, l * 8:(l + 1) * 8].ap.unsqueeze(1).to_broadcast([128, 16, 8]), negA[:].pages)
        TT("dve", g_t[:], ba[:, :, 8:16], dtb, ALU.add)
        ACTF(g_t[:], g_t[:], AF.Exp)
        ACTF(g_t[:], g_t[:], AF.Ln, bias=ccol("one"))
        TT("dve", g_t[:], g_t[:], nab, ALU.mult)
        if dn_stop <= -2:
            return
        ps = psring.next()
        ps2 = psring.next()
        ps3 = psring.next()
        v3 = lambda v, b: View(v.ap.rearrange("p (a b) -> p a b", b=b), v.pages)
        MM(ps[:, 0:64], mat("cum_f"), g_t[:, :, 0:4])
        MM(ps[:, 64:128], mat("cum_b"), g_t[:, :, 4:8])
        MM(ps2[:, 0:128], mat("inda"), g_t[:])
        MM(ps2[:, 128:256], mat("indb"), g_t[:])
        MM(ps3[:, 0:128], mat("blk"), g_t[:])
        flat = lambda t: View(t[:].ap.rearrange("p a b -> p (a b)"), t[:].pages)
        if dn_stop <= -1:
            return
        nel = cfg.get("dn_nel", 99)
        elops = [
            lambda: CP("act", gc_t[:, :, 0:4], v3(ps[:, 0:64], 4)),
            lambda: CP("act", gc_t[:, :, 4:8], v3(ps[:, 64:128], 4)),
            lambda: TS("dve", flat(ngc_t), flat(gc_t), -1.0, None, ALU.mult),
            lambda: ACTF(egl[:, :, 0:8], v3(ps2[:, 0:128], 8), AF.Exp),
            lambda: ACTF(egl[:, :, 8:16], v3(ps2[:, 128:256], 8), AF.Exp),
            lambda: TT("dve", flat(kds_t), ps3[:, 0:128], flat(gc_t), ALU.subtract),
            lambda: ACTF(flat(kds_t), flat(kds_t), AF.Exp),
            lambda: ACTF(flat(bw_t), flat(gc_t), AF.Exp),
            lambda: TT("dve", flat(bw_t), flat(bw_t), flat(beta_t), ALU.mult),
        ]
        for f_ in elops[:nel]:
            f_()
        markh = AR.mark()
        dn_stop = cfg.get("dn_stop", 99)
        if dn_stop <= 0:
            return
        for h in range(cfg.get("dn_heads", 4)):
            AR.reset(markh)
            qkvz = [AR.alloc([SEQ], BF16) for _ in range(4)]
            qT, kT, vT, zg = qkvz
            ktok = AR.alloc([16, 128], BF16)
            vtok = AR.alloc([16, 128], BF16)
            oT = AR.alloc([SEQ], F32)
            S = [AR.alloc([128], F32) for _ in range(2)]
            Sb = [AR.alloc([128], BF16) for _ in range(2)]
            markp = AR.mark()
            wslots = [AR.alloc([8, 128], BF16) for _ in range(2)]
            stage = AR.alloc([SEQ + 4], F32)
            acc = AR.alloc([SEQ], F32)
            sq = Ring([AR.alloc([512], BF16) for _ in range(2)])
            rs = Ring([AR.alloc([512], F32) for _ in range(2)])
            MEMSET("pool", stage[:, 0:2], 0.0)
            MEMSET("pool", stage[:, SEQ + 2:SEQ + 4], 0.0)
            for s in range(4):
                def consume(m, tt, ps, s=s):
                    CP("act", stage[:, 2 + tt * 512:2 + (tt + 1) * 512], ps[:])
                    if tt != NT - 1:
                        return
                    if s == 3:
                        ACTF(zg[:], stage[:, 2:SEQ + 2], AF.Silu)
                        return
                    ch = s * 4 + h
                    w = [ccol("dn_conv", l * 60 + ch * 5 + k) for k in range(5)]
                    TS("dve", acc[:], stage[:, 0:SEQ], w[0], None, ALU.mult)
                    for k in range(1, 5):
                        STT("dve", acc[:], stage[:, k:SEQ + k], w[k], acc[:], ALU.mult, ALU.add)
                    if s == 2:
                        ACTF(vT[:], acc[:], AF.Silu)
                        return
                    ACTF(acc[:], acc[:], AF.Silu)
                    dst = qkvz[s]
                    for t2 in range(NT):
                        ts = slice(t2 * 512, (t2 + 1) * 512)
                        p2 = psring.next()
                        r = rs.next()
                        rstd_from([acc[:, ts]], p2[:], r[:], sq, 1.0, ccol("eps_l2"))
                        if s == 0:
                            STT("dve", dst[:, ts], acc[:, ts], 128.0 ** -0.5, r[:], ALU.mult, ALU.mult)
                        else:
                            TT("dve", dst[:, ts], acc[:, ts], r[:], ALU.mult)
                project(l, s * 512 + h * 128, 128, wslots, consume)
            for blk in range(16):
                bs = slice(blk * 128, (blk + 1) * 128)
                ps = psring.next()
                pbf = psbf[id(ps)]
                for m, srcT in enumerate((kT, vT)):
                    src_v = srcT[:, bs]
                    ov = pbf[:, m * 128:(m + 1) * 128]
                    P.pe(lambda e, ov=ov, src_v=src_v: e.transpose(ov.ap, src_v.ap, ident_bf[:].ap),
                         reads=[src_v, ident_bf[:]], writes=[ov])
                CP("act", ktok[:, blk, :], pbf[:, 0:128])
                CP("act", vtok[:, blk, :], pbf[:, 128:256])
            if dn_stop <= 1:
                continue
            AR.reset(markp)
            NF = 5
            NB = 15
            sets = [[{"f": [AR.alloc([128], F32) for _ in range(NF)], "b": [AR.alloc([128], BF16, align=256) for _ in range(NB)]}
                     for _ in range(2)] for _ in range(2)]
            MEMSET("pool", oT[:], 0.0)
            for d in range(2):
                MEMSET("pool", S[d][:], 0.0)
                MEMSET("pool", Sb[d][:], 0.0)
            ident = mat("ident")
            def unit(step, d):
                blk = step if d == 0 else 15 - step
                bs = slice(blk * 128, (blk + 1) * 128)
                r_ = d * 4 + h
                T = sets[d][step % 2]
                dg, Ds, DTm, Er, Pm_ = T["f"]
                Pb, kbg, vb, kdec, wT, attnT, qd, vn, N_, M_, Na, Ma, Nb_, Mb_, Pw = T["b"]
                gcc = gc_t[:, blk, r_:r_ + 1]
                ngc = ngc_t[:, blk, r_:r_ + 1]
                bcol = beta_t[:, blk, r_:r_ + 1]
                TS("dve", dg[:], ident, gcc, None, ALU.mult)
                yield
                pR = psring.next()
                MM(pR[:, 0:128], mat("ones"), dg[:])
                yield
                STT("dve", Ds[:], pR[:, 0:128], -1.0, mat("neg_ls" if d == 0 else "neg_us"), ALU.mult, ALU.add)
                yield
                ACTF(Ds[:], Ds[:], AF.Exp, bias=gcc)
                yield
                TT("dve", DTm[:], pR[:, 0:128], mat("neg_ui" if d == 0 else "neg_li"), ALU.add)
                yield
                ACTF(DTm[:], DTm[:], AF.Exp, bias=ngc)
                yield
                ACTF(Er[:], pR[:, 0:128], AF.Exp)
                yield
                pK = psring.next()
                MM(pK[:, 0:128], kT[:, bs], kT[:, bs])
                yield
                MM(pK[:, 128:256], kT[:, bs], qT[:, bs])
                yield
                STT("dve", N_[:], pK[:, 0:128], bcol, Ds[:], ALU.mult, ALU.mult)
                yield
                TT("dve", attnT[:], pK[:, 128:256], DTm[:], ALU.mult)
                yield
                TT("dve", qd[:], qT[:, bs], Er[:], ALU.mult)
                yield
                pre_stop = cfg.get("pre_stop", 99)
                if pre_stop <= 2:
                    return
                pT = psring.next()
                pTb = psbf[id(pT)]
                P.pe(lambda e, o_=pTb[:, 0:128], i_=N_[:]: e.transpose(o_.ap, i_.ap, ident_bf[:].ap),
                     reads=[N_[:], ident_bf[:]], writes=[pTb[:, 0:128]])
                CP("act", M_[:], pTb[:, 0:128])
                yield
                STT("dve", Pm_[:], pTb[:, 0:128], -1.0, ident, ALU.mult, ALU.add)
                yield
                STT("dve", Pw[:], pTb[:, 0:128], -1.0, ident, ALU.mult, ALU.add)
                yield
                if pre_stop <= 3:
                    return
                Nc, Mc = N_, M_
                nxt = [(Na, Ma), (Nb_, Mb_)]
                for lvl in range(5):
                    N2, M2 = nxt[lvl % 2]
                    pq = psring.next()
                    MM(pq[:, 0:128], Mc[:], Nc[:])
                    if lvl < 4:
                        MM(pq[:, 128:256], Nc[:], Mc[:])
                    CP("act", N2[:], pq[:, 0:128])
                    yield
                    if lvl < 4:
                        CP("act", M2[:], pq[:, 128:256])
                        yield
                    pp = psring.next()
                    MM(pp[:, 0:128], N2[:], Pw[:])
                    yield
                    if lvl < 4:
                        TT("dve", Pm_[:], Pm_[:], pp[:, 0:128], ALU.add)
                        yield
                        CP("act", Pw[:], Pm_[:])
                        yield
                    else:
                        TT("dve", Pb[:], Pm_[:], pp[:, 0:128], ALU.add)
                        yield
                    Nc, Mc = N2, M2
                if pre_stop <= 4:
                    return
                TS("dve", kbg[:], ktok[:, blk, :], bw_t[:, blk, r_:r_ + 1], None, ALU.mult)
                yield
                TS("dve", vb[:], vtok[:, blk, :], bcol, None, ALU.mult)
                yield
                TS("dve", kdec[:], ktok[:, blk, :], kds_t[:, blk, r_:r_ + 1], None, ALU.mult)
                yield
                if pre_stop <= 5:
                    return
                pu = psring.next()
                MM(pu[:, 0:128], Pb[:], vb[:])
                yield
                MM(pu[:, 128:256], kbg[:], Pb[:])
                yield
                u_ = dg
                CP("act", u_[:], pu[:, 0:128])
                yield
                CP("act", wT[:], pu[:, 128:256])
                yield
                if dn_stop <= 2:
                    return
                yield "SPLIT"
                for cc in ((0, 1) if d == 0 else (1, 0)):
                    r0 = cc * 64
                    rr = slice(r0, r0 + 64)
                    pw = psring.next()
                    MM(pw[rr, 0:128], wT[:, rr], Sb[d][:])
                    yield
                    TT("dve", vn[rr, :], u_[rr, :], pw[rr, 0:128], ALU.subtract)
                    yield
                    po = psring.next()
                    MM(po[:, 0:64], Sb[d][:], qd[:, rr], start=True, stop=False)
                    yield
                    MM(po[:, 0:64], vn[rr, :], attnT[rr, rr], start=False, stop=True)
                    yield
                    ov = oT[:, blk * 128 + r0:blk * 128 + r0 + 64]
                    TT("dve", ov, ov, po[:, 0:64], ALU.add)
                    yield
                    pd = psring.next()
                    MM(pd[:, 0:128], kdec[rr, :], vn[rr, :])
                    yield
                    STT("dve", S[d][:], S[d][:], egl[:, blk, cc * 8 + r_:cc * 8 + r_ + 1], pd[:, 0:128], ALU.mult, ALU.add)
                    yield
                    CP("act", Sb[d][:], S[d][:])
                    yield

                yield

            nsteps = cfg.get("dn_steps", 16)

            def drive(active):
                while active:
                    for it in list(active):
                        try:
                            v_ = next(it[0])
                        except StopIteration:
                            active.remove(it)
                            continue
                        if v_ == "SPLIT" and it[1] == "pre":
                            active.remove(it)

            cur = [unit(0, 0), unit(0, 1)]
            drive([[g_, "pre"] for g_ in cur])
            for step in range(nsteps):
                nxt = [unit(step + 1, 0), unit(step + 1, 1)] if step + 1 < nsteps else []
                drive([[g_, "rec"] for g_ in cur] + [[g_, "pre"] for g_ in nxt])
                cur = nxt

            AR.reset(markp)
            sq = Ring([AR.alloc([512], BF16) for _ in range(2)])
            rs = Ring([AR.alloc([512], F32) for _ in range(2)])
            odn = AR.alloc([1, SEQ], BF16)
            for tt in range(NT):
                ts = slice(tt * 512, (tt + 1) * 512)
                p2 = psring.next()
                r = rs.next()
                rstd_from([oT[:, ts]], p2[:], r[:], sq, 128.0, ccol("eps"))
                STT("dve", oT[:, ts], oT[:, ts], ccol("dn_norm", l), r[:], ALU.mult, ALU.mult)
                TT("dve", odn[:, 0, ts], oT[:, ts], zg[:, ts], ALU.mult)
            wout_apply(l, h * 128, 1, odn)

    for l in layers:
        if "ffn1" in stages:
            ffn(l, 0)
        if "hy" in stages:
            hyena(l)
        else:
            AR.reset()
            hy_mark[0] = 0
        if "dn" in stages:
            deltanet(l)
        if "hy" in stages:
            hyena_out(l)
        if "ffn2" in stages:
            ffn(l, 1)
    if cfg.get("final_norm", True):
        AR.reset()
        rmsnorm("final_norm", 0, final=True)
    else:
        AR.reset()
        d_out.new_batch()
        for k in range(8):
            v = xT[:, k, :]
            P.dma("sp", d_out, lambda e, v=v, k=k: e.dma_start(out=outT_d[k * 128:(k + 1) * 128, :], in_=v.ap),
                  reads=[v], writes=[DramRes()], new_batch=False)
    for ds_ in [d_out] + d_outs:
        if ds_.batch is not None:
            P.final_batches.append(ds_.batch)
    stats = P.emit()
    P.es.close()
    return nc, stats


_CACHE = {}


def make_in_maps(inputs, cfg=None):
    stages = (cfg or {}).get("stages", ("ffn1", "dn", "hy", "ffn2"))
    st = static_consts()
    shared = {"cst": build_cst(inputs)}
    shared.update(st)
    for nm in ("ffn1_w_gate", "ffn1_w_up", "ffn1_w_down", "ffn2_w_gate", "ffn2_w_up", "ffn2_w_down",
               "w_in", "w_out", "hy_f_wout"):
        if nm.startswith("ffn") and nm[:4] not in stages:
            continue
        a_ = np.asarray(inputs[nm], dtype=np.float32)
        if cfg and "layers" in cfg:
            a_ = a_[list(cfg["layers"])]
        shared[nm] = np.ascontiguousarray(a_)
    maps = []
    xT_list = (cfg or {}).get("_xT")
    x = None if xT_list is not None else np.asarray(inputs["x"], dtype=np.float32)
    for b in range(8):
        m = dict(shared)
        m["xT"] = xT_list[b] if xT_list is not None else np.ascontiguousarray(x[b].T)
        maps.append(m)
    return maps


LAUNCH_GROUPS = ((0, 1, 2, 3),)


def _get_prog(cfg):
    key = repr(sorted((k, v) for k, v in cfg.items() if k != "_xT"))
    if key not in _CACHE:
        _CACHE[key] = build_program({k: v for k, v in cfg.items() if k != "_xT"})
    return _CACHE[key]


def kernel(**inputs):
    cfg = inputs.pop("_cfg", None)
    if cfg is not None:
        nc, stats = _get_prog(cfg)
        maps = make_in_maps(inputs, cfg)
        res = run_bass_kernel_spmd(nc, maps, core_ids=list(range(8)))
        out = np.stack([np.ascontiguousarray(r["outT"].T) for r in res.results], axis=0)
        return out.astype(np.float32)
    xT_list = None
    for gi, grp in enumerate(LAUNCH_GROUPS):
        last = gi == len(LAUNCH_GROUPS) - 1
        if len(LAUNCH_GROUPS) == 1:
            cfg = {}
        else:
            cfg = {"layers": tuple(grp), "final_norm": last}
        nc, stats = _get_prog(cfg)
        c2 = dict(cfg)
        if xT_list is not None:
            c2["_xT"] = xT_list
        maps = make_in_maps(inputs, c2)
        res = run_bass_kernel_spmd(nc, maps, core_ids=list(range(8)))
        xT_list = [np.ascontiguousarray(r["outT"]) for r in res.results]
    out = np.stack([np.ascontiguousarray(a.T) for a in xT_list], axis=0)
    return out.astype(np.float32)
```
